# Optimizing a Trainium2 kernel written in Bass

```python
import math, functools
import jax, jax.numpy as jnp
from jax import lax
import numpy as np

D_MODEL = 1024
BATCH = 1
SEQ = 16384
DEPTH = 2
DEC_BATCH = 32
DEC_SEQ = 64
PAST_LEN = 4096

CHUNK = 64
HEAD_DIM = 64
A_HEADS = 4
A_DK = 64
A_DV = 128
A_GATE_RANK = 16
A_GATE_TAU = 16.0
B_HEADS = 4
B_KV_HEADS = 2
B_WINDOW = 128
B_PREV_CHUNKS = B_WINDOW // CHUNK
C_HEADS = 4
C_PREV_CHUNKS = 8
C_BAND = C_PREV_CHUNKS * CHUNK
C_CLIP = 256
T5_BUCKETS = 32
T5_MAX_DIST = 128
D_FF = 4 * D_MODEL
D_MIX = A_HEADS * A_DV + B_HEADS * HEAD_DIM + C_HEADS * HEAD_DIM
SPLIT_SIZES = (A_HEADS * A_DK, A_HEADS * A_DK, A_HEADS * A_DV, A_HEADS * A_DV, A_GATE_RANK,
               B_HEADS * HEAD_DIM, B_KV_HEADS * HEAD_DIM, B_KV_HEADS * HEAD_DIM,
               C_HEADS * HEAD_DIM, C_HEADS * HEAD_DIM, C_HEADS * HEAD_DIM)
D_IN = sum(SPLIT_SIZES)
NORM_EPS = 1e-6
NEG_INF = -1e30

kernel_name = 'hybrid_streaming_encoder_step'


def rms_norm(x, g):
    x32 = x.astype(jnp.float32)
    y = x32 * lax.rsqrt(jnp.mean(x32 * x32, axis=-1, keepdims=True) + NORM_EPS)
    return (y * g.astype(jnp.float32)).astype(x.dtype)


def modulate(x, g, shift, scale):
    return rms_norm(x, g) * (1 + scale[:, None, :]) + shift[:, None, :]


def split_offsets():
    offs, t = [], 0
    for s in SPLIT_SIZES[:-1]:
        t += s
        offs.append(t)
    return offs


def project_heads(h, w_in, w_a2, b_a2):
    B_, T, _ = h.shape
    qa, ka, va, ga, ra, qb, kb, vb, qc, kc, vc = jnp.split(h @ w_in, split_offsets(), axis=-1)
    la = jax.nn.log_sigmoid((ra @ w_a2 + b_a2).astype(jnp.float32)) / A_GATE_TAU
    heads = lambda t, n: t.reshape(B_, T, n, -1)
    return (heads(qa, A_HEADS) * (A_DK ** -0.5), heads(ka, A_HEADS), heads(va, A_HEADS), ga,
            heads(la, A_HEADS),
            heads(qb, B_HEADS), heads(kb, B_KV_HEADS), heads(vb, B_KV_HEADS),
            heads(qc, C_HEADS), heads(kc, C_HEADS), heads(vc, C_HEADS))


def gla_chunk(S, q, k, v, la):
    L = q.shape[2]
    b = jnp.cumsum(la, axis=2)
    causal = jnp.tril(jnp.ones((L, L), dtype=bool))
    diff = b[:, :, :, None, :] - b[:, :, None, :, :]
    decay = jnp.exp(jnp.where(causal[None, None, :, :, None], diff, -jnp.inf))
    attn = jnp.einsum('bhtd,bhsd,bhtsd->bhts', q, k, decay)
    o = jnp.einsum('bhts,bhsv->bhtv', attn, v) + jnp.einsum('bhtd,bhdv->bhtv', q * jnp.exp(b), S)
    b_last = b[:, :, -1:, :]
    S_new = jnp.exp(b_last[:, :, 0, :])[..., None] * S + jnp.einsum('bhsd,bhsv->bhdv', k * jnp.exp(b_last - b), v)
    return S_new, o


def gla_prompt(q, k, v, la):
    B_, T, H, dk = q.shape
    dv = v.shape[-1]
    nc = T // CHUNK
    to_blocks = lambda t: t.astype(jnp.float32).reshape(B_, nc, CHUNK, H, t.shape[-1]).transpose(1, 0, 3, 2, 4)
    S0 = jnp.zeros((B_, H, dk, dv), jnp.float32)
    S, o = lax.scan(lambda S, inp: gla_chunk(S, *inp), S0, (to_blocks(q), to_blocks(k), to_blocks(v), to_blocks(la)))
    return S, o.transpose(1, 0, 3, 2, 4).reshape(B_, T, H, dv)


def gla_sample(S, q, k, v, la):
    tr = lambda t: t.astype(jnp.float32).transpose(0, 2, 1, 3)
    S_new, o = gla_chunk(S.astype(jnp.float32), tr(q), tr(k), tr(v), tr(la))
    return S_new, o.transpose(0, 2, 1, 3)


def gla_output(o, g, norm_g):
    B_, T, H, dv = o.shape
    on = rms_norm(o, norm_g)
    return (on * jax.nn.silu(g.astype(jnp.float32)).reshape(B_, T, H, dv)).reshape(B_, T, H * dv)


def t5_bucket(rel):
    half = T5_BUCKETS // 2
    max_exact = half // 2
    n = jnp.abs(rel)
    log_ratio = jnp.log(jnp.maximum(n, 1).astype(jnp.float32) / max_exact) / math.log(T5_MAX_DIST / max_exact)
    large = jnp.minimum(max_exact + (log_ratio * (half - max_exact)).astype(jnp.int32), half - 1)
    return jnp.where(rel > 0, half, 0) + jnp.where(n < max_exact, n, large)


def t5_logits(table, rel):
    return jnp.moveaxis(table[t5_bucket(rel)], -1, 1).astype(jnp.float32)


def clipped_logits(table, rel):
    return jnp.moveaxis(table[jnp.clip(rel, -C_CLIP, C_CLIP) + C_CLIP], -1, 1).astype(jnp.float32)


def band_attention(q, k, v, bias, valid, sink):
    B_, N, Lq, H, hd = q.shape
    Lk, G = k.shape[2], k.shape[3]
    R = H // G
    qg = q.reshape(B_, N, Lq, G, R, hd)
    s = jnp.einsum('bnqgrd,bnkgd->bngrqk', qg, k).astype(jnp.float32) * (hd ** -0.5)
    s = s + bias.reshape(bias.shape[0], G, R, Lq, Lk)[None]
    s = jnp.where(valid[None, :, None, None, None, :], s, NEG_INF)
    if sink is None:
        p = jax.nn.softmax(s, axis=-1)
    else:
        sk = sink.astype(jnp.float32).reshape(G, R)[None, None, :, :, None, None]
        m = jnp.maximum(jnp.max(s, axis=-1, keepdims=True), sk)
        e = jnp.exp(s - m)
        p = e / (jnp.sum(e, axis=-1, keepdims=True) + jnp.exp(sk - m))
    o = jnp.einsum('bngrqk,bnkgd->bnqgrd', p.astype(v.dtype), v)
    return o.reshape(B_, N, Lq, H * hd)


def chunk_band(x, n_prev):
    B_, T = x.shape[:2]
    nc = T // CHUNK
    xc = x.reshape(B_, nc, CHUNK, x.shape[2], x.shape[3])
    xp = jnp.pad(xc, ((0, 0), (n_prev, 0), (0, 0), (0, 0), (0, 0)))
    return jnp.concatenate([xp[:, i:i + nc] for i in range(n_prev + 1)], axis=2)


def attend_prompt(q, k, v, n_prev, bias_fn, sink):
    B_, T, H, hd = q.shape
    nc = T // CHUNK
    lk = (n_prev + 1) * CHUNK
    kpos = jnp.arange(lk) - n_prev * CHUNK
    rel = (kpos[None, :] - jnp.arange(CHUNK)[:, None])[None]
    valid = (jnp.arange(nc)[:, None] * CHUNK + kpos[None, :]) >= 0
    o = band_attention(q.reshape(B_, nc, CHUNK, H, hd), chunk_band(k, n_prev), chunk_band(v, n_prev),
                       bias_fn(rel), valid, sink)
    return o.reshape(B_, T, H * hd)


def attend_sample(q, k_new, v_new, k_cache, v_cache, bias_fn, sink):
    B_, S = q.shape[:2]
    Lc = k_cache.shape[1]
    k = jnp.concatenate([k_cache.astype(k_new.dtype), k_new], axis=1)
    v = jnp.concatenate([v_cache.astype(v_new.dtype), v_new], axis=1)
    lk = Lc + S
    rel = ((jnp.arange(lk) - Lc)[None, :] - jnp.arange(S)[:, None])[None]
    valid = jnp.ones((1, lk), dtype=bool)
    o = band_attention(q[:, None], k[:, None], v[:, None], bias_fn(rel), valid, sink)
    return o.reshape(B_, S, -1), k[:, -Lc:], v[:, -Lc:]


def run_trunk(x, c, state_gla, cache_b_k, cache_b_v, cache_c_k, cache_c_v,
              w_ada, b_ada, norm_mix_g, norm_mlp_g, w_in, w_a2, b_a2, a_norm_g, b_sink, t5_bias,
              c_rel_bias, w_out, w_up, w_down, final_norm_g):
    is_prompt = state_gla is None
    T = x.shape[1]
    new_gla, new_kb, new_vb, new_kc, new_vc = [], [], [], [], []
    t5_fn = functools.partial(t5_logits, t5_bias)
    for l in range(DEPTH):
        sh_m, sc_m, gt_m, sh_f, sc_f, gt_f = jnp.split(jax.nn.silu(c) @ w_ada[l] + b_ada[l], 6, axis=-1)
        h = modulate(x, norm_mix_g[l], sh_m, sc_m)
        qa, ka, va, ga, la, qb, kb, vb, qc, kc, vc = project_heads(h, w_in[l], w_a2[l], b_a2[l])
        c_fn = functools.partial(clipped_logits, c_rel_bias[l])
        if is_prompt:
            S, oa = gla_prompt(qa, ka, va, la)
            ob = attend_prompt(qb, kb, vb, B_PREV_CHUNKS, t5_fn, b_sink[l])
            oc = attend_prompt(qc, kc, vc, C_PREV_CHUNKS, c_fn, None)
            lc = min(C_BAND, T)
            kb_buf, vb_buf = kb[:, -B_WINDOW:], vb[:, -B_WINDOW:]
            kc_buf, vc_buf = kc[:, -lc:], vc[:, -lc:]
        else:
            S, oa = gla_sample(state_gla[l], qa, ka, va, la)
            ob, kb_buf, vb_buf = attend_sample(qb, kb, vb, cache_b_k[l], cache_b_v[l], t5_fn, b_sink[l])
            oc, kc_buf, vc_buf = attend_sample(qc, kc, vc, cache_c_k[l], cache_c_v[l], c_fn, None)
        oa = gla_output(oa, ga, a_norm_g[l]).astype(x.dtype)
        mixed = jnp.concatenate([oa, ob, oc], axis=-1) @ w_out[l]
        x = x + gt_m[:, None, :] * mixed
        h = modulate(x, norm_mlp_g[l], sh_f, sc_f)
        x = x + gt_f[:, None, :] * (jnp.square(jax.nn.relu(h @ w_up[l])) @ w_down[l])
        new_gla.append(S.astype(x.dtype))
        new_kb.append(kb_buf)
        new_vb.append(vb_buf)
        new_kc.append(kc_buf)
        new_vc.append(vc_buf)
    y = rms_norm(x, final_norm_g)
    return y, (jnp.stack(new_gla), jnp.stack(new_kb), jnp.stack(new_vb), jnp.stack(new_kc), jnp.stack(new_vc))


def setup_inputs(seed: int = 0) -> dict:
    key = jax.random.key(seed)
    ks = jax.random.split(key, 32)
    nrm = lambda k, shape, scale: jax.random.normal(k, shape, jnp.float32) * scale
    lc = min(C_BAND, PAST_LEN)
    return {
        'x_prompt': nrm(ks[0], (BATCH, SEQ, D_MODEL), 1.0),
        'x_sample': nrm(ks[1], (DEC_BATCH, DEC_SEQ, D_MODEL), 1.0),
        'c_prompt': nrm(ks[2], (BATCH, D_MODEL), 1.0),
        'c_sample': nrm(ks[3], (DEC_BATCH, D_MODEL), 1.0),
        'state_gla': nrm(ks[4], (DEPTH, DEC_BATCH, A_HEADS, A_DK, A_DV), 1.0),
        'cache_b_k': nrm(ks[5], (DEPTH, DEC_BATCH, B_WINDOW, B_KV_HEADS, HEAD_DIM), 1.0),
        'cache_b_v': nrm(ks[6], (DEPTH, DEC_BATCH, B_WINDOW, B_KV_HEADS, HEAD_DIM), 1.0),
        'cache_c_k': nrm(ks[7], (DEPTH, DEC_BATCH, lc, C_HEADS, HEAD_DIM), 1.0),
        'cache_c_v': nrm(ks[8], (DEPTH, DEC_BATCH, lc, C_HEADS, HEAD_DIM), 1.0),
        'w_ada': nrm(ks[9], (DEPTH, D_MODEL, 6 * D_MODEL), 0.5 * D_MODEL ** -0.5),
        'b_ada': nrm(ks[10], (DEPTH, 6 * D_MODEL), 0.01),
        'norm_mix_g': 1.0 + nrm(ks[11], (DEPTH, D_MODEL), 0.05),
        'norm_mlp_g': 1.0 + nrm(ks[12], (DEPTH, D_MODEL), 0.05),
        'w_in': nrm(ks[13], (DEPTH, D_MODEL, D_IN), D_MODEL ** -0.5),
        'w_a2': nrm(ks[14], (DEPTH, A_GATE_RANK, A_HEADS * A_DK), A_GATE_RANK ** -0.5),
        'b_a2': nrm(ks[15], (DEPTH, A_HEADS * A_DK), 0.1),
        'a_norm_g': 1.0 + nrm(ks[16], (DEPTH, A_DV), 0.05),
        'b_sink': nrm(ks[17], (DEPTH, B_HEADS), 1.0),
        't5_bias': nrm(ks[18], (T5_BUCKETS, B_HEADS), 0.5),
        'c_rel_bias': nrm(ks[19], (DEPTH, 2 * C_CLIP + 1, C_HEADS), 0.5),
        'w_out': nrm(ks[20], (DEPTH, D_MIX, D_MODEL), D_MIX ** -0.5),
        'w_up': nrm(ks[21], (DEPTH, D_MODEL, D_FF), D_MODEL ** -0.5),
        'w_down': nrm(ks[22], (DEPTH, D_FF, D_MODEL), D_FF ** -0.5),
        'final_norm_g': 1.0 + nrm(ks[23], (D_MODEL,), 0.05),
    }


def reference(x_prompt, x_sample, c_prompt, c_sample, state_gla, cache_b_k, cache_b_v, cache_c_k, cache_c_v,
              w_ada, b_ada, norm_mix_g, norm_mlp_g, w_in, w_a2, b_a2, a_norm_g, b_sink, t5_bias, c_rel_bias,
              w_out, w_up, w_down, final_norm_g):
    weights = (w_ada, b_ada, norm_mix_g, norm_mlp_g, w_in, w_a2, b_a2, a_norm_g, b_sink, t5_bias,
               c_rel_bias, w_out, w_up, w_down, final_norm_g)
    y_prompt, (sg_p, kb_p, vb_p, kc_p, vc_p) = run_trunk(x_prompt, c_prompt, None, None, None, None, None, *weights)
    y_sample, (sg_s, kb_s, vb_s, kc_s, vc_s) = run_trunk(x_sample, c_sample, state_gla, cache_b_k, cache_b_v,
                                                         cache_c_k, cache_c_v, *weights)
    return (y_prompt, y_sample, sg_p, kb_p, vb_p, kc_p, vc_p, sg_s, kb_s, vb_s, kc_s, vc_s)
```

```python
import contextlib
import os
import types
import numpy as np
import concourse.bass as bass
import concourse.mybir as mybir
from concourse.bass_utils import run_bass_kernel_spmd

F32 = mybir.dt.float32
BF16 = mybir.dt.bfloat16
AF = mybir.ActivationFunctionType
ALU = mybir.AluOpType
ENGS = ("pe", "act", "dve", "pool", "sp")

NCORES = 8
D = 1024
SEQ = 16384
NSEQ_CORE = 4
EPS = 1e-6


class T:
    __slots__ = ("name", "ap", "last_w", "readers", "dsem", "dval", "excl", "last_acc", "t")

    def __init__(self, name, ap, excl=False):
        self.name = name
        self.ap = ap
        self.last_w = None
        self.readers = []
        self.dsem = {}
        self.dval = {}
        self.excl = excl
        self.last_acc = {}
        self.t = self

    def __getitem__(self, k):
        return self.ap[k]


class V:
    __slots__ = ("t", "ap")

    def __init__(self, t, ap):
        self.t = t
        self.ap = ap

    def __getitem__(self, k):
        return self.ap[k]


class Ins:
    __slots__ = ("eng", "fn", "reads", "writes", "is_dma", "deps", "signal", "tok", "dtile", "ndma")

    def __init__(self, eng, fn, reads, writes, is_dma=False, dtile=None, ndma=1):
        self.eng = eng
        self.fn = fn
        self.reads = reads
        self.writes = writes
        self.is_dma = is_dma
        self.deps = []
        self.signal = False
        self.tok = None
        self.dtile = dtile
        self.ndma = ndma


def _freeze(fn):
    if fn.__closure__ is None:
        return fn
    cells = []
    for c in fn.__closure__:
        try:
            cells.append(types.CellType(c.cell_contents))
        except ValueError:
            cells.append(c)
    g = types.FunctionType(fn.__code__, fn.__globals__, fn.__name__, fn.__defaults__, tuple(cells))
    g.__kwdefaults__ = fn.__kwdefaults__
    return g


class Prog:
    def __init__(self, nc, same_engine_sync=True):
        self.nc = nc
        self.ins = []
        self.stack = contextlib.ExitStack()
        self.same_engine_sync = same_engine_sync
        self.tiles = []

    def tile(self, name, ap):
        t = T(name, ap)
        self.tiles.append(t)
        return t

    def new(self, name, shape, dtype, psum=False):
        if psum:
            h = self.stack.enter_context(self.nc.psum_tensor(name, list(shape), dtype))
        else:
            h = self.stack.enter_context(self.nc.sbuf_tensor(name, list(shape), dtype))
        return self.tile(name, h)

    def op(self, eng, fn, reads=(), writes=()):
        self.ins.append(Ins(eng, _freeze(fn), [x.t for x in reads], [x.t for x in writes]))

    def dma(self, eng, fns, reads=(), writes=(), dtile=None):
        if not isinstance(fns, (list, tuple)):
            fns = [fns]
        reads = [x.t for x in reads]
        writes = [x.t for x in writes]
        if dtile is None:
            dtile = (writes + reads)[0]
        self.ins.append(Ins(eng, [_freeze(f) for f in fns], reads, writes, True, dtile, len(fns)))

    def build(self):
        nc = self.nc
        ins = self.ins
        for idx, i in enumerate(ins):
            deps = set()
            for t in i.reads:
                if t.last_w is not None:
                    deps.add(t.last_w)
            for t in i.writes:
                if t.last_w is not None:
                    deps.add(t.last_w)
                deps.update(t.readers)
            for t in set(i.reads + i.writes):
                if t.excl:
                    for f_eng, f_idx in t.last_acc.items():
                        if f_eng != i.eng:
                            deps.add(f_idx)
                    t.last_acc[i.eng] = idx
            deps.discard(idx)
            for t in i.reads:
                if not i.is_dma:
                    t.readers = [r for r in t.readers if ins[r].is_dma or ins[r].eng != i.eng]
                if not t.readers or t.readers[-1] != idx:
                    t.readers.append(idx)
            for t in i.writes:
                t.last_w = idx
                t.readers = []
            keep = []
            for d in deps:
                p = ins[d]
                if (not p.is_dma) and (not i.is_dma) and p.eng == i.eng:
                    if p.eng == "pe" or not self.same_engine_sync:
                        continue
                keep.append(d)
            i.deps = keep
            for d in keep:
                ins[d].signal = True
        esem = {e: self.stack.enter_context(nc.semaphore("s_" + e)) for e in ENGS}
        ecnt = {e: 0 for e in ENGS}
        for i in ins:
            if i.is_dma:
                t = i.dtile
                kq = "sw" if i.eng == "pool" else "hw"
                if kq not in t.dsem:
                    t.dsem[kq] = self.stack.enter_context(nc.semaphore("d%s_%s" % (kq, t.name)))
                    t.dval[kq] = 0
                t.dval[kq] += 16 * i.ndma
                i.tok = (t.dsem[kq], t.dval[kq])
            elif i.signal:
                ecnt[i.eng] += 1
                i.tok = (esem[i.eng], ecnt[i.eng])
        progs = {e: [] for e in ENGS}
        waited = {e: {} for e in ENGS}
        nw = 0
        for i in ins:
            w = {}
            for d in i.deps:
                s, v = ins[d].tok
                k = id(s)
                if k not in w or w[k][1] < v:
                    w[k] = (s, v)
            for k, (s, v) in w.items():
                if waited[i.eng].get(k, 0) >= v:
                    continue
                waited[i.eng][k] = v
                progs[i.eng].append(("w", s, v))
                nw += 1
            progs[i.eng].append(("i", i))
        for t in self.tiles:
            for kq in t.dsem:
                progs["sp"].append(("w", t.dsem[kq], t.dval[kq]))
        self.stats = dict(n_ins=len(ins), n_waits=nw, ecnt=dict(ecnt))

        def run(name, e):
            for item in progs[name]:
                if item[0] == "w":
                    e.wait_ge(item[1], item[2])
                else:
                    i = item[1]
                    if i.is_dma:
                        for f in i.fn:
                            f(e).then_inc(i.tok[0], 16)
                    else:
                        r = i.fn(e)
                        if i.signal:
                            r.then_inc(i.tok[0], 1)

        with nc.Block() as block:
            @block.tensor
            def _(e):
                run("pe", e)

            @block.scalar
            def _(e):
                run("act", e)

            @block.vector
            def _(e):
                run("dve", e)

            @block.gpsimd
            def _(e):
                run("pool", e)

            @block.sync
            def _(e):
                run("sp", e)
        self.stack.close()


def _t5_bucket_np(rel):
    import jax
    with jax.default_device(jax.devices("cpu")[0]):
        return _t5_bucket_impl(rel)


def _t5_bucket_impl(rel):
    import jax.numpy as jnp
    import math
    rel = jnp.asarray(rel)
    half = 16
    max_exact = 8
    n = jnp.abs(rel)
    log_ratio = jnp.log(jnp.maximum(n, 1).astype(jnp.float32) / max_exact) / math.log(128 / max_exact)
    large = jnp.minimum(max_exact + (log_ratio * (half - max_exact)).astype(jnp.int32), half - 1)
    return np.asarray(jnp.where(rel > 0, half, 0) + jnp.where(n < max_exact, n, large))


C_EVEN = [(-512, 128), (-384, 128), (-256, 128), (-128, 128), (0, 64)]
C_ODD = [(-512, 64), (-448, 128), (-320, 128), (-192, 128), (-64, 128)]
B_EVEN = [(-128, 128), (0, 64)]
B_ODD = [(-128, 64), (-64, 128)]


def _rel_for(a, nk):
    j = np.arange(128)
    if nk == 64:
        j = j % 64
    q = np.arange(64)
    return a + j[:, None] - q[None, :]


def build_bias_tables(t5_bias, c_rel_bias):
    biasC = np.zeros((128, 2, 10, 4, 64), np.float32)
    for v, (a, nk) in enumerate(C_EVEN + C_ODD):
        idx = np.clip(_rel_for(a, nk), -256, 256) + 256
        for l in range(2):
            biasC[:, l, v] = np.transpose(c_rel_bias[l][idx], (0, 2, 1))
    biasB = np.zeros((128, 4, 2, 2, 64), np.float32)
    for v, (a, nk) in enumerate(B_EVEN + B_ODD):
        bk = _t5_bucket_np(_rel_for(a, nk))
        tb = t5_bias[bk]
        for g in range(2):
            for r in range(2):
                biasB[:, v, g, r] = tb[:, :, 2 * g + r]
    keep = [0, 2, 3, 4, 7, 8, 9]
    for v in (1, 5, 6):
        assert np.array_equal(biasC[:, :, v], biasC[:, :, 0])
    return np.ascontiguousarray(biasC[:, :, keep].reshape(128, 2, 7, 256)), biasB.reshape(128, 4, 256)


def build_program(n_pgroups):
    nc = bass.Bass("TRN2", target_bir_lowering=False)
    NTOK_P = n_pgroups * 512

    def din(name, shape, dt=F32):
        return nc.dram_tensor(name, list(shape), dt, kind="ExternalInput").ap()

    def dout(name, shape, dt=F32):
        return nc.dram_tensor(name, list(shape), dt, kind="ExternalOutput").ap()

    xTp = din("xTp", [128, 8, NTOK_P])
    xTs = din("xTs", [128, 8, 256])
    cT_d = din("cT", [128, 8, 5])
    st_d = din("st", [2, 4, 128, 2, 128])
    kbT_c = din("kbT_c", [2, 4, 128, 128])
    vb_c = din("vb_c", [2, 4, 128, 128])
    kcT_c = din("kcT_c", [2, 4, 128, 2, 512])
    vc_c = din("vc_c", [2, 4, 128, 4, 256])
    cbk = din("cbk", [2, 4, 128, 128])
    cbv = din("cbv", [2, 4, 128, 128])
    cck = din("cck", [2, 4, 512, 256])
    ccv = din("ccv", [2, 4, 512, 256])
    Wf = din("Wf", [2, 1024, 2048])
    Wt = din("Wt", [2, 1024, 1536])
    Wo = din("Wo", [2, 1024, 1024])
    Wu = din("Wu", [2, 1024, 4096])
    Wd = din("Wd", [2, 4096, 1024])
    Wa = din("Wa", [2, 1024, 6144])
    badaR = din("badaR", [128, 2, 48, 5])
    gmix_d = din("gmix", [128, 2, 8])
    gmlp_d = din("gmlp", [128, 2, 8])
    gfin_d = din("gfin", [128, 8])
    wa2_d = din("wa2", [32, 2, 256])
    anorm_d = din("anorm", [128, 2])
    sink_d = din("sinkT", [128, 2, 2])
    biasB_d = din("biasB", [128, 4, 256])
    biasC_d = din("biasC", [128, 2, 7, 256])
    lincl_d = din("lincl", [128, 128])
    lafter_d = din("lafter", [128, 128])
    mask_d = din("mask01", [128, 256])
    ident_d = din("ident", [128, 128])

    yTp = dout("yTp", [128, 8, NTOK_P])
    yTs = dout("yTs", [128, 8, 256])
    sgp = dout("sgp", [2, 128, 2, 128])
    kbp = dout("kbp", [2, 128, 128])
    vbp = dout("vbp", [2, 128, 128])
    kcp = dout("kcp", [2, 512, 256])
    vcp = dout("vcp", [2, 512, 256])
    sgs = dout("sgs", [2, 4, 128, 2, 128])
    kbs = dout("kbs", [2, 4, 128, 128])
    vbs = dout("vbs", [2, 4, 128, 128])
    kcs = dout("kcs", [2, 4, 512, 256])
    vcs = dout("vcs", [2, 4, 512, 256])

    P = Prog(nc)
    OUT = {n: P.tile(n, a) for n, a in [("yTp", yTp), ("yTs", yTs), ("sgp", sgp), ("kbp", kbp), ("vbp", vbp),
                                        ("kcp", kcp), ("vcp", vcp), ("sgs", sgs), ("kbs", kbs), ("vbs", vbs),
                                        ("kcs", kcs), ("vcs", vcs)]}

    def load_const(name, src, shape, dt=F32, eng="sp"):
        t = P.new(name, shape, dt)
        P.dma(eng, lambda e: e.dma_start(out=t[:], in_=src), writes=[t])
        return t

    cT = load_const("cT_sb", cT_d[:, :, :], [128, 8, 5])
    bada = load_const("bada_sb", badaR[:, :, :, :], [128, 2, 48, 5])
    gmix = load_const("gmix_sb", gmix_d[:, :, :], [128, 2, 8])
    gmlp = load_const("gmlp_sb", gmlp_d[:, :, :], [128, 2, 8])
    gfin = load_const("gfin_sb", gfin_d[:, :], [128, 8])
    wa2 = load_const("wa2_sb", wa2_d[:, :, :], [32, 2, 256])
    anorm = load_const("anorm_sb", anorm_d[:, :], [128, 2])
    sinkr = load_const("sink_sb", sink_d[:, :, :], [128, 2, 2])
    biasB = load_const("biasB_sb", biasB_d[:, :, :], [128, 4, 256], BF16, eng="pool")
    biasC = load_const("biasC_sb", biasC_d[:, :, :, :], [128, 2, 7, 256], BF16, eng="pool")
    ident = load_const("ident_sb", ident_d[:, :], [128, 128], BF16, eng="pool")
    CVMAP = {0: 0, 1: 0, 5: 0, 6: 0, 2: 1, 3: 2, 4: 3, 7: 4, 8: 5, 9: 6}
    lincl = load_const("lincl_sb", lincl_d[:, :], [128, 128])
    lafter = load_const("lafter_sb", lafter_d[:, :], [128, 128])
    mask01 = load_const("mask_sb", mask_d[:, :], [128, 256])

    stg_b = P.new("stg_b", [128, 256], F32)
    stg_c = P.new("stg_c", [128, 512], F32)
    ones_n = P.new("ones_n", [128, 128], BF16)
    ones_dv = P.new("ones_dv", [128, 128], BF16)
    ones_1 = P.new("ones_1", [128, 64], BF16)
    P.op("pool", lambda e: e.memset(ones_n[:], 1.0 / 1024.0), writes=[ones_n])
    P.op("pool", lambda e: e.memset(ones_dv[:], 1.0 / 128.0), writes=[ones_dv])
    P.op("pool", lambda e: e.memset(ones_1[:], 1.0), writes=[ones_1])
    sinke = P.new("sinke", [128, 2, 2], F32)
    P.op("act", lambda e: e.activation(out=sinke[:], in_=sinkr[:], func=AF.Exp), reads=[sinkr], writes=[sinke])

    banks = [P.stack.enter_context(nc.psum_tensor("bank%d" % i, [128, 512], F32)) for i in range(8)]
    bankT = [P.tile("bank%d" % i, banks[i]) for i in range(8)]
    for b in bankT:
        b.excl = True
    big = [bankT[0], bankT[1], bankT[2]]
    bigc = [0]

    def nextbig():
        b = big[bigc[0] % 3]
        bigc[0] += 1
        return b

    pz = V(bankT[4], banks[4][:, 0:256])
    pbrem = V(bankT[4], banks[4][:, 256:512])
    pbT = V(bankT[5], banks[5][:, 0:256])
    pat = V(bankT[5], banks[5][:, 256:512])
    po = V(bankT[6], banks[6][:, 0:256])
    pn = V(bankT[6], banks[6][:, 256:512])
    ss_bank = V(bankT[4], banks[4][:, 0:512])
    psc_c = V(bankT[3], banks[3][:, 0:256])
    pnd_c = V(bankT[3], banks[3][:, 256:512])
    psc_c1 = V(bankT[4], banks[4][:, 0:256])
    pnd_c1 = V(bankT[4], banks[4][:, 256:512])
    psc_b = V(bankT[7], banks[7][:, 0:256])
    pnd_b = V(bankT[7], banks[7][:, 256:512])

    NB = int(os.environ.get('KERNEL_NB', '5'))
    wbuf = [P.new("wbuf%d" % i, [128, 8, 512], BF16) for i in range(NB)]
    specs = []

    def wspec_layer(l):
        s = []
        for p in range(4):
            s.append(("wf%d_%d" % (l, p), Wf[l].rearrange("(kc p) n -> p kc n", p=128)[:, :, p * 512:(p + 1) * 512]))
        for p in range(3):
            s.append(("wt%d_%d" % (l, p), Wt[l].rearrange("(kc p) n -> p kc n", p=128)[:, :, p * 512:(p + 1) * 512]))
        for p in range(2):
            s.append(("wo%d_%d" % (l, p), Wo[l].rearrange("(kc p) n -> p kc n", p=128)[:, :, p * 512:(p + 1) * 512]))
        for fb in range(8):
            s.append(("wu%d_%d" % (l, fb), Wu[l].rearrange("(kc p) n -> p kc n", p=128)[:, :, fb * 512:(fb + 1) * 512]))
            s.append(("wd%d_%d" % (l, fb),
                      Wd[l][fb * 512:(fb + 1) * 512, :].rearrange("(fc p) (hf n) -> p fc hf n", p=128, hf=2)))
        return s

    for l in range(2):
        for p in range(12):
            specs.append(("wa%d_%d" % (l, p), Wa[l].rearrange("(kc p) n -> p kc n", p=128)[:, :, p * 512:(p + 1) * 512]))
    n_groups = n_pgroups + 1
    for g in range(n_groups):
        for l in range(2):
            specs.extend(wspec_layer(l))
    Wbf = nc.dram_tensor("Wbf", [50, 128, 4096], BF16, kind="Internal").ap()
    scratch = {}
    sidx = 0
    for l in range(2):
        groups = {}
        order = []
        for name, src in wspec_layer(l):
            kind = name[:2] + str(l)
            if kind not in groups:
                groups[kind] = []
                order.append(kind)
            groups[kind].append((sidx, src))
            scratch[name] = (sidx, kind)
            sidx += 1
        for kind in order:
            kt = P.tile("wbf_" + kind, Wbf)
            fns = []
            for (ix, src) in groups[kind]:
                if len(src.shape) == 4:
                    fns.append(lambda e, ix=ix, src=src: e.dma_start(
                        out=Wbf[ix].rearrange("p (fc hf n) -> p fc hf n", fc=4, hf=2), in_=src))
                else:
                    fns.append(lambda e, ix=ix, src=src: e.dma_start(out=Wbf[ix].rearrange("p (a b) -> p a b", a=8), in_=src))
            for name in [n_ for n_, v in scratch.items() if v[1] == kind]:
                scratch[name] = (scratch[name][0], kt)
            groups[kind] = (kt, fns)
        scratch["__order%d" % l] = [groups[k] for k in order]
    wstate = dict(issued=0, used=0, precast=False)
    PREF = NB - 3

    def w_issue_upto(n):
        while wstate["issued"] < min(n, len(specs)):
            k = wstate["issued"]
            buf = wbuf[k % NB]
            src = specs[k][1]
            if not specs[k][0].startswith("wa"):
                if not wstate["precast"]:
                    wstate["precast"] = True
                    for l_ in range(2):
                        for (kt, fns) in scratch["__order%d" % l_]:
                            P.dma("pool", fns, writes=[kt])
                ix, kt = scratch[specs[k][0]]
                P.dma("sp", lambda e, buf=buf, ix=ix: e.dma_start(out=buf[:], in_=Wbf[ix].rearrange("p (a b) -> p a b", a=8)),
                      reads=[kt], writes=[buf])
                wstate["issued"] += 1
                continue
            if len(src.shape) == 4:
                P.dma("pool", lambda e, buf=buf, src=src: e.dma_start(
                    out=buf[:].rearrange("p (fc hf) n -> p fc hf n", hf=2), in_=src), writes=[buf])
            else:
                P.dma("pool", lambda e, buf=buf, src=src: e.dma_start(out=buf[:], in_=src), writes=[buf])
            wstate["issued"] += 1

    def wget(prefix):
        k = wstate["used"]
        assert specs[k][0].startswith(prefix), (specs[k][0], prefix)
        w_issue_upto(k + 1 + PREF)
        wstate["used"] += 1
        return wbuf[k % NB]

    ce = P.new("ce", [128, 8, 5], F32)
    csil = P.new("csil", [128, 8, 5], BF16)
    P.op("act", lambda e: e.activation(out=ce[:], in_=cT[:], func=AF.Exp, scale=-1.0), reads=[cT], writes=[ce])
    P.op("dve", lambda e: e.tensor_scalar(ce[:], ce[:], 1.0, None, op0=ALU.add), reads=[ce], writes=[ce])
    P.op("dve", lambda e: e.reciprocal(ce[:], ce[:]), reads=[ce], writes=[ce])
    P.op("dve", lambda e: e.tensor_tensor(out=csil[:], in0=cT[:], in1=ce[:], op=ALU.mult), reads=[cT, ce], writes=[csil])
    mod = P.new("mod", [128, 2, 48, 5], F32)
    for l in range(2):
        pm = nextbig()
        for p in range(12):
            wb = wget("wa%d_%d" % (l, p))
            for q in range(4):
                oc = p * 4 + q
                for kc in range(8):
                    P.op("pe", lambda e, wb=wb, q=q, kc=kc, oc=oc, pm=pm: e.matmul(
                        pm[:, oc * 5:(oc + 1) * 5], lhsT=wb[:, kc, q * 128:(q + 1) * 128], rhs=csil[:, kc, :],
                        start=(kc == 0), stop=(kc == 7)), reads=[wb, csil], writes=[pm])
        P.op("dve", lambda e, l=l, pm=pm: e.tensor_tensor(
            out=mod[:, l, :, :], in0=pm[:, 0:240].rearrange("p (a b) -> p a b", b=5), in1=bada[:, l, :, :], op=ALU.add),
            reads=[pm, bada], writes=[mod])
    Amix = P.new("Amix", [128, 2, 8, 5], F32)
    Amlp = P.new("Amlp", [128, 2, 8, 5], F32)
    for l in range(2):
        for c in range(8):
            P.op("dve", lambda e, l=l, c=c: e.tensor_scalar(Amix[:, l, c, :], mod[:, l, 8 + c, :], 1.0, gmix[:, l, c:c + 1],
                                                            op0=ALU.add, op1=ALU.mult), reads=[mod, gmix], writes=[Amix])
            P.op("dve", lambda e, l=l, c=c: e.tensor_scalar(Amlp[:, l, c, :], mod[:, l, 32 + c, :], 1.0, gmlp[:, l, c:c + 1],
                                                            op0=ALU.add, op1=ALU.mult), reads=[mod, gmlp], writes=[Amlp])

    STAGE = int(os.environ.get("KERNEL_STAGE", "99"))

    class StopBuild(Exception):
        pass

    def stage(k):
        if STAGE < k:
            raise StopBuild()

    def mod_ap(l, kind, c, s):
        return mod[:, l, kind * 8 + c, s:s + 1]

    NR = 8
    ring_kcT = [P.new("rkcT%d" % l, [128, 2, NR * 128], BF16) for l in range(2)]
    ring_vc = [P.new("rvc%d" % l, [128, NR, 256], BF16) for l in range(2)]
    ring_kbT = [P.new("rkbT%d" % l, [128, NR * 128], BF16) for l in range(2)]
    ring_vb = [P.new("rvb%d" % l, [128, NR, 128], BF16) for l in range(2)]
    S_p = [P.new("S_p%d" % l, [128, 2, 128], F32) for l in range(2)]
    Sbf_p = [P.new("Sbf_p%d" % l, [128, 2, 128], BF16) for l in range(2)]
    for l in range(2):
        P.op("pool", lambda e, l=l: e.memset(S_p[l][:], 0.0), writes=[S_p[l]])
        P.op("pool", lambda e, l=l: e.memset(Sbf_p[l][:], 0.0), writes=[Sbf_p[l]])
    S_s = [P.new("S_s0", [128, 2, 128], F32)] * 4
    Sbf_s = [P.new("Sbf_s0", [128, 2, 128], BF16)] * 4
    s_kcT = [P.new("s_kcT0", [128, 2, 512], BF16)] * 4
    s_vc = [P.new("s_vc0", [128, 4, 256], BF16)] * 4
    s_kbT = [P.new("s_kbT0", [128, 128], BF16)] * 4
    s_vb = [P.new("s_vb0", [128, 1, 128], BF16)] * 4
    own_kcT = P.new("own_kcT", [128, 2, 256], BF16)
    own_vc = P.new("own_vc", [128, 2, 256], BF16)
    own_kbT = P.new("own_kbT", [128, 256], BF16)
    own_vb = P.new("own_vb", [128, 2, 128], BF16)

    xT = P.new("xT", [128, 8, 512], F32)
    hT = P.new("hT", [128, 8, 512], BF16)
    sq = [P.new("sq%d" % i, [128, 512], BF16) for i in range(4)]
    rstd = P.new("rstd", [128, 512], F32)
    tmpf = [P.new("tmpf%d" % i, [128, 512], F32) for i in range(2)]
    tmpc = [0]

    def nexttmp():
        t = tmpf[tmpc[0] % 2]
        tmpc[0] += 1
        return t

    qaT = P.new("qaT", [128, 2, 512], F32)
    kaT = P.new("kaT", [128, 2, 512], F32)
    gate = P.new("gate", [128, 4, 512], BF16)
    qbz = P.new("qbz", [128, 2, 2, 512], BF16)
    qcz = P.new("qcz", [128, 2, 2, 512], BF16)
    P.op("pool", lambda e: e.memset(qbz[:], 0.0), writes=[qbz])
    P.op("pool", lambda e: e.memset(qcz[:], 0.0), writes=[qcz])
    raT = P.new("raT", [32, 512], F32)
    P.op("pool", lambda e: e.memset(raT[:], 1.0), writes=[raT])
    mixT = P.new("mixT", [128, 8, 512], BF16)
    uT = [P.new("uT0", [128, 4, 512], BF16)]
    ka_tok = P.new("ka_tok", [128, 256], F32)
    va_bf = P.new("va_bf", [128, 512], BF16)
    ez = P.new("ez", [128, 256], F32)
    sp_t = P.new("sp_t", [128, 256], F32)
    EbT = P.new("EbT", [128, 2, 128], F32)
    EnbT = P.new("EnbT", [128, 2, 128], F32)
    qtz = P.new("qtz", [128, 2, 2, 128], BF16)
    P.op("pool", lambda e: e.memset(qtz[:], 0.0), writes=[qtz])
    ktT = P.new("ktT", [128, 2, 128], BF16)
    Ebrem = P.new("Ebrem", [128, 256], F32)
    khat_z = [P.new("khat_z%d" % i, [128, 256], BF16) for i in range(2)]
    attn_z = [P.new("attn_z%d" % i, [128, 4, 64], BF16) for i in range(2)]
    for i_ in range(2):
        P.op("pool", lambda e, i_=i_: e.memset(khat_z[i_][:], 0.0), writes=[khat_z[i_]])
        P.op("pool", lambda e, i_=i_: e.memset(attn_z[i_][:], 0.0), writes=[attn_z[i_]])
    osq = P.new("osq", [128, 256], BF16)
    orstd = P.new("orstd", [128, 256], F32)
    o1 = P.new("o1", [128, 256], F32)
    pTc_full = [P.new("pTc_f%d" % i, [128, 256], BF16) for i in range(4)]
    pTc_half = {0: P.new("pTc_h0", [128, 256], BF16), 64: P.new("pTc_h64", [128, 256], BF16)}
    pTb_full = [P.new("pTb_f0", [128, 256], BF16)]
    pTb_half = {0: P.new("pTb_h0", [128, 256], BF16), 64: P.new("pTb_h64", [128, 256], BF16)}
    pTc_full1 = [P.new("pTc1_f%d" % i, [128, 256], BF16) for i in range(4)]
    pTc_half1 = {0: P.new("pTc1_h0", [128, 256], BF16), 64: P.new("pTc1_h64", [128, 256], BF16)}
    for t_ in (pTc_half[0], pTc_half[64], pTb_half[0], pTb_half[64], pTc_half1[0], pTc_half1[64]):
        P.op("pool", lambda e, t_=t_: e.memset(t_[:], 0.0), writes=[t_])
    sbc = [0]
    rden = P.new("rden", [128, 2, 64], F32)
    rden_b = P.new("rden_b", [128, 2, 64], F32)
    rden_c1 = P.new("rden_c1", [128, 2, 64], F32)
    resC = [dict(psc=psc_c, pnd=pnd_c, full=pTc_full, half=pTc_half, rden=rden),
            dict(psc=psc_c1, pnd=pnd_c1, full=pTc_full1, half=pTc_half1, rden=rden_c1)]
    yst = [P.new("yst%d" % i, [128, 512], F32) for i in range(2)]

    def stats_chunk(c, TG):
        sqc = sq[c % 4]
        P.op("act", lambda e, c=c, sqc=sqc: e.activation(out=sqc[:, 0:TG], in_=xT[:, c, 0:TG], func=AF.Square),
             reads=[xT], writes=[sqc])

        def mm():
            P.op("pe", lambda e, c=c, sqc=sqc: e.matmul(ss_bank[:, 0:TG], lhsT=ones_n[:], rhs=sqc[:, 0:TG],
                                                         start=(c == 0), stop=(c == 7)), reads=[ones_n, sqc], writes=[ss_bank])
        return mm

    def stats_all(TG):
        for c in range(8):
            stats_chunk(c, TG)()

    def rstd_from_stats(TG):
        P.op("act", lambda e: e.activation(out=rstd[:, 0:TG], in_=ss_bank[:, 0:TG], func=AF.Ln, bias=EPS, scale=1.0),
             reads=[ss_bank], writes=[rstd])
        P.op("act", lambda e: e.activation(out=rstd[:, 0:TG], in_=rstd[:, 0:TG], func=AF.Exp, scale=-0.5),
             reads=[rstd], writes=[rstd])

    def norm_apply(TG, segs, Asel, Bsel, out_t):
        rstd_from_stats(TG)
        for c in range(8):
            t = nexttmp()
            P.op("dve", lambda e, c=c, t=t: e.tensor_tensor(out=t[:, 0:TG], in0=xT[:, c, 0:TG], in1=rstd[:, 0:TG], op=ALU.mult),
                 reads=[xT, rstd], writes=[t])
            for (c0, c1, s) in segs:
                a_ap, a_t = Asel(c, s)
                b_ap, b_t = Bsel(c, s)
                P.op("act", lambda e, c=c, t=t, c0=c0, c1=c1, a_ap=a_ap, b_ap=b_ap: e.activation(
                    out=out_t[:, c, c0:c1], in_=t[:, c0:c1], func=AF.Identity, bias=b_ap, scale=a_ap),
                    reads=[t, a_t, b_t], writes=[out_t])

    def gla_chunk(l, i, ci, S, Sbf):
        base = 64 * ci
        cols = slice(i * 128 + base, i * 128 + base + 64)
        az = attn_z[ci]
        for h in range(4):
            j, r = h // 2, h % 2
            P.op("pe", lambda e, h=h, j=j, r=r: e.matmul(
                pat[base:base + 64, h * 64:(h + 1) * 64], lhsT=ktT[:, j, base:base + 64],
                rhs=qtz[:, j, r, base:base + 64], start=True, stop=True), reads=[ktT, qtz], writes=[pat])
        yield
        P.op("dve", lambda e: e.tensor_tensor(out=az[base:base + 64, :, :],
                                              in0=pat[base:base + 64, :].rearrange("p (h t) -> p h t", t=64),
                                              in1=mask01[base:base + 64, :].rearrange("p (h t) -> p h t", t=64), op=ALU.mult),
             reads=[pat, mask01], writes=[az])
        yield
        for h in range(4):
            j, r = h // 2, h % 2
            P.op("pe", lambda e, h=h: e.matmul(po[:, h * 64:(h + 1) * 64], lhsT=va_bf[:, h * 128:(h + 1) * 128],
                                               rhs=az[:, h, :], start=True, stop=False),
                 reads=[va_bf, az], writes=[po])
            P.op("pe", lambda e, h=h, j=j, r=r: e.matmul(po[:, h * 64:(h + 1) * 64], lhsT=Sbf[:, j, :],
                                                         rhs=qtz[:, j, r, base:base + 64], start=False, stop=True),
                 reads=[Sbf, qtz], writes=[po])
        pss = nextbig()
        kz = khat_z[ci]
        for h in range(4):
            j = h // 2
            P.op("pe", lambda e, h=h, j=j, pss=pss: e.matmul(pss[:, h * 128:(h + 1) * 128],
                                                             lhsT=kz[:, j * 128:(j + 1) * 128],
                                                             rhs=va_bf[:, h * 128:(h + 1) * 128], start=True, stop=True),
                 reads=[kz, va_bf], writes=[pss])
        yield
        P.op("act", lambda e: e.activation(out=osq[:], in_=po[:], func=AF.Square), reads=[po], writes=[osq])
        for h in range(4):
            j, r = h // 2, h % 2
            P.op("dve", lambda e, h=h, j=j, r=r, pss=pss: e.scalar_tensor_tensor(
                out=S[r * 64:(r + 1) * 64, j, :], in0=S[r * 64:(r + 1) * 64, j, :],
                scalar=EbT[r * 64:(r + 1) * 64, j, base + 63:base + 64], in1=pss[r * 64:(r + 1) * 64, h * 128:(h + 1) * 128],
                op0=ALU.mult, op1=ALU.add), reads=[S, EbT, pss, Sbf, po], writes=[S])
        P.op("pool", lambda e: e.tensor_copy(out=Sbf[:], in_=S[:]), reads=[S], writes=[Sbf])
        yield
        P.op("pe", lambda e: e.matmul(pn[:], lhsT=ones_dv[:], rhs=osq[:], start=True, stop=True), reads=[ones_dv, osq], writes=[pn])
        yield
        P.op("act", lambda e: e.activation(out=orstd[:], in_=pn[:], func=AF.Ln, bias=EPS, scale=1.0), reads=[pn], writes=[orstd])
        P.op("act", lambda e: e.activation(out=orstd[:], in_=orstd[:], func=AF.Exp, scale=-0.5), reads=[orstd], writes=[orstd])
        yield
        P.op("dve", lambda e: e.tensor_tensor(out=o1[:], in0=po[:], in1=orstd[:], op=ALU.mult), reads=[po, orstd], writes=[o1])
        P.op("dve", lambda e: e.scalar_tensor_tensor(
            out=mixT[:, 0:4, cols], in0=o1[:].rearrange("p (h t) -> p h t", t=64), scalar=anorm[:, l:l + 1],
            in1=gate[:, :, cols], op0=ALU.mult, op1=ALU.mult), reads=[o1, anorm, gate], writes=[mixT])

    def attn_c(l, cols, piecesC, res):
        psc_c, pnd_c, pTc_full, pTc_half, rden = res["psc"], res["pnd"], res["full"], res["half"], res["rden"]
        pts = []
        nfull = 0
        for (kt_t, kfn, v_t, vfn, pb, nk, var) in piecesC:
            for h in range(4):
                j, r = h // 2, h % 2
                P.op("pe", lambda e, h=h, j=j, r=r, kfn=kfn, pb=pb, nk=nk: e.matmul(
                    psc_c[pb:pb + nk, h * 64:(h + 1) * 64], lhsT=kfn(j), rhs=qcz[:, j, r, cols], start=(h == 0), stop=False,
                    skip_group_check=True), reads=[kt_t, qcz], writes=[psc_c])
            P.op("pe", lambda e, pb=pb, nk=nk, var=var: e.matmul(
                psc_c[pb:pb + nk, 0:256], lhsT=ident[:, pb:pb + nk], rhs=biasC[:, l, CVMAP[var], :], start=False, stop=True,
                skip_group_check=True), reads=[ident, biasC], writes=[psc_c])
            yield
            if nk == 128:
                pT = pTc_full[nfull]
                nfull += 1
            else:
                pT = pTc_half[pb]
            P.op("act", lambda e, pT=pT, pb=pb, nk=nk: e.activation(out=pT[pb:pb + nk, :], in_=psc_c[pb:pb + nk, :], func=AF.Exp),
                 reads=[psc_c], writes=[pT])
            pts.append((pT, v_t, vfn))
        yield
        npc = len(pts)
        for h in range(4):
            j, r = h // 2, h % 2
            for pi, (pT, v_t, vfn) in enumerate(pts):
                P.op("pe", lambda e, h=h, j=j, r=r, vfn=vfn, pT=pT, pi=pi: e.matmul(
                    pnd_c[r * 64:(r + 1) * 64, j * 64:(j + 1) * 64], lhsT=vfn(h), rhs=pT[:, h * 64:(h + 1) * 64],
                    start=(pi == 0), stop=(pi == npc - 1)), reads=[v_t, pT], writes=[pnd_c])
            for pi, (pT, v_t, vfn) in enumerate(pts):
                P.op("pe", lambda e, h=h, j=j, r=r, pT=pT, pi=pi: e.matmul(
                    pnd_c[r * 64:(r + 1) * 64, 128 + j * 64:128 + (j + 1) * 64], lhsT=ones_1[:, :],
                    rhs=pT[:, h * 64:(h + 1) * 64], start=(pi == 0), stop=(pi == npc - 1)),
                    reads=[ones_1, pT], writes=[pnd_c])
            if h == 1:
                yield
        yield
        P.op("dve", lambda e: e.reciprocal(rden[:], pnd_c[:, 128:256].rearrange("p (j t) -> p j t", t=64)), reads=[pnd_c], writes=[rden])
        P.op("dve", lambda e: e.tensor_tensor(out=mixT[:, 6:8, cols], in0=pnd_c[:, 0:128].rearrange("p (j t) -> p j t", t=64),
                                              in1=rden[:], op=ALU.mult), reads=[pnd_c, rden], writes=[mixT])

    def attn_b(l, cols, piecesB):
        pts = []
        for (kt_t, kfn, v_t, vfn, pb, nk, var) in piecesB:
            for g in range(2):
                for r in range(2):
                    P.op("pe", lambda e, g=g, r=r, kfn=kfn, pb=pb, nk=nk: e.matmul(
                        psc_b[pb:pb + nk, g * 128 + r * 64:g * 128 + (r + 1) * 64], lhsT=kfn(None),
                        rhs=qbz[:, r, g, cols], start=(g == 0 and r == 0), stop=False, skip_group_check=True),
                        reads=[kt_t, qbz], writes=[psc_b])
            P.op("pe", lambda e, pb=pb, nk=nk, var=var: e.matmul(
                psc_b[pb:pb + nk, 0:256], lhsT=ident[:, pb:pb + nk], rhs=biasB[:, var, :], start=False, stop=True,
                skip_group_check=True), reads=[ident, biasB], writes=[psc_b])
            yield
            pT = pTb_full[0] if nk == 128 else pTb_half[pb]
            P.op("act", lambda e, pT=pT, pb=pb, nk=nk: e.activation(out=pT[pb:pb + nk, :], in_=psc_b[pb:pb + nk, :], func=AF.Exp),
                 reads=[psc_b], writes=[pT])
            pts.append((pT, v_t, vfn))
        yield
        npb = len(pts)
        for h in range(4):
            g, r = h // 2, h % 2
            for pi, (pT, v_t, vfn) in enumerate(pts):
                P.op("pe", lambda e, h=h, g=g, r=r, vfn=vfn, pT=pT, pi=pi: e.matmul(
                    pnd_b[r * 64:(r + 1) * 64, g * 64:(g + 1) * 64], lhsT=vfn(g),
                    rhs=pT[:, g * 128 + r * 64:g * 128 + (r + 1) * 64],
                    start=(pi == 0), stop=(pi == npb - 1)), reads=[v_t, pT], writes=[pnd_b])
            for pi, (pT, v_t, vfn) in enumerate(pts):
                P.op("pe", lambda e, h=h, g=g, r=r, pT=pT, pi=pi: e.matmul(
                    pnd_b[r * 64:(r + 1) * 64, 128 + g * 64:128 + (g + 1) * 64], lhsT=ones_1[:, :],
                    rhs=pT[:, g * 128 + r * 64:g * 128 + (r + 1) * 64],
                    start=(pi == 0), stop=(pi == npb - 1)), reads=[ones_1, pT], writes=[pnd_b])
        yield
        for j in range(2):
            P.op("dve", lambda e, j=j: e.tensor_scalar(rden_b[:, j, :], pnd_b[:, 128 + j * 64:128 + (j + 1) * 64], sinke[:, l, j:j + 1], None,
                                                       op0=ALU.add), reads=[pnd_b, sinke], writes=[rden_b])
        P.op("dve", lambda e: e.reciprocal(rden_b[:], rden_b[:]), reads=[rden_b], writes=[rden_b])
        P.op("dve", lambda e: e.tensor_tensor(out=mixT[:, 4:6, cols], in0=pnd_b[:, 0:128].rearrange("p (j t) -> p j t", t=64),
                                              in1=rden_b[:], op=ALU.mult), reads=[pnd_b, rden_b], writes=[mixT])

    def chain(*gs):
        for g_ in gs:
            yield from g_

    def run_interleaved(gens):
        gens = list(gens)
        while gens:
            for g_ in list(gens):
                try:
                    next(g_)
                except StopIteration:
                    gens.remove(g_)

    def process_group(gi, is_sample):
        NT = 2 if is_sample else 4
        TG = NT * 128
        if is_sample:
            segs = [(s * 64, (s + 1) * 64, 1 + s) for s in range(4)]
            src = xTs[:, :, :]
        else:
            segs = [(0, TG, 0)]
            src = xTp[:, :, gi * 512:(gi + 1) * 512]
        t0 = gi * 4
        P.dma("act", lambda e: e.dma_start(out=xT[:, :, 0:TG], in_=src), writes=[xT])
        for l in range(2):
            if is_sample:
                for s in range(4):
                    P.dma("sp", lambda e, s=s: e.dma_start(out=kbs[l, s, 0:64, :], in_=cbk[l, s, 64:128, :]), writes=[OUT["kbs"]])
                    P.dma("sp", lambda e, s=s: e.dma_start(out=vbs[l, s, 0:64, :], in_=cbv[l, s, 64:128, :]), writes=[OUT["vbs"]])
                    P.dma("sp", lambda e, s=s: e.dma_start(out=kcs[l, s, 0:448, :], in_=cck[l, s, 64:512, :]), writes=[OUT["kcs"]])
                    P.dma("sp", lambda e, s=s: e.dma_start(out=vcs[l, s, 0:448, :], in_=ccv[l, s, 64:512, :]), writes=[OUT["vcs"]])
            if l == 0:
                stats_all(TG)
            norm_apply(TG, segs,
                       lambda c, s: (Amix[:, l, c, s:s + 1], Amix),
                       lambda c, s: (mod_ap(l, 0, c, s), mod), hT)
            stage(2)
            if is_sample:
                rcol0 = 0
                kcT_dst, kbT_dst = own_kcT, own_kbT
            else:
                rcol0 = (t0 % NR) * 128
                kcT_dst, kbT_dst = ring_kcT[l], ring_kbT[l]
            for p in range(4):
                wb = wget("wf%d_%d" % (l, p))
                for q in range(4):
                    cc = p * 4 + q
                    if cc >= int(os.environ.get("KERNEL_CCMAX", "16")):
                        continue
                    M = 16 if cc == 15 else 128
                    pf = nextbig()
                    for kc in range(8):
                        P.op("pe", lambda e, wb=wb, q=q, kc=kc, pf=pf, M=M: e.matmul(
                            pf[0:M, 0:TG], lhsT=wb[:, kc, q * 128:q * 128 + M], rhs=hT[:, kc, 0:TG],
                            start=(kc == 0), stop=(kc == 7)), reads=[wb, hT], writes=[pf])
                    if os.environ.get("KERNEL_SKIPEVAC") == "1":
                        continue
                    if cc in (0, 1):
                        P.op("act", lambda e, cc=cc, pf=pf: e.copy(qaT[:, cc, 0:TG], pf[:, 0:TG]),
                             reads=[pf], writes=[qaT])
                    elif cc in (2, 3):
                        P.op("dve", lambda e, cc=cc, pf=pf: e.tensor_copy(out=kaT[:, cc - 2, 0:TG], in_=pf[:, 0:TG]),
                             reads=[pf], writes=[kaT])
                    elif cc in (4, 5, 6, 7):
                        t = nexttmp()
                        P.op("act", lambda e, pf=pf, t=t: e.activation(out=t[:, 0:TG], in_=pf[:, 0:TG], func=AF.Exp, scale=-1.0),
                             reads=[pf], writes=[t])
                        P.op("dve", lambda e, t=t: e.tensor_scalar(t[:, 0:TG], t[:, 0:TG], 1.0, None, op0=ALU.add), reads=[t], writes=[t])
                        P.op("dve", lambda e, t=t: e.reciprocal(t[:, 0:TG], t[:, 0:TG]), reads=[t], writes=[t])
                        P.op("dve", lambda e, cc=cc, pf=pf, t=t: e.tensor_tensor(out=gate[:, cc - 4, 0:TG], in0=pf[:, 0:TG], in1=t[:, 0:TG],
                                                                                  op=ALU.mult), reads=[pf, t], writes=[gate])
                    elif cc in (8, 9):
                        for g_ in range(2):
                            P.op("act", lambda e, cc=cc, pf=pf, g_=g_: e.mul(qbz[g_ * 64:(g_ + 1) * 64, cc - 8, g_, 0:TG],
                                                                             pf[g_ * 64:(g_ + 1) * 64, 0:TG], 0.125),
                                 reads=[pf], writes=[qbz])
                    elif cc == 10:
                        P.op("act", lambda e, pf=pf: e.copy(kbT_dst[:, rcol0:rcol0 + TG], pf[:, 0:TG]),
                             reads=[pf], writes=[kbT_dst])
                    elif cc in (11, 12):
                        for r_ in range(2):
                            P.op("act", lambda e, cc=cc, pf=pf, r_=r_: e.mul(qcz[r_ * 64:(r_ + 1) * 64, cc - 11, r_, 0:TG],
                                                                             pf[r_ * 64:(r_ + 1) * 64, 0:TG], 0.125),
                                 reads=[pf], writes=[qcz])
                    elif cc in (13, 14):
                        P.op("dve", lambda e, cc=cc, pf=pf: e.tensor_copy(out=kcT_dst[:, cc - 13, rcol0:rcol0 + TG], in_=pf[:, 0:TG]),
                             reads=[pf], writes=[kcT_dst])
                    else:
                        P.op("dve", lambda e, pf=pf: e.tensor_copy(out=raT[0:16, 0:TG], in_=pf[0:16, 0:TG]), reads=[pf], writes=[raT])
            stage(3)
            wts = [wget("wt%d_%d" % (l, p)) for p in range(3)]
            for i in range(NT):
                tcols = slice(i * 128, (i + 1) * 128)
                slot = (t0 + i) % NR
                pbank = []
                for p in range(3):
                    pk = nextbig()
                    for kc in range(8):
                        P.op("pe", lambda e, p=p, kc=kc, pk=pk: e.matmul(pk[:, :], lhsT=hT[:, kc, tcols], rhs=wts[p][:, kc, :],
                                                                         start=(kc == 0), stop=(kc == 7)), reads=[hT, wts[p]], writes=[pk])
                    pbank.append(pk)
                pA, pB, pC = pbank
                P.op("dve", lambda e, pA=pA: e.tensor_copy(out=ka_tok[:], in_=pA[:, 0:256]), reads=[pA], writes=[ka_tok])
                P.op("act", lambda e, pB=pB: e.copy(va_bf[:], pB[:, :]), reads=[pB], writes=[va_bf])
                if is_sample:
                    vb_dst, vb_ap = own_vb, own_vb[:, i, :]
                    vc_dst, vc_ap = own_vc, own_vc[:, i, :]
                else:
                    vb_dst, vb_ap = ring_vb[l], ring_vb[l][:, slot, :]
                    vc_dst, vc_ap = ring_vc[l], ring_vc[l][:, slot, :]
                P.op("act", lambda e, pA=pA, vb_ap=vb_ap: e.copy(vb_ap, pA[:, 384:512]), reads=[pA], writes=[vb_dst])
                P.op("act", lambda e, pC=pC, vc_ap=vc_ap: e.copy(vc_ap, pC[:, 256:512]), reads=[pC], writes=[vc_dst])
                last_group = (not is_sample) and gi == n_pgroups - 1
                NOOUT = os.environ.get("KERNEL_NOOUT") == "1"
                OUTM = os.environ.get("KERNEL_OUTM", "CcBb")
                if (is_sample or last_group) and not NOOUT:
                    if "C" in OUTM:
                        P.op("dve", lambda e, pC=pC: e.tensor_copy(out=stg_c[:, 0:256], in_=pC[:, 0:256]), reads=[pC], writes=[stg_c])
                        P.op("act", lambda e, pC=pC: e.copy(stg_c[:, 256:512], pC[:, 256:512]), reads=[pC], writes=[stg_c])
                    if "c" not in OUTM:
                        pass
                    elif is_sample:
                        for ci in range(2):
                            s = 2 * i + ci
                            b0 = 64 * ci
                            P.dma(os.environ.get("KERNEL_OUTQ", "sp"), lambda e, s=s, b0=b0: e.dma_start(out=kcs[l, s, 448:512, :], in_=stg_c[b0:b0 + 64, 0:256]),
                                  reads=[stg_c], writes=[OUT["kcs"]])
                            P.dma(os.environ.get("KERNEL_OUTQ", "sp"), lambda e, s=s, b0=b0: e.dma_start(out=vcs[l, s, 448:512, :], in_=stg_c[b0:b0 + 64, 256:512]),
                                  reads=[stg_c], writes=[OUT["vcs"]])
                    else:
                        P.dma("sp", lambda e, i=i: e.dma_start(out=kcp[l, i * 128:(i + 1) * 128, :], in_=stg_c[:, 0:256]),
                              reads=[stg_c], writes=[OUT["kcp"]])
                        P.dma("sp", lambda e, i=i: e.dma_start(out=vcp[l, i * 128:(i + 1) * 128, :], in_=stg_c[:, 256:512]),
                              reads=[stg_c], writes=[OUT["vcp"]])
                if (is_sample or (last_group and i == NT - 1)) and not NOOUT:
                    if "B" in OUTM:
                        P.op("dve", lambda e, pA=pA: e.tensor_copy(out=stg_b[:, 0:128], in_=pA[:, 256:384]), reads=[pA], writes=[stg_b])
                        P.op("act", lambda e, pA=pA: e.copy(stg_b[:, 128:256], pA[:, 384:512]), reads=[pA], writes=[stg_b])
                    if "b" not in OUTM:
                        pass
                    elif is_sample:
                        for ci in range(2):
                            s = 2 * i + ci
                            b0 = 64 * ci
                            P.dma(os.environ.get("KERNEL_OUTQ", "sp"), lambda e, s=s, b0=b0: e.dma_start(out=kbs[l, s, 64:128, :], in_=stg_b[b0:b0 + 64, 0:128]),
                                  reads=[stg_b], writes=[OUT["kbs"]])
                            P.dma(os.environ.get("KERNEL_OUTQ", "sp"), lambda e, s=s, b0=b0: e.dma_start(out=vbs[l, s, 64:128, :], in_=stg_b[b0:b0 + 64, 128:256]),
                                  reads=[stg_b], writes=[OUT["vbs"]])
                    else:
                        P.dma("sp", lambda e: e.dma_start(out=kbp[l, :, :], in_=stg_b[:, 0:128]), reads=[stg_b], writes=[OUT["kbp"]])
                        P.dma("sp", lambda e: e.dma_start(out=vbp[l, :, :], in_=stg_b[:, 128:256]), reads=[stg_b], writes=[OUT["vbp"]])
                stage(4)
                P.op("pe", lambda e: e.matmul(pz[:], lhsT=raT[0:32, tcols], rhs=wa2[:, l, :], start=True, stop=True),
                     reads=[raT, wa2], writes=[pz])
                P.op("act", lambda e: e.activation(out=ez[:], in_=pz[:], func=AF.Exp, scale=-1.0), reads=[pz], writes=[ez])
                P.op("act", lambda e: e.activation(out=sp_t[:], in_=ez[:], func=AF.Ln, bias=1.0, scale=1.0), reads=[ez], writes=[sp_t])
                for j in range(2):
                    P.op("pe", lambda e, j=j: e.matmul(pbT[:, j * 128:(j + 1) * 128], lhsT=sp_t[:, j * 128:(j + 1) * 128], rhs=lincl[:],
                                                       start=True, stop=True), reads=[sp_t, lincl], writes=[pbT])
                P.op("pe", lambda e: e.matmul(pbrem[:], lhsT=lafter[:], rhs=sp_t[:], start=True, stop=True),
                     reads=[lafter, sp_t], writes=[pbrem])
                P.op("act", lambda e: e.activation(out=EbT[:], in_=pbT[:].rearrange("p (j t) -> p j t", t=128), func=AF.Exp),
                     reads=[pbT], writes=[EbT])
                P.op("act", lambda e: e.activation(out=EnbT[:], in_=pbT[:].rearrange("p (j t) -> p j t", t=128), func=AF.Exp, scale=-1.0),
                     reads=[pbT], writes=[EnbT])
                P.op("act", lambda e: e.activation(out=Ebrem[:], in_=pbrem[:], func=AF.Exp), reads=[pbrem], writes=[Ebrem])
                for r_ in range(2):
                    P.op("dve", lambda e, r_=r_: e.scalar_tensor_tensor(
                        out=qtz[r_ * 64:(r_ + 1) * 64, :, r_, :], in0=qaT[r_ * 64:(r_ + 1) * 64, :, tcols], scalar=0.125,
                        in1=EbT[r_ * 64:(r_ + 1) * 64, :, :], op0=ALU.mult, op1=ALU.mult), reads=[qaT, EbT], writes=[qtz])
                P.op("dve", lambda e: e.tensor_tensor(out=ktT[:], in0=kaT[:, :, tcols], in1=EnbT[:], op=ALU.mult),
                     reads=[kaT, EnbT], writes=[ktT])
                for c_ in range(2):
                    P.op("dve", lambda e, c_=c_: e.tensor_tensor(out=khat_z[c_][c_ * 64:(c_ + 1) * 64, :], in0=ka_tok[c_ * 64:(c_ + 1) * 64, :],
                                                                 in1=Ebrem[c_ * 64:(c_ + 1) * 64, :], op=ALU.mult),
                         reads=[ka_tok, Ebrem], writes=[khat_z[c_]])
                stage(5)
                chunk_gens = []
                for ci in range(2):
                    cols = slice(i * 128 + 64 * ci, i * 128 + 64 * ci + 64)
                    if is_sample:
                        s = 2 * i + ci
                        S, Sbf = S_s[s], Sbf_s[s]
                        P.dma("sp", lambda e, s=s: e.dma_start(out=S_s[s][:], in_=st_d[l, s, :, :, :]), writes=[S_s[s]])
                        P.op("pool", lambda e, s=s: e.tensor_copy(out=Sbf_s[s][:], in_=S_s[s][:]), reads=[S_s[s]], writes=[Sbf_s[s]])
                        P.dma("pool", lambda e, s=s: e.dma_start(out=s_kcT[s][:], in_=kcT_c[l, s, :, :, :]), writes=[s_kcT[s]])
                        P.dma("pool", lambda e, s=s: e.dma_start(out=s_vc[s][:], in_=vc_c[l, s, :, :, :]), writes=[s_vc[s]])
                        P.dma("pool", lambda e, s=s: e.dma_start(out=s_kbT[s][:], in_=kbT_c[l, s, :, :]), writes=[s_kbT[s]])
                        P.dma("pool", lambda e, s=s: e.dma_start(out=s_vb[s][:, 0, :], in_=vb_c[l, s, :, :]), writes=[s_vb[s]])
                    else:
                        S, Sbf = S_p[l], Sbf_p[l]
                    gen_gla = gla_chunk(l, i, ci, S, Sbf)
                    stage(6)
                    pb_own = 64 * ci
                    if is_sample:
                        pC_l = []
                        for k4 in range(4):
                            pC_l.append((s_kcT[s], (lambda j, k4=k4, s=s: s_kcT[s][:, j, k4 * 128:(k4 + 1) * 128]),
                                         s_vc[s], (lambda h, k4=k4, s=s: s_vc[s][:, k4, h * 64:(h + 1) * 64]), 0, 128, k4))
                        pC_l.append((own_kcT, (lambda j, i=i, pb_own=pb_own: own_kcT[:, j, i * 128 + pb_own:i * 128 + pb_own + 64]),
                                     own_vc, (lambda h, i=i: own_vc[:, i, h * 64:(h + 1) * 64]), pb_own, 64, 4))
                        pB_l = [(s_kbT[s], (lambda j, s=s: s_kbT[s][:, 0:128]),
                                 s_vb[s], (lambda g, s=s: s_vb[s][:, 0, g * 64:(g + 1) * 64]), 0, 128, 0),
                                (own_kbT, (lambda j, i=i, pb_own=pb_own: own_kbT[:, i * 128 + pb_own:i * 128 + pb_own + 64]),
                                 own_vb, (lambda g, i=i: own_vb[:, i, g * 64:(g + 1) * 64]), pb_own, 64, 1)]
                    else:
                        t = t0 + i
                        rk, rv, rkb, rvb = ring_kcT[l], ring_vc[l], ring_kbT[l], ring_vb[l]

                        def mkC(tk, pb, nk, var):
                            sl = tk % NR
                            return (rk, (lambda j, sl=sl, pb=pb, nk=nk, rk=rk: rk[:, j, sl * 128 + pb:sl * 128 + pb + nk]),
                                    rv, (lambda h, sl=sl, rv=rv: rv[:, sl, h * 64:(h + 1) * 64]), pb, nk, var)

                        def mkB(tk, pb, nk, var):
                            sl = tk % NR
                            return (rkb, (lambda j, sl=sl, pb=pb, nk=nk, rkb=rkb: rkb[:, sl * 128 + pb:sl * 128 + pb + nk]),
                                    rvb, (lambda g, sl=sl, rvb=rvb: rvb[:, sl, g * 64:(g + 1) * 64]), pb, nk, var)
                        pC_l, pB_l = [], []
                        if ci == 0:
                            for k4 in range(4):
                                tk = t - 4 + k4
                                if tk >= 0:
                                    pC_l.append(mkC(tk, 0, 128, k4))
                            pC_l.append(mkC(t, 0, 64, 4))
                            if t - 1 >= 0:
                                pB_l.append(mkB(t - 1, 0, 128, 0))
                            pB_l.append(mkB(t, 0, 64, 1))
                        else:
                            if t - 4 >= 0:
                                pC_l.append(mkC(t - 4, 64, 64, 5))
                            for k4 in range(4):
                                tk = t - 3 + k4
                                if tk >= 0:
                                    pC_l.append(mkC(tk, 0, 128, 6 + k4))
                            if t - 1 >= 0:
                                pB_l.append(mkB(t - 1, 64, 64, 2))
                            pB_l.append(mkB(t, 0, 128, 3))
                    if is_sample:
                        run_interleaved([gen_gla, attn_c(l, cols, pC_l, resC[0]), attn_b(l, cols, pB_l)])
                        P.dma("sp", lambda e, s=s: e.dma_start(out=sgs[l, s, :, :, :], in_=S_s[s][:]), reads=[S_s[s]], writes=[OUT["sgs"]])
                    else:
                        chunk_gens.append((gen_gla, attn_c(l, cols, pC_l, resC[ci]), attn_b(l, cols, pB_l)))
                if not is_sample:
                    (g0, c0_, b0_), (g1, c1_, b1_) = chunk_gens
                    run_interleaved([chain(g0, g1), c0_, c1_, chain(b0_, b1_)])
                    if last_group and i == NT - 1:
                        P.dma("sp", lambda e: e.dma_start(out=sgp[l, :, :, :], in_=S_p[l][:]), reads=[S_p[l]], writes=[OUT["sgp"]])
            stage(7)
            wos = [wget("wo%d_%d" % (l, p)) for p in range(2)]
            pend = []
            for oc in range(8):
                wb = wos[oc // 4]
                q = oc % 4
                pf = nextbig()
                for kc in range(8):
                    P.op("pe", lambda e, wb=wb, q=q, kc=kc, pf=pf: e.matmul(pf[:, 0:TG], lhsT=wb[:, kc, q * 128:(q + 1) * 128], rhs=mixT[:, kc, 0:TG],
                                                                            start=(kc == 0), stop=(kc == 7)), reads=[wb, mixT], writes=[pf])
                for (c0, c1, s) in segs:
                    P.op("dve", lambda e, oc=oc, pf=pf, c0=c0, c1=c1, s=s: e.scalar_tensor_tensor(
                        out=xT[:, oc, c0:c1], in0=pf[:, c0:c1], scalar=mod_ap(l, 2, oc, s), in1=xT[:, oc, c0:c1],
                        op0=ALU.mult, op1=ALU.add), reads=[pf, mod, xT], writes=[xT])
                pend.append(stats_chunk(oc, TG))
                if len(pend) > 2:
                    pend.pop(0)()
            while pend:
                pend.pop(0)()
            stage(8)
            norm_apply(TG, segs,
                       lambda c, s: (Amlp[:, l, c, s:s + 1], Amlp),
                       lambda c, s: (mod_ap(l, 3, c, s), mod), hT)
            for fb in range(8):
                wu = wget("wu%d_%d" % (l, fb))
                wd = wget("wd%d_%d" % (l, fb))
                u = uT[0]
                for fc in range(4):
                    pf = nextbig()
                    for kc in range(8):
                        P.op("pe", lambda e, wu=wu, fc=fc, kc=kc, pf=pf: e.matmul(pf[:, 0:TG], lhsT=wu[:, kc, fc * 128:(fc + 1) * 128], rhs=hT[:, kc, 0:TG],
                                                                                  start=(kc == 0), stop=(kc == 7)), reads=[wu, hT], writes=[pf])
                    t = nexttmp()
                    P.op("act", lambda e, pf=pf, t=t: e.activation(out=t[:, 0:TG], in_=pf[:, 0:TG], func=AF.Relu), reads=[pf], writes=[t])
                    P.op("pool", lambda e, u=u, fc=fc, t=t: e.tensor_tensor(out=u[:, fc, 0:TG], in0=t[:, 0:TG], in1=t[:, 0:TG], op=ALU.mult),
                         reads=[t], writes=[u])
                for oc in range(8):
                    hf, q = oc // 4, oc % 4
                    pf = nextbig()
                    for fc in range(4):
                        P.op("pe", lambda e, wd=wd, fc=fc, hf=hf, q=q, pf=pf, u=u: e.matmul(
                            pf[:, 0:TG], lhsT=wd[:, fc * 2 + hf, q * 128:(q + 1) * 128], rhs=u[:, fc, 0:TG],
                            start=(fc == 0), stop=(fc == 3)), reads=[wd, u], writes=[pf])
                    for (c0, c1, s) in segs:
                        P.op("dve", lambda e, oc=oc, pf=pf, c0=c0, c1=c1, s=s: e.scalar_tensor_tensor(
                            out=xT[:, oc, c0:c1], in0=pf[:, c0:c1], scalar=mod_ap(l, 5, oc, s), in1=xT[:, oc, c0:c1],
                            op0=ALU.mult, op1=ALU.add), reads=[pf, mod, xT], writes=[xT])
                    if fb == 7:
                        pend.append(stats_chunk(oc, TG))
                        if len(pend) > 2:
                            pend.pop(0)()
                while fb == 7 and pend:
                    pend.pop(0)()
        rstd_from_stats(TG)
        for c in range(8):
            yc = yst[c % 2]
            P.op("dve", lambda e, c=c, yc=yc: e.scalar_tensor_tensor(out=yc[:, 0:TG], in0=xT[:, c, 0:TG], scalar=gfin[:, c:c + 1],
                                                                     in1=rstd[:, 0:TG], op0=ALU.mult, op1=ALU.mult),
                 reads=[xT, gfin, rstd], writes=[yc])
            if is_sample:
                P.dma("act", lambda e, c=c, yc=yc: e.dma_start(out=yTs[:, c, :], in_=yc[:, 0:TG]), reads=[yc], writes=[OUT["yTs"]])
            else:
                P.dma("act", lambda e, c=c, yc=yc: e.dma_start(out=yTp[:, c, gi * 512:(gi + 1) * 512], in_=yc[:, 0:TG]),
                      reads=[yc], writes=[OUT["yTp"]])

    try:
        stage(1)
        process_group(0, True)
        for gi in range(n_pgroups):
            process_group(gi, False)
        assert wstate["used"] == len(specs), (wstate, len(specs))
    except StopBuild:
        pass
    P.build()
    return nc, P.stats


def _feat_major(x2d):
    T_ = x2d.shape[0]
    return np.ascontiguousarray(x2d.reshape(T_, 8, 128).transpose(2, 1, 0))


def _from_feat_major(yT):
    T_ = yT.shape[2]
    return np.ascontiguousarray(yT.transpose(2, 1, 0).reshape(T_, 1024))


def _vec_fm(v):
    lead = v.shape[:-1]
    a = v.reshape(*lead, 8, 128)
    a = np.moveaxis(a, -1, 0)
    return np.ascontiguousarray(a)


_N_PGROUPS = int(os.environ.get("KERNEL_NPG", "32"))


def kernel(x_prompt, x_sample, c_prompt, c_sample, state_gla, cache_b_k, cache_b_v, cache_c_k, cache_c_v,
           w_ada, b_ada, norm_mix_g, norm_mlp_g, w_in, w_a2, b_a2, a_norm_g, b_sink, t5_bias, c_rel_bias,
           w_out, w_up, w_down, final_norm_g):
    f = lambda a: np.asarray(a, dtype=np.float32)
    x_prompt, x_sample, c_prompt, c_sample = f(x_prompt), f(x_sample), f(c_prompt), f(c_sample)
    state_gla, cache_b_k, cache_b_v, cache_c_k, cache_c_v = f(state_gla), f(cache_b_k), f(cache_b_v), f(cache_c_k), f(cache_c_v)
    w_ada, b_ada, norm_mix_g, norm_mlp_g, w_in, w_a2, b_a2 = f(w_ada), f(b_ada), f(norm_mix_g), f(norm_mlp_g), f(w_in), f(w_a2), f(b_a2)
    a_norm_g, b_sink, t5_bias, c_rel_bias, w_out, w_up, w_down, final_norm_g = (
        f(a_norm_g), f(b_sink), f(t5_bias), f(c_rel_bias), f(w_out), f(w_up), f(w_down), f(final_norm_g))
    npg = _N_PGROUPS
    nc, stats = build_program(npg)

    o = np.cumsum([0, 256, 256, 512, 512, 16, 256, 128, 128, 256, 256, 256])
    qa, ka, va, ga, ra, qb, kb, vb, qc, kc, vc = [np.arange(o[i], o[i + 1]) for i in range(11)]
    qb_re = np.concatenate([qb[0:64], qb[128:192], qb[64:128], qb[192:256]])
    feat_cols = np.concatenate([qa, ka, ga, qb_re, kb, qc, kc, ra])
    Wf = np.zeros((2, 1024, 2048), np.float32)
    Wf[:, :, :feat_cols.size] = w_in[:, :, feat_cols]
    tok_cols = np.concatenate([ka, kb, vb, va, kc, vc])
    Wt = np.ascontiguousarray(w_in[:, :, tok_cols])
    wa2 = np.zeros((32, 2, 256), np.float32)
    wa2[0:16] = w_a2.transpose(1, 0, 2)
    wa2[16] = b_a2
    badaR = np.ascontiguousarray(np.broadcast_to(
        b_ada.reshape(2, 48, 128).transpose(2, 0, 1)[:, :, :, None], (128, 2, 48, 5)))
    gmix = _vec_fm(norm_mix_g)
    gmlp = _vec_fm(norm_mlp_g)
    gfin = _vec_fm(final_norm_g)
    anorm = np.ascontiguousarray(a_norm_g.T)
    sinkT = np.zeros((128, 2, 2), np.float32)
    for l in range(2):
        for j in range(2):
            sinkT[0:64, l, j] = b_sink[l, 2 * j]
            sinkT[64:128, l, j] = b_sink[l, 2 * j + 1]
    biasC, biasB = build_bias_tables(t5_bias, c_rel_bias)
    sidx = np.arange(128)
    same = (sidx[:, None] // 64) == (sidx[None, :] // 64)
    lincl = np.where(same & (sidx[:, None] <= sidx[None, :]), -1.0 / 16.0, 0.0).astype(np.float32)
    lafter = np.where(same & (sidx[:, None] > sidx[None, :]), -1.0 / 16.0, 0.0).astype(np.float32)
    m = ((sidx[:, None] % 64) <= np.arange(64)[None, :]).astype(np.float32)
    mask01 = np.ascontiguousarray(np.broadcast_to(m[:, None, :], (128, 4, 64)).reshape(128, 256))

    xTp = _feat_major(x_prompt[0, :npg * 512])
    cache_k_b2 = cache_b_k.reshape(2, 32, 128, 128)
    cache_v_b2 = cache_b_v.reshape(2, 32, 128, 128)
    cache_k_c2 = cache_c_k.reshape(2, 32, 512, 256)
    cache_v_c2 = cache_c_v.reshape(2, 32, 512, 256)
    shared = dict(xTp=xTp, Wf=Wf, Wt=Wt, Wo=np.ascontiguousarray(w_out), Wu=np.ascontiguousarray(w_up), Wd=np.ascontiguousarray(w_down),
                  Wa=np.ascontiguousarray(w_ada), badaR=badaR, gmix=gmix, gmlp=gmlp, gfin=gfin, wa2=wa2, anorm=anorm,
                  sinkT=sinkT, biasB=biasB, biasC=biasC, lincl=lincl, lafter=lafter, mask01=mask01,
                  ident=np.eye(128, dtype=np.float32))
    in_maps = []
    for c in range(NCORES):
        bs = slice(c * 4, (c + 1) * 4)
        xs = x_sample[bs].reshape(256, 1024)
        call = np.concatenate([c_prompt, c_sample[bs]], axis=0)
        cT = np.ascontiguousarray(call.reshape(5, 8, 128).transpose(2, 1, 0))
        st = state_gla[:, bs].reshape(2, 4, 2, 2, 64, 128).transpose(0, 1, 3, 4, 2, 5).reshape(2, 4, 128, 2, 128)
        kbT_c = cache_k_b2[:, bs].transpose(0, 1, 3, 2)
        kcT_c = cache_c_k[:, bs].reshape(2, 4, 512, 2, 2, 64).transpose(0, 1, 4, 5, 3, 2).reshape(2, 4, 128, 2, 512)
        vc_c = cache_v_c2[:, bs].reshape(2, 4, 4, 128, 256).transpose(0, 1, 3, 2, 4)
        d = dict(shared)
        d.update(xTs=_feat_major(xs), cT=cT, st=np.ascontiguousarray(st), kbT_c=np.ascontiguousarray(kbT_c),
                 vb_c=np.ascontiguousarray(cache_v_b2[:, bs]), kcT_c=np.ascontiguousarray(kcT_c), vc_c=np.ascontiguousarray(vc_c),
                 cbk=np.ascontiguousarray(cache_k_b2[:, bs]), cbv=np.ascontiguousarray(cache_v_b2[:, bs]),
                 cck=np.ascontiguousarray(cache_k_c2[:, bs]), ccv=np.ascontiguousarray(cache_v_c2[:, bs]))
        in_maps.append(d)
    res = run_bass_kernel_spmd(nc, in_maps, core_ids=list(range(NCORES)))
    R = res.results

    def unS(a):
        lead = a.shape[:-3]
        a = a.reshape(*lead, 2, 64, 2, 128)
        a = np.moveaxis(a, -2, -4)
        return np.ascontiguousarray(a.reshape(*lead, 4, 64, 128))

    y_prompt = _from_feat_major(R[0]["yTp"])[None]
    y_sample = np.concatenate([_from_feat_major(R[c]["yTs"]).reshape(4, 64, 1024) for c in range(NCORES)], axis=0)
    sg_p = unS(R[0]["sgp"])[:, None]
    kb_p = R[0]["kbp"].reshape(2, 1, 128, 2, 64)
    vb_p = R[0]["vbp"].reshape(2, 1, 128, 2, 64)
    kc_p = R[0]["kcp"].reshape(2, 1, 512, 4, 64)
    vc_p = R[0]["vcp"].reshape(2, 1, 512, 4, 64)
    sg_s = np.concatenate([unS(R[c]["sgs"]) for c in range(NCORES)], axis=1)
    kb_s = np.concatenate([R[c]["kbs"].reshape(2, 4, 128, 2, 64) for c in range(NCORES)], axis=1)
    vb_s = np.concatenate([R[c]["vbs"].reshape(2, 4, 128, 2, 64) for c in range(NCORES)], axis=1)
    kc_s = np.concatenate([R[c]["kcs"].reshape(2, 4, 512, 4, 64) for c in range(NCORES)], axis=1)
    vc_s = np.concatenate([R[c]["vcs"].reshape(2, 4, 512, 4, 64) for c in range(NCORES)], axis=1)
    outs = (y_prompt, y_sample, sg_p, kb_p, vb_p, kc_p, vc_p, sg_s, kb_s, vb_s, kc_s, vc_s)
    return tuple(np.ascontiguousarray(o_, dtype=np.float32) for o_ in outs)
```

```python
import contextlib
import os
import types
import numpy as np
import concourse.bass as bass
import concourse.mybir as mybir
from concourse.bass_utils import run_bass_kernel_spmd

F32 = mybir.dt.float32
BF16 = mybir.dt.bfloat16
AF = mybir.ActivationFunctionType
ALU = mybir.AluOpType
ENGS = ("pe", "act", "dve", "pool", "sp")

NCORES = 8
D = 1024
SEQ = 16384
NSEQ_CORE = 4
EPS = 1e-6


class T:
    __slots__ = ("name", "ap", "last_w", "readers", "dsem", "dval", "excl", "last_acc", "t")

    def __init__(self, name, ap, excl=False):
        self.name = name
        self.ap = ap
        self.last_w = None
        self.readers = []
        self.dsem = {}
        self.dval = {}
        self.excl = excl
        self.last_acc = {}
        self.t = self

    def __getitem__(self, k):
        return self.ap[k]


class V:
    __slots__ = ("t", "ap")

    def __init__(self, t, ap):
        self.t = t
        self.ap = ap

    def __getitem__(self, k):
        return self.ap[k]


class Ins:
    __slots__ = ("eng", "fn", "reads", "writes", "is_dma", "deps", "signal", "tok", "dtile", "ndma")

    def __init__(self, eng, fn, reads, writes, is_dma=False, dtile=None, ndma=1):
        self.eng = eng
        self.fn = fn
        self.reads = reads
        self.writes = writes
        self.is_dma = is_dma
        self.deps = []
        self.signal = False
        self.tok = None
        self.dtile = dtile
        self.ndma = ndma


def _freeze(fn):
    if fn.__closure__ is None:
        return fn
    cells = []
    for c in fn.__closure__:
        try:
            cells.append(types.CellType(c.cell_contents))
        except ValueError:
            cells.append(c)
    g = types.FunctionType(fn.__code__, fn.__globals__, fn.__name__, fn.__defaults__, tuple(cells))
    g.__kwdefaults__ = fn.__kwdefaults__
    return g


class Prog:
    def __init__(self, nc, same_engine_sync=True):
        self.nc = nc
        self.ins = []
        self.stack = contextlib.ExitStack()
        self.same_engine_sync = same_engine_sync
        self.tiles = []

    def tile(self, name, ap):
        t = T(name, ap)
        self.tiles.append(t)
        return t

    def new(self, name, shape, dtype, psum=False):
        if psum:
            h = self.stack.enter_context(self.nc.psum_tensor(name, list(shape), dtype))
        else:
            h = self.stack.enter_context(self.nc.sbuf_tensor(name, list(shape), dtype))
        return self.tile(name, h)

    def op(self, eng, fn, reads=(), writes=()):
        self.ins.append(Ins(eng, _freeze(fn), [x.t for x in reads], [x.t for x in writes]))

    def dma(self, eng, fns, reads=(), writes=(), dtile=None):
        if not isinstance(fns, (list, tuple)):
            fns = [fns]
        reads = [x.t for x in reads]
        writes = [x.t for x in writes]
        if dtile is None:
            dtile = (writes + reads)[0]
        self.ins.append(Ins(eng, [_freeze(f) for f in fns], reads, writes, True, dtile, len(fns)))

    def build(self):
        nc = self.nc
        ins = self.ins
        for idx, i in enumerate(ins):
            deps = set()
            for t in i.reads:
                if t.last_w is not None:
                    deps.add(t.last_w)
            for t in i.writes:
                if t.last_w is not None:
                    deps.add(t.last_w)
                deps.update(t.readers)
            for t in set(i.reads + i.writes):
                if t.excl:
                    for f_eng, f_idx in t.last_acc.items():
                        if f_eng != i.eng:
                            deps.add(f_idx)
                    t.last_acc[i.eng] = idx
            deps.discard(idx)
            for t in i.reads:
                if not i.is_dma:
                    t.readers = [r for r in t.readers if ins[r].is_dma or ins[r].eng != i.eng]
                if not t.readers or t.readers[-1] != idx:
                    t.readers.append(idx)
            for t in i.writes:
                t.last_w = idx
                t.readers = []
            keep = []
            for d in deps:
                p = ins[d]
                if (not p.is_dma) and (not i.is_dma) and p.eng == i.eng:
                    if p.eng == "pe" or not self.same_engine_sync:
                        continue
                keep.append(d)
            i.deps = keep
            for d in keep:
                ins[d].signal = True
        esem = {e: self.stack.enter_context(nc.semaphore("s_" + e)) for e in ENGS}
        ecnt = {e: 0 for e in ENGS}
        for i in ins:
            if i.is_dma:
                t = i.dtile
                kq = "sw" if i.eng == "pool" else "hw"
                if kq not in t.dsem:
                    t.dsem[kq] = self.stack.enter_context(nc.semaphore("d%s_%s" % (kq, t.name)))
                    t.dval[kq] = 0
                t.dval[kq] += 16 * i.ndma
                i.tok = (t.dsem[kq], t.dval[kq])
            elif i.signal:
                ecnt[i.eng] += 1
                i.tok = (esem[i.eng], ecnt[i.eng])
        progs = {e: [] for e in ENGS}
        waited = {e: {} for e in ENGS}
        nw = 0
        for i in ins:
            w = {}
            for d in i.deps:
                s, v = ins[d].tok
                k = id(s)
                if k not in w or w[k][1] < v:
                    w[k] = (s, v)
            for k, (s, v) in w.items():
                if waited[i.eng].get(k, 0) >= v:
                    continue
                waited[i.eng][k] = v
                progs[i.eng].append(("w", s, v))
                nw += 1
            progs[i.eng].append(("i", i))
        for t in self.tiles:
            for kq in t.dsem:
                progs["sp"].append(("w", t.dsem[kq], t.dval[kq]))
        self.stats = dict(n_ins=len(ins), n_waits=nw, ecnt=dict(ecnt))

        def run(name, e):
            for item in progs[name]:
                if item[0] == "w":
                    e.wait_ge(item[1], item[2])
                else:
                    i = item[1]
                    if i.is_dma:
                        for f in i.fn:
                            f(e).then_inc(i.tok[0], 16)
                    else:
                        r = i.fn(e)
                        if i.signal:
                            r.then_inc(i.tok[0], 1)

        with nc.Block() as block:
            @block.tensor
            def _(e):
                run("pe", e)

            @block.scalar
            def _(e):
                run("act", e)

            @block.vector
            def _(e):
                run("dve", e)

            @block.gpsimd
            def _(e):
                run("pool", e)

            @block.sync
            def _(e):
                run("sp", e)
        self.stack.close()


def _t5_bucket_np(rel):
    import jax
    with jax.default_device(jax.devices("cpu")[0]):
        return _t5_bucket_impl(rel)


def _t5_bucket_impl(rel):
    import jax.numpy as jnp
    import math
    rel = jnp.asarray(rel)
    half = 16
    max_exact = 8
    n = jnp.abs(rel)
    log_ratio = jnp.log(jnp.maximum(n, 1).astype(jnp.float32) / max_exact) / math.log(128 / max_exact)
    large = jnp.minimum(max_exact + (log_ratio * (half - max_exact)).astype(jnp.int32), half - 1)
    return np.asarray(jnp.where(rel > 0, half, 0) + jnp.where(n < max_exact, n, large))


C_EVEN = [(-512, 128), (-384, 128), (-256, 128), (-128, 128), (0, 64)]
C_ODD = [(-512, 64), (-448, 128), (-320, 128), (-192, 128), (-64, 128)]
B_EVEN = [(-128, 128), (0, 64)]
B_ODD = [(-128, 64), (-64, 128)]


def _rel_for(a, nk):
    j = np.arange(128)
    if nk == 64:
        j = j % 64
    q = np.arange(64)
    return a + j[:, None] - q[None, :]


def build_bias_tables(t5_bias, c_rel_bias):
    biasC = np.zeros((128, 2, 10, 4, 64), np.float32)
    for v, (a, nk) in enumerate(C_EVEN + C_ODD):
        idx = np.clip(_rel_for(a, nk), -256, 256) + 256
        for l in range(2):
            biasC[:, l, v] = np.transpose(c_rel_bias[l][idx], (0, 2, 1))
    biasB = np.zeros((128, 4, 2, 2, 64), np.float32)
    for v, (a, nk) in enumerate(B_EVEN + B_ODD):
        bk = _t5_bucket_np(_rel_for(a, nk))
        tb = t5_bias[bk]
        for g in range(2):
            for r in range(2):
                biasB[:, v, g, r] = tb[:, :, 2 * g + r]
    keep = [0, 2, 3, 4, 7, 8, 9]
    for v in (1, 5, 6):
        assert np.array_equal(biasC[:, :, v], biasC[:, :, 0])
    return np.ascontiguousarray(biasC[:, :, keep].reshape(128, 2, 7, 256)), biasB.reshape(128, 4, 256)


def build_program(n_pgroups):
    nc = bass.Bass("TRN2", target_bir_lowering=False)
    NTOK_P = n_pgroups * 512

    def din(name, shape, dt=F32):
        return nc.dram_tensor(name, list(shape), dt, kind="ExternalInput").ap()

    def dout(name, shape, dt=F32):
        return nc.dram_tensor(name, list(shape), dt, kind="ExternalOutput").ap()

    xTp = din("xTp", [128, 8, NTOK_P])
    xTs = din("xTs", [128, 8, 256])
    cT_d = din("cT", [128, 8, 5])
    st_d = din("st", [2, 4, 128, 2, 128])
    kbT_c = din("kbT_c", [2, 4, 128, 128])
    vb_c = din("vb_c", [2, 4, 128, 128])
    kcT_c = din("kcT_c", [2, 4, 128, 2, 512])
    vc_c = din("vc_c", [2, 4, 128, 4, 256])
    cbk = din("cbk", [2, 4, 128, 128])
    cbv = din("cbv", [2, 4, 128, 128])
    cck = din("cck", [2, 4, 512, 256])
    ccv = din("ccv", [2, 4, 512, 256])
    Wf = din("Wf", [2, 1024, 2048])
    Wt = din("Wt", [2, 1024, 1536])
    Wo = din("Wo", [2, 1024, 1024])
    Wu = din("Wu", [2, 1024, 4096])
    Wd = din("Wd", [2, 4096, 1024])
    Wa = din("Wa", [2, 1024, 6144])
    badaR = din("badaR", [128, 2, 48, 5])
    gmix_d = din("gmix", [128, 2, 8])
    gmlp_d = din("gmlp", [128, 2, 8])
    gfin_d = din("gfin", [128, 8])
    wa2_d = din("wa2", [32, 2, 256])
    anorm_d = din("anorm", [128, 2])
    sink_d = din("sinkT", [128, 2, 2])
    biasB_d = din("biasB", [128, 4, 256])
    biasC_d = din("biasC", [128, 2, 7, 256])
    lincl_d = din("lincl", [128, 128])
    lafter_d = din("lafter", [128, 128])
    mask_d = din("mask01", [128, 256])
    ident_d = din("ident", [128, 128])

    yTp = dout("yTp", [128, 8, NTOK_P])
    yTs = dout("yTs", [128, 8, 256])
    sgp = dout("sgp", [2, 128, 2, 128])
    kbp = dout("kbp", [2, 128, 128])
    vbp = dout("vbp", [2, 128, 128])
    kcp = dout("kcp", [2, 512, 256])
    vcp = dout("vcp", [2, 512, 256])
    sgs = dout("sgs", [2, 4, 128, 2, 128])
    kbs = dout("kbs", [2, 4, 128, 128])
    vbs = dout("vbs", [2, 4, 128, 128])
    kcs = dout("kcs", [2, 4, 512, 256])
    vcs = dout("vcs", [2, 4, 512, 256])

    P = Prog(nc)
    OUT = {n: P.tile(n, a) for n, a in [("yTp", yTp), ("yTs", yTs), ("sgp", sgp), ("kbp", kbp), ("vbp", vbp),
                                        ("kcp", kcp), ("vcp", vcp), ("sgs", sgs), ("kbs", kbs), ("vbs", vbs),
                                        ("kcs", kcs), ("vcs", vcs)]}

    def load_const(name, src, shape, dt=F32, eng="sp"):
        t = P.new(name, shape, dt)
        P.dma(eng, lambda e: e.dma_start(out=t[:], in_=src), writes=[t])
        return t

    cT = load_const("cT_sb", cT_d[:, :, :], [128, 8, 5])
    bada = load_const("bada_sb", badaR[:, :, :, :], [128, 2, 48, 5])
    gmix = load_const("gmix_sb", gmix_d[:, :, :], [128, 2, 8])
    gmlp = load_const("gmlp_sb", gmlp_d[:, :, :], [128, 2, 8])
    gfin = load_const("gfin_sb", gfin_d[:, :], [128, 8])
    wa2 = load_const("wa2_sb", wa2_d[:, :, :], [32, 2, 256])
    anorm = load_const("anorm_sb", anorm_d[:, :], [128, 2])
    sinkr = load_const("sink_sb", sink_d[:, :, :], [128, 2, 2])
    biasB = load_const("biasB_sb", biasB_d[:, :, :], [128, 4, 256], BF16, eng="pool")
    biasC = load_const("biasC_sb", biasC_d[:, :, :, :], [128, 2, 7, 256], BF16, eng="pool")
    ident = load_const("ident_sb", ident_d[:, :], [128, 128], BF16, eng="pool")
    CVMAP = {0: 0, 1: 0, 5: 0, 6: 0, 2: 1, 3: 2, 4: 3, 7: 4, 8: 5, 9: 6}
    lincl = load_const("lincl_sb", lincl_d[:, :], [128, 128])
    lafter = load_const("lafter_sb", lafter_d[:, :], [128, 128])
    mask01 = load_const("mask_sb", mask_d[:, :], [128, 256])

    stg_b = P.new("stg_b", [128, 256], F32)
    stg_c = P.new("stg_c", [128, 512], F32)
    ones_n = P.new("ones_n", [128, 128], BF16)
    ones_dv = P.new("ones_dv", [128, 128], BF16)
    ones_1 = P.new("ones_1", [128, 64], BF16)
    P.op("pool", lambda e: e.memset(ones_n[:], 1.0 / 1024.0), writes=[ones_n])
    P.op("pool", lambda e: e.memset(ones_dv[:], 1.0 / 128.0), writes=[ones_dv])
    P.op("pool", lambda e: e.memset(ones_1[:], 1.0), writes=[ones_1])
    sinke = P.new("sinke", [128, 2, 2], F32)
    P.op("act", lambda e: e.activation(out=sinke[:], in_=sinkr[:], func=AF.Exp), reads=[sinkr], writes=[sinke])

    banks = [P.stack.enter_context(nc.psum_tensor("bank%d" % i, [128, 512], F32)) for i in range(8)]
    bankT = [P.tile("bank%d" % i, banks[i]) for i in range(8)]
    for b in bankT:
        b.excl = True
    big = [bankT[0], bankT[1], bankT[2]]
    bigc = [0]

    def nextbig():
        b = big[bigc[0] % 3]
        bigc[0] += 1
        return b

    pz = V(bankT[4], banks[4][:, 0:256])
    pbrem = V(bankT[4], banks[4][:, 256:512])
    pbT = V(bankT[5], banks[5][:, 0:256])
    pat = V(bankT[5], banks[5][:, 256:512])
    po = V(bankT[6], banks[6][:, 0:256])
    pn = V(bankT[6], banks[6][:, 256:512])
    ss_bank = V(bankT[4], banks[4][:, 0:512])
    psc_c = V(bankT[3], banks[3][:, 0:256])
    pnd_c = V(bankT[3], banks[3][:, 256:512])
    psc_b = V(bankT[7], banks[7][:, 0:256])
    pnd_b = V(bankT[7], banks[7][:, 256:512])

    NB = int(os.environ.get('KERNEL_NB', '5'))
    wbuf = [P.new("wbuf%d" % i, [128, 8, 512], BF16) for i in range(NB)]
    specs = []

    def wspec_layer(l):
        s = []
        for p in range(4):
            s.append(("wf%d_%d" % (l, p), Wf[l].rearrange("(kc p) n -> p kc n", p=128)[:, :, p * 512:(p + 1) * 512]))
        for p in range(3):
            s.append(("wt%d_%d" % (l, p), Wt[l].rearrange("(kc p) n -> p kc n", p=128)[:, :, p * 512:(p + 1) * 512]))
        for p in range(2):
            s.append(("wo%d_%d" % (l, p), Wo[l].rearrange("(kc p) n -> p kc n", p=128)[:, :, p * 512:(p + 1) * 512]))
        for fb in range(8):
            s.append(("wu%d_%d" % (l, fb), Wu[l].rearrange("(kc p) n -> p kc n", p=128)[:, :, fb * 512:(fb + 1) * 512]))
            s.append(("wd%d_%d" % (l, fb),
                      Wd[l][fb * 512:(fb + 1) * 512, :].rearrange("(fc p) (hf n) -> p fc hf n", p=128, hf=2)))
        return s

    for l in range(2):
        for p in range(12):
            specs.append(("wa%d_%d" % (l, p), Wa[l].rearrange("(kc p) n -> p kc n", p=128)[:, :, p * 512:(p + 1) * 512]))
    n_groups = n_pgroups + 1
    for g in range(n_groups):
        for l in range(2):
            specs.extend(wspec_layer(l))
    Wbf = nc.dram_tensor("Wbf", [50, 128, 4096], BF16, kind="Internal").ap()
    scratch = {}
    sidx = 0
    for l in range(2):
        groups = {}
        order = []
        for name, src in wspec_layer(l):
            kind = name[:2] + str(l)
            if kind not in groups:
                groups[kind] = []
                order.append(kind)
            groups[kind].append((sidx, src))
            scratch[name] = (sidx, kind)
            sidx += 1
        for kind in order:
            kt = P.tile("wbf_" + kind, Wbf)
            fns = []
            for (ix, src) in groups[kind]:
                if len(src.shape) == 4:
                    fns.append(lambda e, ix=ix, src=src: e.dma_start(
                        out=Wbf[ix].rearrange("p (fc hf n) -> p fc hf n", fc=4, hf=2), in_=src))
                else:
                    fns.append(lambda e, ix=ix, src=src: e.dma_start(out=Wbf[ix].rearrange("p (a b) -> p a b", a=8), in_=src))
            for name in [n_ for n_, v in scratch.items() if v[1] == kind]:
                scratch[name] = (scratch[name][0], kt)
            groups[kind] = (kt, fns)
        scratch["__order%d" % l] = [groups[k] for k in order]
    wstate = dict(issued=0, used=0, precast=False)
    PREF = NB - 3

    def w_issue_upto(n):
        while wstate["issued"] < min(n, len(specs)):
            k = wstate["issued"]
            buf = wbuf[k % NB]
            src = specs[k][1]
            if not specs[k][0].startswith("wa"):
                if not wstate["precast"]:
                    wstate["precast"] = True
                    for l_ in range(2):
                        for (kt, fns) in scratch["__order%d" % l_]:
                            P.dma("pool", fns, writes=[kt])
                ix, kt = scratch[specs[k][0]]
                P.dma("sp", lambda e, buf=buf, ix=ix: e.dma_start(out=buf[:], in_=Wbf[ix].rearrange("p (a b) -> p a b", a=8)),
                      reads=[kt], writes=[buf])
                wstate["issued"] += 1
                continue
            if len(src.shape) == 4:
                P.dma("pool", lambda e, buf=buf, src=src: e.dma_start(
                    out=buf[:].rearrange("p (fc hf) n -> p fc hf n", hf=2), in_=src), writes=[buf])
            else:
                P.dma("pool", lambda e, buf=buf, src=src: e.dma_start(out=buf[:], in_=src), writes=[buf])
            wstate["issued"] += 1

    def wget(prefix):
        k = wstate["used"]
        assert specs[k][0].startswith(prefix), (specs[k][0], prefix)
        w_issue_upto(k + 1 + PREF)
        wstate["used"] += 1
        return wbuf[k % NB]

    ce = P.new("ce", [128, 8, 5], F32)
    csil = P.new("csil", [128, 8, 5], BF16)
    P.op("act", lambda e: e.activation(out=ce[:], in_=cT[:], func=AF.Exp, scale=-1.0), reads=[cT], writes=[ce])
    P.op("dve", lambda e: e.tensor_scalar(ce[:], ce[:], 1.0, None, op0=ALU.add), reads=[ce], writes=[ce])
    P.op("dve", lambda e: e.reciprocal(ce[:], ce[:]), reads=[ce], writes=[ce])
    P.op("dve", lambda e: e.tensor_tensor(out=csil[:], in0=cT[:], in1=ce[:], op=ALU.mult), reads=[cT, ce], writes=[csil])
    mod = P.new("mod", [128, 2, 48, 5], F32)
    for l in range(2):
        pm = nextbig()
        for p in range(12):
            wb = wget("wa%d_%d" % (l, p))
            for q in range(4):
                oc = p * 4 + q
                for kc in range(8):
                    P.op("pe", lambda e, wb=wb, q=q, kc=kc, oc=oc, pm=pm: e.matmul(
                        pm[:, oc * 5:(oc + 1) * 5], lhsT=wb[:, kc, q * 128:(q + 1) * 128], rhs=csil[:, kc, :],
                        start=(kc == 0), stop=(kc == 7)), reads=[wb, csil], writes=[pm])
        P.op("dve", lambda e, l=l, pm=pm: e.tensor_tensor(
            out=mod[:, l, :, :], in0=pm[:, 0:240].rearrange("p (a b) -> p a b", b=5), in1=bada[:, l, :, :], op=ALU.add),
            reads=[pm, bada], writes=[mod])
    Amix = P.new("Amix", [128, 2, 8, 5], F32)
    Amlp = P.new("Amlp", [128, 2, 8, 5], F32)
    for l in range(2):
        for c in range(8):
            P.op("dve", lambda e, l=l, c=c: e.tensor_scalar(Amix[:, l, c, :], mod[:, l, 8 + c, :], 1.0, gmix[:, l, c:c + 1],
                                                            op0=ALU.add, op1=ALU.mult), reads=[mod, gmix], writes=[Amix])
            P.op("dve", lambda e, l=l, c=c: e.tensor_scalar(Amlp[:, l, c, :], mod[:, l, 32 + c, :], 1.0, gmlp[:, l, c:c + 1],
                                                            op0=ALU.add, op1=ALU.mult), reads=[mod, gmlp], writes=[Amlp])

    STAGE = int(os.environ.get("KERNEL_STAGE", "99"))

    class StopBuild(Exception):
        pass

    def stage(k):
        if STAGE < k:
            raise StopBuild()

    def mod_ap(l, kind, c, s):
        return mod[:, l, kind * 8 + c, s:s + 1]

    NR = 8
    ring_kcT = [P.new("rkcT%d" % l, [128, 2, NR * 128], BF16) for l in range(2)]
    ring_vc = [P.new("rvc%d" % l, [128, NR, 256], BF16) for l in range(2)]
    ring_kbT = [P.new("rkbT%d" % l, [128, NR * 128], BF16) for l in range(2)]
    ring_vb = [P.new("rvb%d" % l, [128, NR, 128], BF16) for l in range(2)]
    S_p = [P.new("S_p%d" % l, [128, 2, 128], F32) for l in range(2)]
    Sbf_p = [P.new("Sbf_p%d" % l, [128, 2, 128], BF16) for l in range(2)]
    for l in range(2):
        P.op("pool", lambda e, l=l: e.memset(S_p[l][:], 0.0), writes=[S_p[l]])
        P.op("pool", lambda e, l=l: e.memset(Sbf_p[l][:], 0.0), writes=[Sbf_p[l]])
    S_s = [P.new("S_s0", [128, 2, 128], F32)] * 4
    Sbf_s = [P.new("Sbf_s0", [128, 2, 128], BF16)] * 4
    s_kcT = [P.new("s_kcT0", [128, 2, 512], BF16)] * 4
    s_vc = [P.new("s_vc0", [128, 4, 256], BF16)] * 4
    s_kbT = [P.new("s_kbT0", [128, 128], BF16)] * 4
    s_vb = [P.new("s_vb0", [128, 1, 128], BF16)] * 4
    own_kcT = P.new("own_kcT", [128, 2, 256], BF16)
    own_vc = P.new("own_vc", [128, 2, 256], BF16)
    own_kbT = P.new("own_kbT", [128, 256], BF16)
    own_vb = P.new("own_vb", [128, 2, 128], BF16)

    xT = P.new("xT", [128, 8, 512], F32)
    hT = P.new("hT", [128, 8, 512], BF16)
    sq = [P.new("sq%d" % i, [128, 512], BF16) for i in range(4)]
    rstd = P.new("rstd", [128, 512], F32)
    tmpf = [P.new("tmpf%d" % i, [128, 512], F32) for i in range(2)]
    tmpc = [0]

    def nexttmp():
        t = tmpf[tmpc[0] % 2]
        tmpc[0] += 1
        return t

    qaT = P.new("qaT", [128, 2, 512], F32)
    kaT = P.new("kaT", [128, 2, 512], F32)
    gate = P.new("gate", [128, 4, 512], BF16)
    qbz = P.new("qbz", [128, 2, 2, 512], BF16)
    qcz = P.new("qcz", [128, 2, 2, 512], BF16)
    P.op("pool", lambda e: e.memset(qbz[:], 0.0), writes=[qbz])
    P.op("pool", lambda e: e.memset(qcz[:], 0.0), writes=[qcz])
    raT = P.new("raT", [32, 512], F32)
    P.op("pool", lambda e: e.memset(raT[:], 1.0), writes=[raT])
    mixT = P.new("mixT", [128, 8, 512], BF16)
    uT = [P.new("uT%d" % i, [128, 4, 512], BF16) for i in range(2)]
    ka_tok = P.new("ka_tok", [128, 256], F32)
    va_bf = P.new("va_bf", [128, 512], BF16)
    ez = P.new("ez", [128, 256], F32)
    sp_t = P.new("sp_t", [128, 256], F32)
    EbT = P.new("EbT", [128, 2, 128], F32)
    EnbT = P.new("EnbT", [128, 2, 128], F32)
    qtz = P.new("qtz", [128, 2, 2, 128], BF16)
    P.op("pool", lambda e: e.memset(qtz[:], 0.0), writes=[qtz])
    ktT = P.new("ktT", [128, 2, 128], BF16)
    Ebrem = P.new("Ebrem", [128, 256], F32)
    khat_z = [P.new("khat_z%d" % i, [128, 256], BF16) for i in range(2)]
    attn_z = [P.new("attn_z%d" % i, [128, 4, 64], BF16) for i in range(2)]
    for i_ in range(2):
        P.op("pool", lambda e, i_=i_: e.memset(khat_z[i_][:], 0.0), writes=[khat_z[i_]])
        P.op("pool", lambda e, i_=i_: e.memset(attn_z[i_][:], 0.0), writes=[attn_z[i_]])
    osq = P.new("osq", [128, 256], BF16)
    orstd = P.new("orstd", [128, 256], F32)
    o1 = P.new("o1", [128, 256], F32)
    pTc_full = [P.new("pTc_f%d" % i, [128, 256], BF16) for i in range(4)]
    pTc_half = {0: P.new("pTc_h0", [128, 256], BF16), 64: P.new("pTc_h64", [128, 256], BF16)}
    pTb_full = [P.new("pTb_f0", [128, 256], BF16)]
    pTb_half = {0: P.new("pTb_h0", [128, 256], BF16), 64: P.new("pTb_h64", [128, 256], BF16)}
    for t_ in (pTc_half[0], pTc_half[64], pTb_half[0], pTb_half[64]):
        P.op("pool", lambda e, t_=t_: e.memset(t_[:], 0.0), writes=[t_])
    sbc = [0]
    rden = P.new("rden", [128, 2, 64], F32)
    rden_b = P.new("rden_b", [128, 2, 64], F32)
    yst = [P.new("yst%d" % i, [128, 512], F32) for i in range(2)]

    def stats_chunk(c, TG):
        sqc = sq[c % 4]
        P.op("act", lambda e, c=c, sqc=sqc: e.activation(out=sqc[:, 0:TG], in_=xT[:, c, 0:TG], func=AF.Square),
             reads=[xT], writes=[sqc])

        def mm():
            P.op("pe", lambda e, c=c, sqc=sqc: e.matmul(ss_bank[:, 0:TG], lhsT=ones_n[:], rhs=sqc[:, 0:TG],
                                                         start=(c == 0), stop=(c == 7)), reads=[ones_n, sqc], writes=[ss_bank])
        return mm

    def stats_all(TG):
        for c in range(8):
            stats_chunk(c, TG)()

    def rstd_from_stats(TG):
        P.op("act", lambda e: e.activation(out=rstd[:, 0:TG], in_=ss_bank[:, 0:TG], func=AF.Ln, bias=EPS, scale=1.0),
             reads=[ss_bank], writes=[rstd])
        P.op("act", lambda e: e.activation(out=rstd[:, 0:TG], in_=rstd[:, 0:TG], func=AF.Exp, scale=-0.5),
             reads=[rstd], writes=[rstd])

    def norm_apply(TG, segs, Asel, Bsel, out_t):
        rstd_from_stats(TG)
        for c in range(8):
            t = nexttmp()
            P.op("dve", lambda e, c=c, t=t: e.tensor_tensor(out=t[:, 0:TG], in0=xT[:, c, 0:TG], in1=rstd[:, 0:TG], op=ALU.mult),
                 reads=[xT, rstd], writes=[t])
            for (c0, c1, s) in segs:
                a_ap, a_t = Asel(c, s)
                b_ap, b_t = Bsel(c, s)
                P.op("act", lambda e, c=c, t=t, c0=c0, c1=c1, a_ap=a_ap, b_ap=b_ap: e.activation(
                    out=out_t[:, c, c0:c1], in_=t[:, c0:c1], func=AF.Identity, bias=b_ap, scale=a_ap),
                    reads=[t, a_t, b_t], writes=[out_t])

    def gla_chunk(l, i, ci, S, Sbf):
        base = 64 * ci
        cols = slice(i * 128 + base, i * 128 + base + 64)
        az = attn_z[ci]
        for h in range(4):
            j, r = h // 2, h % 2
            P.op("pe", lambda e, h=h, j=j, r=r: e.matmul(
                pat[base:base + 64, h * 64:(h + 1) * 64], lhsT=ktT[:, j, base:base + 64],
                rhs=qtz[:, j, r, base:base + 64], start=True, stop=True), reads=[ktT, qtz], writes=[pat])
        yield
        P.op("dve", lambda e: e.tensor_tensor(out=az[base:base + 64, :, :],
                                              in0=pat[base:base + 64, :].rearrange("p (h t) -> p h t", t=64),
                                              in1=mask01[base:base + 64, :].rearrange("p (h t) -> p h t", t=64), op=ALU.mult),
             reads=[pat, mask01], writes=[az])
        yield
        for h in range(4):
            j, r = h // 2, h % 2
            P.op("pe", lambda e, h=h: e.matmul(po[:, h * 64:(h + 1) * 64], lhsT=va_bf[:, h * 128:(h + 1) * 128],
                                               rhs=az[:, h, :], start=True, stop=False),
                 reads=[va_bf, az], writes=[po])
            P.op("pe", lambda e, h=h, j=j, r=r: e.matmul(po[:, h * 64:(h + 1) * 64], lhsT=Sbf[:, j, :],
                                                         rhs=qtz[:, j, r, base:base + 64], start=False, stop=True),
                 reads=[Sbf, qtz], writes=[po])
        pss = nextbig()
        kz = khat_z[ci]
        for h in range(4):
            j = h // 2
            P.op("pe", lambda e, h=h, j=j, pss=pss: e.matmul(pss[:, h * 128:(h + 1) * 128],
                                                             lhsT=kz[:, j * 128:(j + 1) * 128],
                                                             rhs=va_bf[:, h * 128:(h + 1) * 128], start=True, stop=True),
                 reads=[kz, va_bf], writes=[pss])
        yield
        P.op("act", lambda e: e.activation(out=osq[:], in_=po[:], func=AF.Square), reads=[po], writes=[osq])
        for h in range(4):
            j, r = h // 2, h % 2
            P.op("dve", lambda e, h=h, j=j, r=r, pss=pss: e.scalar_tensor_tensor(
                out=S[r * 64:(r + 1) * 64, j, :], in0=S[r * 64:(r + 1) * 64, j, :],
                scalar=EbT[r * 64:(r + 1) * 64, j, base + 63:base + 64], in1=pss[r * 64:(r + 1) * 64, h * 128:(h + 1) * 128],
                op0=ALU.mult, op1=ALU.add), reads=[S, EbT, pss, Sbf, po], writes=[S])
        P.op("pool", lambda e: e.tensor_copy(out=Sbf[:], in_=S[:]), reads=[S], writes=[Sbf])
        yield
        P.op("pe", lambda e: e.matmul(pn[:], lhsT=ones_dv[:], rhs=osq[:], start=True, stop=True), reads=[ones_dv, osq], writes=[pn])
        yield
        P.op("act", lambda e: e.activation(out=orstd[:], in_=pn[:], func=AF.Ln, bias=EPS, scale=1.0), reads=[pn], writes=[orstd])
        P.op("act", lambda e: e.activation(out=orstd[:], in_=orstd[:], func=AF.Exp, scale=-0.5), reads=[orstd], writes=[orstd])
        yield
        P.op("dve", lambda e: e.tensor_tensor(out=o1[:], in0=po[:], in1=orstd[:], op=ALU.mult), reads=[po, orstd], writes=[o1])
        P.op("dve", lambda e: e.scalar_tensor_tensor(
            out=mixT[:, 0:4, cols], in0=o1[:].rearrange("p (h t) -> p h t", t=64), scalar=anorm[:, l:l + 1],
            in1=gate[:, :, cols], op0=ALU.mult, op1=ALU.mult), reads=[o1, anorm, gate], writes=[mixT])

    def attn_c(l, cols, piecesC):
        pts = []
        nfull = 0
        for (kt_t, kfn, v_t, vfn, pb, nk, var) in piecesC:
            for h in range(4):
                j, r = h // 2, h % 2
                P.op("pe", lambda e, h=h, j=j, r=r, kfn=kfn, pb=pb, nk=nk: e.matmul(
                    psc_c[pb:pb + nk, h * 64:(h + 1) * 64], lhsT=kfn(j), rhs=qcz[:, j, r, cols], start=(h == 0), stop=False,
                    skip_group_check=True), reads=[kt_t, qcz], writes=[psc_c])
            P.op("pe", lambda e, pb=pb, nk=nk, var=var: e.matmul(
                psc_c[pb:pb + nk, 0:256], lhsT=ident[:, pb:pb + nk], rhs=biasC[:, l, CVMAP[var], :], start=False, stop=True,
                skip_group_check=True), reads=[ident, biasC], writes=[psc_c])
            yield
            if nk == 128:
                pT = pTc_full[nfull]
                nfull += 1
            else:
                pT = pTc_half[pb]
            P.op("act", lambda e, pT=pT, pb=pb, nk=nk: e.activation(out=pT[pb:pb + nk, :], in_=psc_c[pb:pb + nk, :], func=AF.Exp),
                 reads=[psc_c], writes=[pT])
            pts.append((pT, v_t, vfn))
        yield
        npc = len(pts)
        for h in range(4):
            j, r = h // 2, h % 2
            for pi, (pT, v_t, vfn) in enumerate(pts):
                P.op("pe", lambda e, h=h, j=j, r=r, vfn=vfn, pT=pT, pi=pi: e.matmul(
                    pnd_c[r * 64:(r + 1) * 64, j * 64:(j + 1) * 64], lhsT=vfn(h), rhs=pT[:, h * 64:(h + 1) * 64],
                    start=(pi == 0), stop=(pi == npc - 1)), reads=[v_t, pT], writes=[pnd_c])
            for pi, (pT, v_t, vfn) in enumerate(pts):
                P.op("pe", lambda e, h=h, j=j, r=r, pT=pT, pi=pi: e.matmul(
                    pnd_c[r * 64:(r + 1) * 64, 128 + j * 64:128 + (j + 1) * 64], lhsT=ones_1[:, :],
                    rhs=pT[:, h * 64:(h + 1) * 64], start=(pi == 0), stop=(pi == npc - 1)),
                    reads=[ones_1, pT], writes=[pnd_c])
            if h == 1:
                yield
        yield
        P.op("dve", lambda e: e.reciprocal(rden[:], pnd_c[:, 128:256].rearrange("p (j t) -> p j t", t=64)), reads=[pnd_c], writes=[rden])
        P.op("dve", lambda e: e.tensor_tensor(out=mixT[:, 6:8, cols], in0=pnd_c[:, 0:128].rearrange("p (j t) -> p j t", t=64),
                                              in1=rden[:], op=ALU.mult), reads=[pnd_c, rden], writes=[mixT])

    def attn_b(l, cols, piecesB):
        pts = []
        for (kt_t, kfn, v_t, vfn, pb, nk, var) in piecesB:
            for g in range(2):
                for r in range(2):
                    P.op("pe", lambda e, g=g, r=r, kfn=kfn, pb=pb, nk=nk: e.matmul(
                        psc_b[pb:pb + nk, g * 128 + r * 64:g * 128 + (r + 1) * 64], lhsT=kfn(None),
                        rhs=qbz[:, r, g, cols], start=(g == 0 and r == 0), stop=False, skip_group_check=True),
                        reads=[kt_t, qbz], writes=[psc_b])
            P.op("pe", lambda e, pb=pb, nk=nk, var=var: e.matmul(
                psc_b[pb:pb + nk, 0:256], lhsT=ident[:, pb:pb + nk], rhs=biasB[:, var, :], start=False, stop=True,
                skip_group_check=True), reads=[ident, biasB], writes=[psc_b])
            yield
            pT = pTb_full[0] if nk == 128 else pTb_half[pb]
            P.op("act", lambda e, pT=pT, pb=pb, nk=nk: e.activation(out=pT[pb:pb + nk, :], in_=psc_b[pb:pb + nk, :], func=AF.Exp),
                 reads=[psc_b], writes=[pT])
            pts.append((pT, v_t, vfn))
        yield
        npb = len(pts)
        for h in range(4):
            g, r = h // 2, h % 2
            for pi, (pT, v_t, vfn) in enumerate(pts):
                P.op("pe", lambda e, h=h, g=g, r=r, vfn=vfn, pT=pT, pi=pi: e.matmul(
                    pnd_b[r * 64:(r + 1) * 64, g * 64:(g + 1) * 64], lhsT=vfn(g),
                    rhs=pT[:, g * 128 + r * 64:g * 128 + (r + 1) * 64],
                    start=(pi == 0), stop=(pi == npb - 1)), reads=[v_t, pT], writes=[pnd_b])
            for pi, (pT, v_t, vfn) in enumerate(pts):
                P.op("pe", lambda e, h=h, g=g, r=r, pT=pT, pi=pi: e.matmul(
                    pnd_b[r * 64:(r + 1) * 64, 128 + g * 64:128 + (g + 1) * 64], lhsT=ones_1[:, :],
                    rhs=pT[:, g * 128 + r * 64:g * 128 + (r + 1) * 64],
                    start=(pi == 0), stop=(pi == npb - 1)), reads=[ones_1, pT], writes=[pnd_b])
        yield
        for j in range(2):
            P.op("dve", lambda e, j=j: e.tensor_scalar(rden_b[:, j, :], pnd_b[:, 128 + j * 64:128 + (j + 1) * 64], sinke[:, l, j:j + 1], None,
                                                       op0=ALU.add), reads=[pnd_b, sinke], writes=[rden_b])
        P.op("dve", lambda e: e.reciprocal(rden_b[:], rden_b[:]), reads=[rden_b], writes=[rden_b])
        P.op("dve", lambda e: e.tensor_tensor(out=mixT[:, 4:6, cols], in0=pnd_b[:, 0:128].rearrange("p (j t) -> p j t", t=64),
                                              in1=rden_b[:], op=ALU.mult), reads=[pnd_b, rden_b], writes=[mixT])

    def run_interleaved(gens):
        gens = list(gens)
        while gens:
            for g_ in list(gens):
                try:
                    next(g_)
                except StopIteration:
                    gens.remove(g_)

    def process_group(gi, is_sample):
        NT = 2 if is_sample else 4
        TG = NT * 128
        if is_sample:
            segs = [(s * 64, (s + 1) * 64, 1 + s) for s in range(4)]
            src = xTs[:, :, :]
        else:
            segs = [(0, TG, 0)]
            src = xTp[:, :, gi * 512:(gi + 1) * 512]
        t0 = gi * 4
        P.dma("act", lambda e: e.dma_start(out=xT[:, :, 0:TG], in_=src), writes=[xT])
        for l in range(2):
            if is_sample:
                for s in range(4):
                    P.dma("sp", lambda e, s=s: e.dma_start(out=kbs[l, s, 0:64, :], in_=cbk[l, s, 64:128, :]), writes=[OUT["kbs"]])
                    P.dma("sp", lambda e, s=s: e.dma_start(out=vbs[l, s, 0:64, :], in_=cbv[l, s, 64:128, :]), writes=[OUT["vbs"]])
                    P.dma("sp", lambda e, s=s: e.dma_start(out=kcs[l, s, 0:448, :], in_=cck[l, s, 64:512, :]), writes=[OUT["kcs"]])
                    P.dma("sp", lambda e, s=s: e.dma_start(out=vcs[l, s, 0:448, :], in_=ccv[l, s, 64:512, :]), writes=[OUT["vcs"]])
            if l == 0:
                stats_all(TG)
            norm_apply(TG, segs,
                       lambda c, s: (Amix[:, l, c, s:s + 1], Amix),
                       lambda c, s: (mod_ap(l, 0, c, s), mod), hT)
            stage(2)
            if is_sample:
                rcol0 = 0
                kcT_dst, kbT_dst = own_kcT, own_kbT
            else:
                rcol0 = (t0 % NR) * 128
                kcT_dst, kbT_dst = ring_kcT[l], ring_kbT[l]
            for p in range(4):
                wb = wget("wf%d_%d" % (l, p))
                for q in range(4):
                    cc = p * 4 + q
                    if cc >= int(os.environ.get("KERNEL_CCMAX", "16")):
                        continue
                    M = 16 if cc == 15 else 128
                    pf = nextbig()
                    for kc in range(8):
                        P.op("pe", lambda e, wb=wb, q=q, kc=kc, pf=pf, M=M: e.matmul(
                            pf[0:M, 0:TG], lhsT=wb[:, kc, q * 128:q * 128 + M], rhs=hT[:, kc, 0:TG],
                            start=(kc == 0), stop=(kc == 7)), reads=[wb, hT], writes=[pf])
                    if os.environ.get("KERNEL_SKIPEVAC") == "1":
                        continue
                    if cc in (0, 1):
                        P.op("act", lambda e, cc=cc, pf=pf: e.copy(qaT[:, cc, 0:TG], pf[:, 0:TG]),
                             reads=[pf], writes=[qaT])
                    elif cc in (2, 3):
                        P.op("dve", lambda e, cc=cc, pf=pf: e.tensor_copy(out=kaT[:, cc - 2, 0:TG], in_=pf[:, 0:TG]),
                             reads=[pf], writes=[kaT])
                    elif cc in (4, 5, 6, 7):
                        t = nexttmp()
                        P.op("act", lambda e, pf=pf, t=t: e.activation(out=t[:, 0:TG], in_=pf[:, 0:TG], func=AF.Exp, scale=-1.0),
                             reads=[pf], writes=[t])
                        P.op("dve", lambda e, t=t: e.tensor_scalar(t[:, 0:TG], t[:, 0:TG], 1.0, None, op0=ALU.add), reads=[t], writes=[t])
                        P.op("dve", lambda e, t=t: e.reciprocal(t[:, 0:TG], t[:, 0:TG]), reads=[t], writes=[t])
                        P.op("dve", lambda e, cc=cc, pf=pf, t=t: e.tensor_tensor(out=gate[:, cc - 4, 0:TG], in0=pf[:, 0:TG], in1=t[:, 0:TG],
                                                                                  op=ALU.mult), reads=[pf, t], writes=[gate])
                    elif cc in (8, 9):
                        for g_ in range(2):
                            P.op("act", lambda e, cc=cc, pf=pf, g_=g_: e.mul(qbz[g_ * 64:(g_ + 1) * 64, cc - 8, g_, 0:TG],
                                                                             pf[g_ * 64:(g_ + 1) * 64, 0:TG], 0.125),
                                 reads=[pf], writes=[qbz])
                    elif cc == 10:
                        P.op("act", lambda e, pf=pf: e.copy(kbT_dst[:, rcol0:rcol0 + TG], pf[:, 0:TG]),
                             reads=[pf], writes=[kbT_dst])
                    elif cc in (11, 12):
                        for r_ in range(2):
                            P.op("act", lambda e, cc=cc, pf=pf, r_=r_: e.mul(qcz[r_ * 64:(r_ + 1) * 64, cc - 11, r_, 0:TG],
                                                                             pf[r_ * 64:(r_ + 1) * 64, 0:TG], 0.125),
                                 reads=[pf], writes=[qcz])
                    elif cc in (13, 14):
                        P.op("dve", lambda e, cc=cc, pf=pf: e.tensor_copy(out=kcT_dst[:, cc - 13, rcol0:rcol0 + TG], in_=pf[:, 0:TG]),
                             reads=[pf], writes=[kcT_dst])
                    else:
                        P.op("dve", lambda e, pf=pf: e.tensor_copy(out=raT[0:16, 0:TG], in_=pf[0:16, 0:TG]), reads=[pf], writes=[raT])
            stage(3)
            wts = [wget("wt%d_%d" % (l, p)) for p in range(3)]
            for i in range(NT):
                tcols = slice(i * 128, (i + 1) * 128)
                slot = (t0 + i) % NR
                pbank = []
                for p in range(3):
                    pk = nextbig()
                    for kc in range(8):
                        P.op("pe", lambda e, p=p, kc=kc, pk=pk: e.matmul(pk[:, :], lhsT=hT[:, kc, tcols], rhs=wts[p][:, kc, :],
                                                                         start=(kc == 0), stop=(kc == 7)), reads=[hT, wts[p]], writes=[pk])
                    pbank.append(pk)
                pA, pB, pC = pbank
                P.op("dve", lambda e, pA=pA: e.tensor_copy(out=ka_tok[:], in_=pA[:, 0:256]), reads=[pA], writes=[ka_tok])
                P.op("act", lambda e, pB=pB: e.copy(va_bf[:], pB[:, :]), reads=[pB], writes=[va_bf])
                if is_sample:
                    vb_dst, vb_ap = own_vb, own_vb[:, i, :]
                    vc_dst, vc_ap = own_vc, own_vc[:, i, :]
                else:
                    vb_dst, vb_ap = ring_vb[l], ring_vb[l][:, slot, :]
                    vc_dst, vc_ap = ring_vc[l], ring_vc[l][:, slot, :]
                P.op("act", lambda e, pA=pA, vb_ap=vb_ap: e.copy(vb_ap, pA[:, 384:512]), reads=[pA], writes=[vb_dst])
                P.op("act", lambda e, pC=pC, vc_ap=vc_ap: e.copy(vc_ap, pC[:, 256:512]), reads=[pC], writes=[vc_dst])
                last_group = (not is_sample) and gi == n_pgroups - 1
                NOOUT = os.environ.get("KERNEL_NOOUT") == "1"
                OUTM = os.environ.get("KERNEL_OUTM", "CcBb")
                if (is_sample or last_group) and not NOOUT:
                    if "C" in OUTM:
                        P.op("dve", lambda e, pC=pC: e.tensor_copy(out=stg_c[:, 0:256], in_=pC[:, 0:256]), reads=[pC], writes=[stg_c])
                        P.op("act", lambda e, pC=pC: e.copy(stg_c[:, 256:512], pC[:, 256:512]), reads=[pC], writes=[stg_c])
                    if "c" not in OUTM:
                        pass
                    elif is_sample:
                        for ci in range(2):
                            s = 2 * i + ci
                            b0 = 64 * ci
                            P.dma(os.environ.get("KERNEL_OUTQ", "sp"), lambda e, s=s, b0=b0: e.dma_start(out=kcs[l, s, 448:512, :], in_=stg_c[b0:b0 + 64, 0:256]),
                                  reads=[stg_c], writes=[OUT["kcs"]])
                            P.dma(os.environ.get("KERNEL_OUTQ", "sp"), lambda e, s=s, b0=b0: e.dma_start(out=vcs[l, s, 448:512, :], in_=stg_c[b0:b0 + 64, 256:512]),
                                  reads=[stg_c], writes=[OUT["vcs"]])
                    else:
                        P.dma("sp", lambda e, i=i: e.dma_start(out=kcp[l, i * 128:(i + 1) * 128, :], in_=stg_c[:, 0:256]),
                              reads=[stg_c], writes=[OUT["kcp"]])
                        P.dma("sp", lambda e, i=i: e.dma_start(out=vcp[l, i * 128:(i + 1) * 128, :], in_=stg_c[:, 256:512]),
                              reads=[stg_c], writes=[OUT["vcp"]])
                if (is_sample or (last_group and i == NT - 1)) and not NOOUT:
                    if "B" in OUTM:
                        P.op("dve", lambda e, pA=pA: e.tensor_copy(out=stg_b[:, 0:128], in_=pA[:, 256:384]), reads=[pA], writes=[stg_b])
                        P.op("act", lambda e, pA=pA: e.copy(stg_b[:, 128:256], pA[:, 384:512]), reads=[pA], writes=[stg_b])
                    if "b" not in OUTM:
                        pass
                    elif is_sample:
                        for ci in range(2):
                            s = 2 * i + ci
                            b0 = 64 * ci
                            P.dma(os.environ.get("KERNEL_OUTQ", "sp"), lambda e, s=s, b0=b0: e.dma_start(out=kbs[l, s, 64:128, :], in_=stg_b[b0:b0 + 64, 0:128]),
                                  reads=[stg_b], writes=[OUT["kbs"]])
                            P.dma(os.environ.get("KERNEL_OUTQ", "sp"), lambda e, s=s, b0=b0: e.dma_start(out=vbs[l, s, 64:128, :], in_=stg_b[b0:b0 + 64, 128:256]),
                                  reads=[stg_b], writes=[OUT["vbs"]])
                    else:
                        P.dma("sp", lambda e: e.dma_start(out=kbp[l, :, :], in_=stg_b[:, 0:128]), reads=[stg_b], writes=[OUT["kbp"]])
                        P.dma("sp", lambda e: e.dma_start(out=vbp[l, :, :], in_=stg_b[:, 128:256]), reads=[stg_b], writes=[OUT["vbp"]])
                stage(4)
                P.op("pe", lambda e: e.matmul(pz[:], lhsT=raT[0:32, tcols], rhs=wa2[:, l, :], start=True, stop=True),
                     reads=[raT, wa2], writes=[pz])
                P.op("act", lambda e: e.activation(out=ez[:], in_=pz[:], func=AF.Exp, scale=-1.0), reads=[pz], writes=[ez])
                P.op("act", lambda e: e.activation(out=sp_t[:], in_=ez[:], func=AF.Ln, bias=1.0, scale=1.0), reads=[ez], writes=[sp_t])
                for j in range(2):
                    P.op("pe", lambda e, j=j: e.matmul(pbT[:, j * 128:(j + 1) * 128], lhsT=sp_t[:, j * 128:(j + 1) * 128], rhs=lincl[:],
                                                       start=True, stop=True), reads=[sp_t, lincl], writes=[pbT])
                P.op("pe", lambda e: e.matmul(pbrem[:], lhsT=lafter[:], rhs=sp_t[:], start=True, stop=True),
                     reads=[lafter, sp_t], writes=[pbrem])
                P.op("act", lambda e: e.activation(out=EbT[:], in_=pbT[:].rearrange("p (j t) -> p j t", t=128), func=AF.Exp),
                     reads=[pbT], writes=[EbT])
                P.op("act", lambda e: e.activation(out=EnbT[:], in_=pbT[:].rearrange("p (j t) -> p j t", t=128), func=AF.Exp, scale=-1.0),
                     reads=[pbT], writes=[EnbT])
                P.op("act", lambda e: e.activation(out=Ebrem[:], in_=pbrem[:], func=AF.Exp), reads=[pbrem], writes=[Ebrem])
                for r_ in range(2):
                    P.op("dve", lambda e, r_=r_: e.scalar_tensor_tensor(
                        out=qtz[r_ * 64:(r_ + 1) * 64, :, r_, :], in0=qaT[r_ * 64:(r_ + 1) * 64, :, tcols], scalar=0.125,
                        in1=EbT[r_ * 64:(r_ + 1) * 64, :, :], op0=ALU.mult, op1=ALU.mult), reads=[qaT, EbT], writes=[qtz])
                P.op("dve", lambda e: e.tensor_tensor(out=ktT[:], in0=kaT[:, :, tcols], in1=EnbT[:], op=ALU.mult),
                     reads=[kaT, EnbT], writes=[ktT])
                for c_ in range(2):
                    P.op("dve", lambda e, c_=c_: e.tensor_tensor(out=khat_z[c_][c_ * 64:(c_ + 1) * 64, :], in0=ka_tok[c_ * 64:(c_ + 1) * 64, :],
                                                                 in1=Ebrem[c_ * 64:(c_ + 1) * 64, :], op=ALU.mult),
                         reads=[ka_tok, Ebrem], writes=[khat_z[c_]])
                stage(5)
                for ci in range(2):
                    cols = slice(i * 128 + 64 * ci, i * 128 + 64 * ci + 64)
                    if is_sample:
                        s = 2 * i + ci
                        S, Sbf = S_s[s], Sbf_s[s]
                        P.dma("sp", lambda e, s=s: e.dma_start(out=S_s[s][:], in_=st_d[l, s, :, :, :]), writes=[S_s[s]])
                        P.op("pool", lambda e, s=s: e.tensor_copy(out=Sbf_s[s][:], in_=S_s[s][:]), reads=[S_s[s]], writes=[Sbf_s[s]])
                        P.dma("pool", lambda e, s=s: e.dma_start(out=s_kcT[s][:], in_=kcT_c[l, s, :, :, :]), writes=[s_kcT[s]])
                        P.dma("pool", lambda e, s=s: e.dma_start(out=s_vc[s][:], in_=vc_c[l, s, :, :, :]), writes=[s_vc[s]])
                        P.dma("pool", lambda e, s=s: e.dma_start(out=s_kbT[s][:], in_=kbT_c[l, s, :, :]), writes=[s_kbT[s]])
                        P.dma("pool", lambda e, s=s: e.dma_start(out=s_vb[s][:, 0, :], in_=vb_c[l, s, :, :]), writes=[s_vb[s]])
                    else:
                        S, Sbf = S_p[l], Sbf_p[l]
                    gen_gla = gla_chunk(l, i, ci, S, Sbf)
                    stage(6)
                    pb_own = 64 * ci
                    if is_sample:
                        pC_l = []
                        for k4 in range(4):
                            pC_l.append((s_kcT[s], (lambda j, k4=k4, s=s: s_kcT[s][:, j, k4 * 128:(k4 + 1) * 128]),
                                         s_vc[s], (lambda h, k4=k4, s=s: s_vc[s][:, k4, h * 64:(h + 1) * 64]), 0, 128, k4))
                        pC_l.append((own_kcT, (lambda j, i=i, pb_own=pb_own: own_kcT[:, j, i * 128 + pb_own:i * 128 + pb_own + 64]),
                                     own_vc, (lambda h, i=i: own_vc[:, i, h * 64:(h + 1) * 64]), pb_own, 64, 4))
                        pB_l = [(s_kbT[s], (lambda j, s=s: s_kbT[s][:, 0:128]),
                                 s_vb[s], (lambda g, s=s: s_vb[s][:, 0, g * 64:(g + 1) * 64]), 0, 128, 0),
                                (own_kbT, (lambda j, i=i, pb_own=pb_own: own_kbT[:, i * 128 + pb_own:i * 128 + pb_own + 64]),
                                 own_vb, (lambda g, i=i: own_vb[:, i, g * 64:(g + 1) * 64]), pb_own, 64, 1)]
                    else:
                        t = t0 + i
                        rk, rv, rkb, rvb = ring_kcT[l], ring_vc[l], ring_kbT[l], ring_vb[l]

                        def mkC(tk, pb, nk, var):
                            sl = tk % NR
                            return (rk, (lambda j, sl=sl, pb=pb, nk=nk, rk=rk: rk[:, j, sl * 128 + pb:sl * 128 + pb + nk]),
                                    rv, (lambda h, sl=sl, rv=rv: rv[:, sl, h * 64:(h + 1) * 64]), pb, nk, var)

                        def mkB(tk, pb, nk, var):
                            sl = tk % NR
                            return (rkb, (lambda j, sl=sl, pb=pb, nk=nk, rkb=rkb: rkb[:, sl * 128 + pb:sl * 128 + pb + nk]),
                                    rvb, (lambda g, sl=sl, rvb=rvb: rvb[:, sl, g * 64:(g + 1) * 64]), pb, nk, var)
                        pC_l, pB_l = [], []
                        if ci == 0:
                            for k4 in range(4):
                                tk = t - 4 + k4
                                if tk >= 0:
                                    pC_l.append(mkC(tk, 0, 128, k4))
                            pC_l.append(mkC(t, 0, 64, 4))
                            if t - 1 >= 0:
                                pB_l.append(mkB(t - 1, 0, 128, 0))
                            pB_l.append(mkB(t, 0, 64, 1))
                        else:
                            if t - 4 >= 0:
                                pC_l.append(mkC(t - 4, 64, 64, 5))
                            for k4 in range(4):
                                tk = t - 3 + k4
                                if tk >= 0:
                                    pC_l.append(mkC(tk, 0, 128, 6 + k4))
                            if t - 1 >= 0:
                                pB_l.append(mkB(t - 1, 64, 64, 2))
                            pB_l.append(mkB(t, 0, 128, 3))
                    run_interleaved([gen_gla, attn_c(l, cols, pC_l), attn_b(l, cols, pB_l)])
                    if is_sample:
                        P.dma("sp", lambda e, s=s: e.dma_start(out=sgs[l, s, :, :, :], in_=S_s[s][:]), reads=[S_s[s]], writes=[OUT["sgs"]])
                    elif last_group and i == NT - 1 and ci == 1:
                        P.dma("sp", lambda e: e.dma_start(out=sgp[l, :, :, :], in_=S_p[l][:]), reads=[S_p[l]], writes=[OUT["sgp"]])
            stage(7)
            wos = [wget("wo%d_%d" % (l, p)) for p in range(2)]
            pend = []
            for oc in range(8):
                wb = wos[oc // 4]
                q = oc % 4
                pf = nextbig()
                for kc in range(8):
                    P.op("pe", lambda e, wb=wb, q=q, kc=kc, pf=pf: e.matmul(pf[:, 0:TG], lhsT=wb[:, kc, q * 128:(q + 1) * 128], rhs=mixT[:, kc, 0:TG],
                                                                            start=(kc == 0), stop=(kc == 7)), reads=[wb, mixT], writes=[pf])
                for (c0, c1, s) in segs:
                    P.op("dve", lambda e, oc=oc, pf=pf, c0=c0, c1=c1, s=s: e.scalar_tensor_tensor(
                        out=xT[:, oc, c0:c1], in0=pf[:, c0:c1], scalar=mod_ap(l, 2, oc, s), in1=xT[:, oc, c0:c1],
                        op0=ALU.mult, op1=ALU.add), reads=[pf, mod, xT], writes=[xT])
                pend.append(stats_chunk(oc, TG))
                if len(pend) > 2:
                    pend.pop(0)()
            while pend:
                pend.pop(0)()
            stage(8)
            norm_apply(TG, segs,
                       lambda c, s: (Amlp[:, l, c, s:s + 1], Amlp),
                       lambda c, s: (mod_ap(l, 3, c, s), mod), hT)
            def up_block(fb, wu, u):
                for fc in range(4):
                    pf = nextbig()
                    for kc in range(8):
                        P.op("pe", lambda e, wu=wu, fc=fc, kc=kc, pf=pf: e.matmul(pf[:, 0:TG], lhsT=wu[:, kc, fc * 128:(fc + 1) * 128], rhs=hT[:, kc, 0:TG],
                                                                                  start=(kc == 0), stop=(kc == 7)), reads=[wu, hT], writes=[pf])
                    t = nexttmp()
                    P.op("act", lambda e, pf=pf, t=t: e.activation(out=t[:, 0:TG], in_=pf[:, 0:TG], func=AF.Relu), reads=[pf], writes=[t])
                    P.op("pool", lambda e, u=u, fc=fc, t=t: e.tensor_tensor(out=u[:, fc, 0:TG], in0=t[:, 0:TG], in1=t[:, 0:TG], op=ALU.mult),
                         reads=[t], writes=[u])

            def down_block(fb, wd, u):
                for oc in range(8):
                    hf, q = oc // 4, oc % 4
                    pf = nextbig()
                    for fc in range(4):
                        P.op("pe", lambda e, wd=wd, fc=fc, hf=hf, q=q, pf=pf, u=u: e.matmul(
                            pf[:, 0:TG], lhsT=wd[:, fc * 2 + hf, q * 128:(q + 1) * 128], rhs=u[:, fc, 0:TG],
                            start=(fc == 0), stop=(fc == 3)), reads=[wd, u], writes=[pf])
                    for (c0, c1, s) in segs:
                        P.op("dve", lambda e, oc=oc, pf=pf, c0=c0, c1=c1, s=s: e.scalar_tensor_tensor(
                            out=xT[:, oc, c0:c1], in0=pf[:, c0:c1], scalar=mod_ap(l, 5, oc, s), in1=xT[:, oc, c0:c1],
                            op0=ALU.mult, op1=ALU.add), reads=[pf, mod, xT], writes=[xT])
                    if fb == 7:
                        pend.append(stats_chunk(oc, TG))
                        if len(pend) > 2:
                            pend.pop(0)()
                while fb == 7 and pend:
                    pend.pop(0)()

            wu_cur = wget("wu%d_%d" % (l, 0))
            up_block(0, wu_cur, uT[0])
            for fb in range(8):
                wd_cur = wget("wd%d_%d" % (l, fb))
                if fb + 1 < 8:
                    wu_nxt = wget("wu%d_%d" % (l, fb + 1))
                    up_block(fb + 1, wu_nxt, uT[(fb + 1) % 2])
                down_block(fb, wd_cur, uT[fb % 2])
        rstd_from_stats(TG)
        for c in range(8):
            yc = yst[c % 2]
            P.op("dve", lambda e, c=c, yc=yc: e.scalar_tensor_tensor(out=yc[:, 0:TG], in0=xT[:, c, 0:TG], scalar=gfin[:, c:c + 1],
                                                                     in1=rstd[:, 0:TG], op0=ALU.mult, op1=ALU.mult),
                 reads=[xT, gfin, rstd], writes=[yc])
            if is_sample:
                P.dma("act", lambda e, c=c, yc=yc: e.dma_start(out=yTs[:, c, :], in_=yc[:, 0:TG]), reads=[yc], writes=[OUT["yTs"]])
            else:
                P.dma("act", lambda e, c=c, yc=yc: e.dma_start(out=yTp[:, c, gi * 512:(gi + 1) * 512], in_=yc[:, 0:TG]),
                      reads=[yc], writes=[OUT["yTp"]])

    try:
        stage(1)
        process_group(0, True)
        for gi in range(n_pgroups):
            process_group(gi, False)
        assert wstate["used"] == len(specs), (wstate, len(specs))
    except StopBuild:
        pass
    P.build()
    return nc, P.stats


def _feat_major(x2d):
    T_ = x2d.shape[0]
    return np.ascontiguousarray(x2d.reshape(T_, 8, 128).transpose(2, 1, 0))


def _from_feat_major(yT):
    T_ = yT.shape[2]
    return np.ascontiguousarray(yT.transpose(2, 1, 0).reshape(T_, 1024))


def _vec_fm(v):
    lead = v.shape[:-1]
    a = v.reshape(*lead, 8, 128)
    a = np.moveaxis(a, -1, 0)
    return np.ascontiguousarray(a)


_N_PGROUPS = int(os.environ.get("KERNEL_NPG", "32"))


def kernel(x_prompt, x_sample, c_prompt, c_sample, state_gla, cache_b_k, cache_b_v, cache_c_k, cache_c_v,
           w_ada, b_ada, norm_mix_g, norm_mlp_g, w_in, w_a2, b_a2, a_norm_g, b_sink, t5_bias, c_rel_bias,
           w_out, w_up, w_down, final_norm_g):
    f = lambda a: np.asarray(a, dtype=np.float32)
    x_prompt, x_sample, c_prompt, c_sample = f(x_prompt), f(x_sample), f(c_prompt), f(c_sample)
    state_gla, cache_b_k, cache_b_v, cache_c_k, cache_c_v = f(state_gla), f(cache_b_k), f(cache_b_v), f(cache_c_k), f(cache_c_v)
    w_ada, b_ada, norm_mix_g, norm_mlp_g, w_in, w_a2, b_a2 = f(w_ada), f(b_ada), f(norm_mix_g), f(norm_mlp_g), f(w_in), f(w_a2), f(b_a2)
    a_norm_g, b_sink, t5_bias, c_rel_bias, w_out, w_up, w_down, final_norm_g = (
        f(a_norm_g), f(b_sink), f(t5_bias), f(c_rel_bias), f(w_out), f(w_up), f(w_down), f(final_norm_g))
    npg = _N_PGROUPS
    nc, stats = build_program(npg)

    o = np.cumsum([0, 256, 256, 512, 512, 16, 256, 128, 128, 256, 256, 256])
    qa, ka, va, ga, ra, qb, kb, vb, qc, kc, vc = [np.arange(o[i], o[i + 1]) for i in range(11)]
    qb_re = np.concatenate([qb[0:64], qb[128:192], qb[64:128], qb[192:256]])
    feat_cols = np.concatenate([qa, ka, ga, qb_re, kb, qc, kc, ra])
    Wf = np.zeros((2, 1024, 2048), np.float32)
    Wf[:, :, :feat_cols.size] = w_in[:, :, feat_cols]
    tok_cols = np.concatenate([ka, kb, vb, va, kc, vc])
    Wt = np.ascontiguousarray(w_in[:, :, tok_cols])
    wa2 = np.zeros((32, 2, 256), np.float32)
    wa2[0:16] = w_a2.transpose(1, 0, 2)
    wa2[16] = b_a2
    badaR = np.ascontiguousarray(np.broadcast_to(
        b_ada.reshape(2, 48, 128).transpose(2, 0, 1)[:, :, :, None], (128, 2, 48, 5)))
    gmix = _vec_fm(norm_mix_g)
    gmlp = _vec_fm(norm_mlp_g)
    gfin = _vec_fm(final_norm_g)
    anorm = np.ascontiguousarray(a_norm_g.T)
    sinkT = np.zeros((128, 2, 2), np.float32)
    for l in range(2):
        for j in range(2):
            sinkT[0:64, l, j] = b_sink[l, 2 * j]
            sinkT[64:128, l, j] = b_sink[l, 2 * j + 1]
    biasC, biasB = build_bias_tables(t5_bias, c_rel_bias)
    sidx = np.arange(128)
    same = (sidx[:, None] // 64) == (sidx[None, :] // 64)
    lincl = np.where(same & (sidx[:, None] <= sidx[None, :]), -1.0 / 16.0, 0.0).astype(np.float32)
    lafter = np.where(same & (sidx[:, None] > sidx[None, :]), -1.0 / 16.0, 0.0).astype(np.float32)
    m = ((sidx[:, None] % 64) <= np.arange(64)[None, :]).astype(np.float32)
    mask01 = np.ascontiguousarray(np.broadcast_to(m[:, None, :], (128, 4, 64)).reshape(128, 256))

    xTp = _feat_major(x_prompt[0, :npg * 512])
    cache_k_b2 = cache_b_k.reshape(2, 32, 128, 128)
    cache_v_b2 = cache_b_v.reshape(2, 32, 128, 128)
    cache_k_c2 = cache_c_k.reshape(2, 32, 512, 256)
    cache_v_c2 = cache_c_v.reshape(2, 32, 512, 256)
    shared = dict(xTp=xTp, Wf=Wf, Wt=Wt, Wo=np.ascontiguousarray(w_out), Wu=np.ascontiguousarray(w_up), Wd=np.ascontiguousarray(w_down),
                  Wa=np.ascontiguousarray(w_ada), badaR=badaR, gmix=gmix, gmlp=gmlp, gfin=gfin, wa2=wa2, anorm=anorm,
                  sinkT=sinkT, biasB=biasB, biasC=biasC, lincl=lincl, lafter=lafter, mask01=mask01,
                  ident=np.eye(128, dtype=np.float32))
    in_maps = []
    for c in range(NCORES):
        bs = slice(c * 4, (c + 1) * 4)
        xs = x_sample[bs].reshape(256, 1024)
        call = np.concatenate([c_prompt, c_sample[bs]], axis=0)
        cT = np.ascontiguousarray(call.reshape(5, 8, 128).transpose(2, 1, 0))
        st = state_gla[:, bs].reshape(2, 4, 2, 2, 64, 128).transpose(0, 1, 3, 4, 2, 5).reshape(2, 4, 128, 2, 128)
        kbT_c = cache_k_b2[:, bs].transpose(0, 1, 3, 2)
        kcT_c = cache_c_k[:, bs].reshape(2, 4, 512, 2, 2, 64).transpose(0, 1, 4, 5, 3, 2).reshape(2, 4, 128, 2, 512)
        vc_c = cache_v_c2[:, bs].reshape(2, 4, 4, 128, 256).transpose(0, 1, 3, 2, 4)
        d = dict(shared)
        d.update(xTs=_feat_major(xs), cT=cT, st=np.ascontiguousarray(st), kbT_c=np.ascontiguousarray(kbT_c),
                 vb_c=np.ascontiguousarray(cache_v_b2[:, bs]), kcT_c=np.ascontiguousarray(kcT_c), vc_c=np.ascontiguousarray(vc_c),
                 cbk=np.ascontiguousarray(cache_k_b2[:, bs]), cbv=np.ascontiguousarray(cache_v_b2[:, bs]),
                 cck=np.ascontiguousarray(cache_k_c2[:, bs]), ccv=np.ascontiguousarray(cache_v_c2[:, bs]))
        in_maps.append(d)
    res = run_bass_kernel_spmd(nc, in_maps, core_ids=list(range(NCORES)))
    R = res.results

    def unS(a):
        lead = a.shape[:-3]
        a = a.reshape(*lead, 2, 64, 2, 128)
        a = np.moveaxis(a, -2, -4)
        return np.ascontiguousarray(a.reshape(*lead, 4, 64, 128))

    y_prompt = _from_feat_major(R[0]["yTp"])[None]
    y_sample = np.concatenate([_from_feat_major(R[c]["yTs"]).reshape(4, 64, 1024) for c in range(NCORES)], axis=0)
    sg_p = unS(R[0]["sgp"])[:, None]
    kb_p = R[0]["kbp"].reshape(2, 1, 128, 2, 64)
    vb_p = R[0]["vbp"].reshape(2, 1, 128, 2, 64)
    kc_p = R[0]["kcp"].reshape(2, 1, 512, 4, 64)
    vc_p = R[0]["vcp"].reshape(2, 1, 512, 4, 64)
    sg_s = np.concatenate([unS(R[c]["sgs"]) for c in range(NCORES)], axis=1)
    kb_s = np.concatenate([R[c]["kbs"].reshape(2, 4, 128, 2, 64) for c in range(NCORES)], axis=1)
    vb_s = np.concatenate([R[c]["vbs"].reshape(2, 4, 128, 2, 64) for c in range(NCORES)], axis=1)
    kc_s = np.concatenate([R[c]["kcs"].reshape(2, 4, 512, 4, 64) for c in range(NCORES)], axis=1)
    vc_s = np.concatenate([R[c]["vcs"].reshape(2, 4, 512, 4, 64) for c in range(NCORES)], axis=1)
    outs = (y_prompt, y_sample, sg_p, kb_p, vb_p, kc_p, vc_p, sg_s, kb_s, vb_s, kc_s, vc_s)
    return tuple(np.ascontiguousarray(o_, dtype=np.float32) for o_ in outs)
```

```python
import contextlib
import os
import types
import numpy as np
import concourse.bass as bass
import concourse.mybir as mybir
from concourse.bass_utils import run_bass_kernel_spmd

F32 = mybir.dt.float32
BF16 = mybir.dt.bfloat16
AF = mybir.ActivationFunctionType
ALU = mybir.AluOpType
ENGS = ("pe", "act", "dve", "pool", "sp")

NCORES = 8
D = 1024
SEQ = 16384
NSEQ_CORE = 4
EPS = 1e-6


class T:
    __slots__ = ("name", "ap", "last_w", "readers", "dsem", "dval", "excl", "last_acc", "t")

    def __init__(self, name, ap, excl=False):
        self.name = name
        self.ap = ap
        self.last_w = None
        self.readers = []
        self.dsem = {}
        self.dval = {}
        self.excl = excl
        self.last_acc = {}
        self.t = self

    def __getitem__(self, k):
        return self.ap[k]


class V:
    __slots__ = ("t", "ap")

    def __init__(self, t, ap):
        self.t = t
        self.ap = ap

    def __getitem__(self, k):
        return self.ap[k]


class Ins:
    __slots__ = ("eng", "fn", "reads", "writes", "is_dma", "deps", "signal", "tok", "dtile", "ndma")

    def __init__(self, eng, fn, reads, writes, is_dma=False, dtile=None, ndma=1):
        self.eng = eng
        self.fn = fn
        self.reads = reads
        self.writes = writes
        self.is_dma = is_dma
        self.deps = []
        self.signal = False
        self.tok = None
        self.dtile = dtile
        self.ndma = ndma


def _freeze(fn):
    if fn.__closure__ is None:
        return fn
    cells = []
    for c in fn.__closure__:
        try:
            cells.append(types.CellType(c.cell_contents))
        except ValueError:
            cells.append(c)
    g = types.FunctionType(fn.__code__, fn.__globals__, fn.__name__, fn.__defaults__, tuple(cells))
    g.__kwdefaults__ = fn.__kwdefaults__
    return g


class Prog:
    def __init__(self, nc, same_engine_sync=True):
        self.nc = nc
        self.ins = []
        self.stack = contextlib.ExitStack()
        self.same_engine_sync = same_engine_sync
        self.tiles = []

    def tile(self, name, ap):
        t = T(name, ap)
        self.tiles.append(t)
        return t

    def new(self, name, shape, dtype, psum=False):
        if psum:
            h = self.stack.enter_context(self.nc.psum_tensor(name, list(shape), dtype))
        else:
            h = self.stack.enter_context(self.nc.sbuf_tensor(name, list(shape), dtype))
        return self.tile(name, h)

    def op(self, eng, fn, reads=(), writes=()):
        self.ins.append(Ins(eng, _freeze(fn), [x.t for x in reads], [x.t for x in writes]))

    def dma(self, eng, fns, reads=(), writes=(), dtile=None):
        if not isinstance(fns, (list, tuple)):
            fns = [fns]
        reads = [x.t for x in reads]
        writes = [x.t for x in writes]
        if dtile is None:
            dtile = (writes + reads)[0]
        self.ins.append(Ins(eng, [_freeze(f) for f in fns], reads, writes, True, dtile, len(fns)))

    def build(self):
        nc = self.nc
        ins = self.ins
        for idx, i in enumerate(ins):
            deps = set()
            for t in i.reads:
                if t.last_w is not None:
                    deps.add(t.last_w)
            for t in i.writes:
                if t.last_w is not None:
                    deps.add(t.last_w)
                deps.update(t.readers)
            for t in set(i.reads + i.writes):
                if t.excl:
                    for f_eng, f_idx in t.last_acc.items():
                        if f_eng != i.eng:
                            deps.add(f_idx)
                    t.last_acc[i.eng] = idx
            deps.discard(idx)
            for t in i.reads:
                if not i.is_dma:
                    t.readers = [r for r in t.readers if ins[r].is_dma or ins[r].eng != i.eng]
                if not t.readers or t.readers[-1] != idx:
                    t.readers.append(idx)
            for t in i.writes:
                t.last_w = idx
                t.readers = []
            keep = []
            for d in deps:
                p = ins[d]
                if (not p.is_dma) and (not i.is_dma) and p.eng == i.eng:
                    if p.eng == "pe" or not self.same_engine_sync:
                        continue
                keep.append(d)
            i.deps = keep
            for d in keep:
                ins[d].signal = True
        esem = {e: self.stack.enter_context(nc.semaphore("s_" + e)) for e in ENGS}
        ecnt = {e: 0 for e in ENGS}
        for i in ins:
            if i.is_dma:
                t = i.dtile
                kq = "sw" if i.eng == "pool" else "hw"
                if kq not in t.dsem:
                    t.dsem[kq] = self.stack.enter_context(nc.semaphore("d%s_%s" % (kq, t.name)))
                    t.dval[kq] = 0
                t.dval[kq] += 16 * i.ndma
                i.tok = (t.dsem[kq], t.dval[kq])
            elif i.signal:
                ecnt[i.eng] += 1
                i.tok = (esem[i.eng], ecnt[i.eng])
        progs = {e: [] for e in ENGS}
        waited = {e: {} for e in ENGS}
        nw = 0
        for i in ins:
            w = {}
            for d in i.deps:
                s, v = ins[d].tok
                k = id(s)
                if k not in w or w[k][1] < v:
                    w[k] = (s, v)
            for k, (s, v) in w.items():
                if waited[i.eng].get(k, 0) >= v:
                    continue
                waited[i.eng][k] = v
                progs[i.eng].append(("w", s, v))
                nw += 1
            progs[i.eng].append(("i", i))
        for t in self.tiles:
            for kq in t.dsem:
                progs["sp"].append(("w", t.dsem[kq], t.dval[kq]))
        self.stats = dict(n_ins=len(ins), n_waits=nw, ecnt=dict(ecnt))

        def run(name, e):
            for item in progs[name]:
                if item[0] == "w":
                    e.wait_ge(item[1], item[2])
                else:
                    i = item[1]
                    if i.is_dma:
                        for f in i.fn:
                            f(e).then_inc(i.tok[0], 16)
                    else:
                        r = i.fn(e)
                        if i.signal:
                            r.then_inc(i.tok[0], 1)

        with nc.Block() as block:
            @block.tensor
            def _(e):
                run("pe", e)

            @block.scalar
            def _(e):
                run("act", e)

            @block.vector
            def _(e):
                run("dve", e)

            @block.gpsimd
            def _(e):
                run("pool", e)

            @block.sync
            def _(e):
                run("sp", e)
        self.stack.close()


def _t5_bucket_np(rel):
    import jax
    with jax.default_device(jax.devices("cpu")[0]):
        return _t5_bucket_impl(rel)


def _t5_bucket_impl(rel):
    import jax.numpy as jnp
    import math
    rel = jnp.asarray(rel)
    half = 16
    max_exact = 8
    n = jnp.abs(rel)
    log_ratio = jnp.log(jnp.maximum(n, 1).astype(jnp.float32) / max_exact) / math.log(128 / max_exact)
    large = jnp.minimum(max_exact + (log_ratio * (half - max_exact)).astype(jnp.int32), half - 1)
    return np.asarray(jnp.where(rel > 0, half, 0) + jnp.where(n < max_exact, n, large))


C_EVEN = [(-512, 128), (-384, 128), (-256, 128), (-128, 128), (0, 64)]
C_ODD = [(-512, 64), (-448, 128), (-320, 128), (-192, 128), (-64, 128)]
B_EVEN = [(-128, 128), (0, 64)]
B_ODD = [(-128, 64), (-64, 128)]


def _rel_for(a, nk):
    j = np.arange(128)
    if nk == 64:
        j = j % 64
    q = np.arange(64)
    return a + j[:, None] - q[None, :]


def build_bias_tables(t5_bias, c_rel_bias):
    biasC = np.zeros((128, 2, 10, 4, 64), np.float32)
    for v, (a, nk) in enumerate(C_EVEN + C_ODD):
        idx = np.clip(_rel_for(a, nk), -256, 256) + 256
        for l in range(2):
            biasC[:, l, v] = np.transpose(c_rel_bias[l][idx], (0, 2, 1))
    biasB = np.zeros((128, 4, 2, 2, 64), np.float32)
    for v, (a, nk) in enumerate(B_EVEN + B_ODD):
        bk = _t5_bucket_np(_rel_for(a, nk))
        tb = t5_bias[bk]
        for g in range(2):
            for r in range(2):
                biasB[:, v, g, r] = tb[:, :, 2 * g + r]
    keep = [0, 2, 3, 4, 7, 8, 9]
    for v in (1, 5, 6):
        assert np.array_equal(biasC[:, :, v], biasC[:, :, 0])
    return np.ascontiguousarray(biasC[:, :, keep].reshape(128, 2, 7, 256)), biasB.reshape(128, 4, 256)


def build_program(n_pgroups):
    nc = bass.Bass("TRN2", target_bir_lowering=False)
    NTOK_P = n_pgroups * 512

    def din(name, shape, dt=F32):
        return nc.dram_tensor(name, list(shape), dt, kind="ExternalInput").ap()

    def dout(name, shape, dt=F32):
        return nc.dram_tensor(name, list(shape), dt, kind="ExternalOutput").ap()

    xTp = din("xTp", [128, 8, NTOK_P])
    xTs = din("xTs", [128, 8, 256])
    cT_d = din("cT", [128, 8, 5])
    st_d = din("st", [2, 4, 128, 2, 128])
    kbT_c = din("kbT_c", [2, 4, 128, 128])
    vb_c = din("vb_c", [2, 4, 128, 128])
    kcT_c = din("kcT_c", [2, 4, 128, 2, 512])
    vc_c = din("vc_c", [2, 4, 128, 4, 256])
    cbk = din("cbk", [2, 4, 128, 128])
    cbv = din("cbv", [2, 4, 128, 128])
    cck = din("cck", [2, 4, 512, 256])
    ccv = din("ccv", [2, 4, 512, 256])
    Wf = din("Wf", [2, 1024, 2048])
    Wt = din("Wt", [2, 1024, 1536])
    Wo = din("Wo", [2, 1024, 1024])
    Wu = din("Wu", [2, 1024, 4096])
    Wd = din("Wd", [2, 4096, 1024])
    Wa = din("Wa", [2, 1024, 6144])
    badaR = din("badaR", [128, 2, 48, 5])
    gmix_d = din("gmix", [128, 2, 8])
    gmlp_d = din("gmlp", [128, 2, 8])
    gfin_d = din("gfin", [128, 8])
    wa2_d = din("wa2", [32, 2, 256])
    anorm_d = din("anorm", [128, 2])
    sink_d = din("sinkT", [128, 2, 2])
    biasB_d = din("biasB", [128, 4, 256])
    biasC_d = din("biasC", [128, 2, 7, 256])
    lincl_d = din("lincl", [128, 128])
    lafter_d = din("lafter", [128, 128])
    mask_d = din("mask01", [128, 256])
    ident_d = din("ident", [128, 128])

    yTp = dout("yTp", [128, 8, NTOK_P])
    yTs = dout("yTs", [128, 8, 256])
    sgp = dout("sgp", [2, 128, 2, 128])
    kbp = dout("kbp", [2, 128, 128])
    vbp = dout("vbp", [2, 128, 128])
    kcp = dout("kcp", [2, 512, 256])
    vcp = dout("vcp", [2, 512, 256])
    sgs = dout("sgs", [2, 4, 128, 2, 128])
    kbs = dout("kbs", [2, 4, 128, 128])
    vbs = dout("vbs", [2, 4, 128, 128])
    kcs = dout("kcs", [2, 4, 512, 256])
    vcs = dout("vcs", [2, 4, 512, 256])

    P = Prog(nc)
    OUT = {n: P.tile(n, a) for n, a in [("yTp", yTp), ("yTs", yTs), ("sgp", sgp), ("kbp", kbp), ("vbp", vbp),
                                        ("kcp", kcp), ("vcp", vcp), ("sgs", sgs), ("kbs", kbs), ("vbs", vbs),
                                        ("kcs", kcs), ("vcs", vcs)]}

    def load_const(name, src, shape, dt=F32, eng="sp"):
        t = P.new(name, shape, dt)
        P.dma(eng, lambda e: e.dma_start(out=t[:], in_=src), writes=[t])
        return t

    cT = load_const("cT_sb", cT_d[:, :, :], [128, 8, 5])
    bada = load_const("bada_sb", badaR[:, :, :, :], [128, 2, 48, 5])
    gmix = load_const("gmix_sb", gmix_d[:, :, :], [128, 2, 8])
    gmlp = load_const("gmlp_sb", gmlp_d[:, :, :], [128, 2, 8])
    gfin = load_const("gfin_sb", gfin_d[:, :], [128, 8])
    wa2 = load_const("wa2_sb", wa2_d[:, :, :], [32, 2, 256])
    anorm = load_const("anorm_sb", anorm_d[:, :], [128, 2])
    sinkr = load_const("sink_sb", sink_d[:, :, :], [128, 2, 2])
    biasB = load_const("biasB_sb", biasB_d[:, :, :], [128, 4, 256], BF16, eng="pool")
    biasC = load_const("biasC_sb", biasC_d[:, :, :, :], [128, 2, 7, 256], BF16, eng="pool")
    ident = load_const("ident_sb", ident_d[:, :], [128, 128], BF16, eng="pool")
    CVMAP = {0: 0, 1: 0, 5: 0, 6: 0, 2: 1, 3: 2, 4: 3, 7: 4, 8: 5, 9: 6}
    lincl = load_const("lincl_sb", lincl_d[:, :], [128, 128])
    lafter = load_const("lafter_sb", lafter_d[:, :], [128, 128])
    mask01 = load_const("mask_sb", mask_d[:, :], [128, 256])

    stg_b = P.new("stg_b", [128, 256], F32)
    stg_c = P.new("stg_c", [128, 512], F32)
    ones_n = P.new("ones_n", [128, 128], BF16)
    ones_dv = P.new("ones_dv", [128, 128], BF16)
    ones_1 = P.new("ones_1", [128, 64], BF16)
    P.op("pool", lambda e: e.memset(ones_n[:], 1.0 / 1024.0), writes=[ones_n])
    P.op("pool", lambda e: e.memset(ones_dv[:], 1.0 / 128.0), writes=[ones_dv])
    P.op("pool", lambda e: e.memset(ones_1[:], 1.0), writes=[ones_1])
    sinke = P.new("sinke", [128, 2, 2], F32)
    P.op("act", lambda e: e.activation(out=sinke[:], in_=sinkr[:], func=AF.Exp), reads=[sinkr], writes=[sinke])

    banks = [P.stack.enter_context(nc.psum_tensor("bank%d" % i, [128, 512], F32)) for i in range(8)]
    bankT = [P.tile("bank%d" % i, banks[i]) for i in range(8)]
    for b in bankT:
        b.excl = True
    big = [bankT[0], bankT[1], bankT[2]]
    bigc = [0]

    def nextbig():
        b = big[bigc[0] % 3]
        bigc[0] += 1
        return b

    pz = V(bankT[4], banks[4][:, 0:256])
    pbrem = V(bankT[4], banks[4][:, 256:512])
    pbT = V(bankT[5], banks[5][:, 0:256])
    pat = V(bankT[5], banks[5][:, 256:512])
    po = V(bankT[6], banks[6][:, 0:256])
    pn = V(bankT[6], banks[6][:, 256:512])
    ss_bank = V(bankT[4], banks[4][:, 0:512])
    psc_c = V(bankT[3], banks[3][:, 0:256])
    pnd_c = V(bankT[3], banks[3][:, 256:512])
    psc_b = V(bankT[7], banks[7][:, 0:256])
    pnd_b = V(bankT[7], banks[7][:, 256:512])

    NB = int(os.environ.get('KERNEL_NB', '5'))
    wbuf = [P.new("wbuf%d" % i, [128, 8, 512], BF16) for i in range(NB)]
    specs = []

    def wspec_layer(l):
        s = []
        for p in range(4):
            s.append(("wf%d_%d" % (l, p), Wf[l].rearrange("(kc p) n -> p kc n", p=128)[:, :, p * 512:(p + 1) * 512]))
        for p in range(3):
            s.append(("wt%d_%d" % (l, p), Wt[l].rearrange("(kc p) n -> p kc n", p=128)[:, :, p * 512:(p + 1) * 512]))
        for p in range(2):
            s.append(("wo%d_%d" % (l, p), Wo[l].rearrange("(kc p) n -> p kc n", p=128)[:, :, p * 512:(p + 1) * 512]))
        for fb in range(8):
            s.append(("wu%d_%d" % (l, fb), Wu[l].rearrange("(kc p) n -> p kc n", p=128)[:, :, fb * 512:(fb + 1) * 512]))
            s.append(("wd%d_%d" % (l, fb),
                      Wd[l][fb * 512:(fb + 1) * 512, :].rearrange("(fc p) (hf n) -> p fc hf n", p=128, hf=2)))
        return s

    for l in range(2):
        for p in range(12):
            specs.append(("wa%d_%d" % (l, p), Wa[l].rearrange("(kc p) n -> p kc n", p=128)[:, :, p * 512:(p + 1) * 512]))
    n_groups = n_pgroups + 1
    for g in range(n_groups):
        for l in range(2):
            specs.extend(wspec_layer(l))
    Wbf = nc.dram_tensor("Wbf", [50, 128, 4096], BF16, kind="Internal").ap()
    scratch = {}
    sidx = 0
    for l in range(2):
        groups = {}
        order = []
        for name, src in wspec_layer(l):
            kind = name[:2] + str(l)
            if kind not in groups:
                groups[kind] = []
                order.append(kind)
            groups[kind].append((sidx, src))
            scratch[name] = (sidx, kind)
            sidx += 1
        for kind in order:
            kt = P.tile("wbf_" + kind, Wbf)
            fns = []
            for (ix, src) in groups[kind]:
                if len(src.shape) == 4:
                    fns.append(lambda e, ix=ix, src=src: e.dma_start(
                        out=Wbf[ix].rearrange("p (fc hf n) -> p fc hf n", fc=4, hf=2), in_=src))
                else:
                    fns.append(lambda e, ix=ix, src=src: e.dma_start(out=Wbf[ix].rearrange("p (a b) -> p a b", a=8), in_=src))
            for name in [n_ for n_, v in scratch.items() if v[1] == kind]:
                scratch[name] = (scratch[name][0], kt)
            groups[kind] = (kt, fns)
        scratch["__order%d" % l] = [groups[k] for k in order]
    wstate = dict(issued=0, used=0, precast=False)
    PREF = NB - 3

    def w_issue_upto(n):
        while wstate["issued"] < min(n, len(specs)):
            k = wstate["issued"]
            buf = wbuf[k % NB]
            src = specs[k][1]
            if not specs[k][0].startswith("wa"):
                if not wstate["precast"]:
                    wstate["precast"] = True
                    for l_ in range(2):
                        for (kt, fns) in scratch["__order%d" % l_]:
                            P.dma("pool", fns, writes=[kt])
                ix, kt = scratch[specs[k][0]]
                P.dma("sp", lambda e, buf=buf, ix=ix: e.dma_start(out=buf[:], in_=Wbf[ix].rearrange("p (a b) -> p a b", a=8)),
                      reads=[kt], writes=[buf])
                wstate["issued"] += 1
                continue
            if len(src.shape) == 4:
                P.dma("pool", lambda e, buf=buf, src=src: e.dma_start(
                    out=buf[:].rearrange("p (fc hf) n -> p fc hf n", hf=2), in_=src), writes=[buf])
            else:
                P.dma("pool", lambda e, buf=buf, src=src: e.dma_start(out=buf[:], in_=src), writes=[buf])
            wstate["issued"] += 1

    def wget(prefix):
        k = wstate["used"]
        assert specs[k][0].startswith(prefix), (specs[k][0], prefix)
        w_issue_upto(k + 1 + PREF)
        wstate["used"] += 1
        return wbuf[k % NB]

    ce = P.new("ce", [128, 8, 5], F32)
    csil = P.new("csil", [128, 8, 5], BF16)
    P.op("act", lambda e: e.activation(out=ce[:], in_=cT[:], func=AF.Exp, scale=-1.0), reads=[cT], writes=[ce])
    P.op("dve", lambda e: e.tensor_scalar(ce[:], ce[:], 1.0, None, op0=ALU.add), reads=[ce], writes=[ce])
    P.op("dve", lambda e: e.reciprocal(ce[:], ce[:]), reads=[ce], writes=[ce])
    P.op("dve", lambda e: e.tensor_tensor(out=csil[:], in0=cT[:], in1=ce[:], op=ALU.mult), reads=[cT, ce], writes=[csil])
    mod = P.new("mod", [128, 2, 48, 5], F32)
    for l in range(2):
        pm = nextbig()
        for p in range(12):
            wb = wget("wa%d_%d" % (l, p))
            for q in range(4):
                oc = p * 4 + q
                for kc in range(8):
                    P.op("pe", lambda e, wb=wb, q=q, kc=kc, oc=oc, pm=pm: e.matmul(
                        pm[:, oc * 5:(oc + 1) * 5], lhsT=wb[:, kc, q * 128:(q + 1) * 128], rhs=csil[:, kc, :],
                        start=(kc == 0), stop=(kc == 7)), reads=[wb, csil], writes=[pm])
        P.op("dve", lambda e, l=l, pm=pm: e.tensor_tensor(
            out=mod[:, l, :, :], in0=pm[:, 0:240].rearrange("p (a b) -> p a b", b=5), in1=bada[:, l, :, :], op=ALU.add),
            reads=[pm, bada], writes=[mod])
    Amix = P.new("Amix", [128, 2, 8, 5], F32)
    Amlp = P.new("Amlp", [128, 2, 8, 5], F32)
    for l in range(2):
        for c in range(8):
            P.op("dve", lambda e, l=l, c=c: e.tensor_scalar(Amix[:, l, c, :], mod[:, l, 8 + c, :], 1.0, gmix[:, l, c:c + 1],
                                                            op0=ALU.add, op1=ALU.mult), reads=[mod, gmix], writes=[Amix])
            P.op("dve", lambda e, l=l, c=c: e.tensor_scalar(Amlp[:, l, c, :], mod[:, l, 32 + c, :], 1.0, gmlp[:, l, c:c + 1],
                                                            op0=ALU.add, op1=ALU.mult), reads=[mod, gmlp], writes=[Amlp])

    STAGE = int(os.environ.get("KERNEL_STAGE", "99"))

    class StopBuild(Exception):
        pass

    def stage(k):
        if STAGE < k:
            raise StopBuild()

    def mod_ap(l, kind, c, s):
        return mod[:, l, kind * 8 + c, s:s + 1]

    NR = 8
    ring_kcT = [P.new("rkcT%d" % l, [128, 2, NR * 128], BF16) for l in range(2)]
    ring_vc = [P.new("rvc%d" % l, [128, NR, 256], BF16) for l in range(2)]
    ring_kbT = [P.new("rkbT%d" % l, [128, NR * 128], BF16) for l in range(2)]
    ring_vb = [P.new("rvb%d" % l, [128, NR, 128], BF16) for l in range(2)]
    S_p = [P.new("S_p%d" % l, [128, 2, 128], F32) for l in range(2)]
    Sbf_p = [P.new("Sbf_p%d" % l, [128, 2, 128], BF16) for l in range(2)]
    for l in range(2):
        P.op("pool", lambda e, l=l: e.memset(S_p[l][:], 0.0), writes=[S_p[l]])
        P.op("pool", lambda e, l=l: e.memset(Sbf_p[l][:], 0.0), writes=[Sbf_p[l]])
    S_s = [P.new("S_s0", [128, 2, 128], F32)] * 4
    Sbf_s = [P.new("Sbf_s0", [128, 2, 128], BF16)] * 4
    s_kcT = [P.new("s_kcT0", [128, 2, 512], BF16)] * 4
    s_vc = [P.new("s_vc0", [128, 4, 256], BF16)] * 4
    s_kbT = [P.new("s_kbT0", [128, 128], BF16)] * 4
    s_vb = [P.new("s_vb0", [128, 1, 128], BF16)] * 4
    own_kcT = P.new("own_kcT", [128, 2, 256], BF16)
    own_vc = P.new("own_vc", [128, 2, 256], BF16)
    own_kbT = P.new("own_kbT", [128, 256], BF16)
    own_vb = P.new("own_vb", [128, 2, 128], BF16)

    xT = P.new("xT", [128, 8, 512], F32)
    hT = P.new("hT", [128, 8, 512], BF16)
    sq = [P.new("sq%d" % i, [128, 512], BF16) for i in range(4)]
    rstd = P.new("rstd", [128, 512], F32)
    tmpf = [P.new("tmpf%d" % i, [128, 512], F32) for i in range(2)]
    tmpc = [0]

    def nexttmp():
        t = tmpf[tmpc[0] % 2]
        tmpc[0] += 1
        return t

    qaT = P.new("qaT", [128, 2, 512], F32)
    kaT = P.new("kaT", [128, 2, 512], F32)
    gate = P.new("gate", [128, 4, 512], BF16)
    qbz = P.new("qbz", [128, 2, 2, 512], BF16)
    qcz = P.new("qcz", [128, 2, 2, 512], BF16)
    P.op("pool", lambda e: e.memset(qbz[:], 0.0), writes=[qbz])
    P.op("pool", lambda e: e.memset(qcz[:], 0.0), writes=[qcz])
    raT = P.new("raT", [32, 512], F32)
    P.op("pool", lambda e: e.memset(raT[:], 1.0), writes=[raT])
    mixT = P.new("mixT", [128, 8, 512], BF16)
    uT = [P.new("uT%d" % i, [128, 4, 512], BF16) for i in range(2)]
    ka_tok = P.new("ka_tok", [128, 256], F32)
    va_bf_l = [P.new("va_bf%d" % i, [128, 512], BF16) for i in range(2)]
    ez = P.new("ez", [128, 256], F32)
    sp_t = P.new("sp_t", [128, 256], F32)
    EbT_l = [P.new("EbT%d" % i, [128, 2, 128], F32) for i in range(2)]
    EnbT = P.new("EnbT", [128, 2, 128], F32)
    qtz_l = [P.new("qtz%d" % i, [128, 2, 2, 128], BF16) for i in range(2)]
    for i_ in range(2):
        P.op("pool", lambda e, i_=i_: e.memset(qtz_l[i_][:], 0.0), writes=[qtz_l[i_]])
    ktT_l = [P.new("ktT%d" % i, [128, 2, 128], BF16) for i in range(2)]
    Ebrem = P.new("Ebrem", [128, 256], F32)
    khat_z_l = [[P.new("khat_z%d_%d" % (pp_, i), [128, 256], BF16) for i in range(2)] for pp_ in range(2)]
    attn_z = [P.new("attn_z%d" % i, [128, 4, 64], BF16) for i in range(2)]
    for i_ in range(2):
        for pp_ in range(2):
            P.op("pool", lambda e, i_=i_, pp_=pp_: e.memset(khat_z_l[pp_][i_][:], 0.0), writes=[khat_z_l[pp_][i_]])
        P.op("pool", lambda e, i_=i_: e.memset(attn_z[i_][:], 0.0), writes=[attn_z[i_]])
    osq = P.new("osq", [128, 256], BF16)
    orstd = P.new("orstd", [128, 256], F32)
    o1 = P.new("o1", [128, 256], F32)
    pTc_full = [P.new("pTc_f%d" % i, [128, 256], BF16) for i in range(4)]
    pTc_half = {0: P.new("pTc_h0", [128, 256], BF16), 64: P.new("pTc_h64", [128, 256], BF16)}
    pTb_full = [P.new("pTb_f0", [128, 256], BF16)]
    pTb_half = {0: P.new("pTb_h0", [128, 256], BF16), 64: P.new("pTb_h64", [128, 256], BF16)}
    for t_ in (pTc_half[0], pTc_half[64], pTb_half[0], pTb_half[64]):
        P.op("pool", lambda e, t_=t_: e.memset(t_[:], 0.0), writes=[t_])
    sbc = [0]
    rden = P.new("rden", [128, 2, 64], F32)
    rden_b = P.new("rden_b", [128, 2, 64], F32)
    yst = [P.new("yst%d" % i, [128, 512], F32) for i in range(2)]

    def stats_chunk(c, TG):
        sqc = sq[c % 4]
        P.op("act", lambda e, c=c, sqc=sqc: e.activation(out=sqc[:, 0:TG], in_=xT[:, c, 0:TG], func=AF.Square),
             reads=[xT], writes=[sqc])

        def mm():
            P.op("pe", lambda e, c=c, sqc=sqc: e.matmul(ss_bank[:, 0:TG], lhsT=ones_n[:], rhs=sqc[:, 0:TG],
                                                         start=(c == 0), stop=(c == 7)), reads=[ones_n, sqc], writes=[ss_bank])
        return mm

    def stats_all(TG):
        for c in range(8):
            stats_chunk(c, TG)()

    def rstd_from_stats(TG):
        P.op("act", lambda e: e.activation(out=rstd[:, 0:TG], in_=ss_bank[:, 0:TG], func=AF.Ln, bias=EPS, scale=1.0),
             reads=[ss_bank], writes=[rstd])
        P.op("act", lambda e: e.activation(out=rstd[:, 0:TG], in_=rstd[:, 0:TG], func=AF.Exp, scale=-0.5),
             reads=[rstd], writes=[rstd])

    def norm_apply(TG, segs, Asel, Bsel, out_t):
        rstd_from_stats(TG)
        for c in range(8):
            t = nexttmp()
            P.op("dve", lambda e, c=c, t=t: e.tensor_tensor(out=t[:, 0:TG], in0=xT[:, c, 0:TG], in1=rstd[:, 0:TG], op=ALU.mult),
                 reads=[xT, rstd], writes=[t])
            for (c0, c1, s) in segs:
                a_ap, a_t = Asel(c, s)
                b_ap, b_t = Bsel(c, s)
                P.op("act", lambda e, c=c, t=t, c0=c0, c1=c1, a_ap=a_ap, b_ap=b_ap: e.activation(
                    out=out_t[:, c, c0:c1], in_=t[:, c0:c1], func=AF.Identity, bias=b_ap, scale=a_ap),
                    reads=[t, a_t, b_t], writes=[out_t])

    def gla_chunk(l, i, ci, S, Sbf):
        base = 64 * ci
        cols = slice(i * 128 + base, i * 128 + base + 64)
        pp = i % 2
        va_bf, EbT, qtz, ktT, khat_z = va_bf_l[pp], EbT_l[pp], qtz_l[pp], ktT_l[pp], khat_z_l[pp]
        az = attn_z[ci]
        for h in range(4):
            j, r = h // 2, h % 2
            P.op("pe", lambda e, h=h, j=j, r=r: e.matmul(
                pat[base:base + 64, h * 64:(h + 1) * 64], lhsT=ktT[:, j, base:base + 64],
                rhs=qtz[:, j, r, base:base + 64], start=True, stop=True), reads=[ktT, qtz], writes=[pat])
        yield
        P.op("dve", lambda e: e.tensor_tensor(out=az[base:base + 64, :, :],
                                              in0=pat[base:base + 64, :].rearrange("p (h t) -> p h t", t=64),
                                              in1=mask01[base:base + 64, :].rearrange("p (h t) -> p h t", t=64), op=ALU.mult),
             reads=[pat, mask01], writes=[az])
        yield
        for h in range(4):
            j, r = h // 2, h % 2
            P.op("pe", lambda e, h=h: e.matmul(po[:, h * 64:(h + 1) * 64], lhsT=va_bf[:, h * 128:(h + 1) * 128],
                                               rhs=az[:, h, :], start=True, stop=False),
                 reads=[va_bf, az], writes=[po])
            P.op("pe", lambda e, h=h, j=j, r=r: e.matmul(po[:, h * 64:(h + 1) * 64], lhsT=Sbf[:, j, :],
                                                         rhs=qtz[:, j, r, base:base + 64], start=False, stop=True),
                 reads=[Sbf, qtz], writes=[po])
        pss = nextbig()
        kz = khat_z[ci]
        for h in range(4):
            j = h // 2
            P.op("pe", lambda e, h=h, j=j, pss=pss: e.matmul(pss[:, h * 128:(h + 1) * 128],
                                                             lhsT=kz[:, j * 128:(j + 1) * 128],
                                                             rhs=va_bf[:, h * 128:(h + 1) * 128], start=True, stop=True),
                 reads=[kz, va_bf], writes=[pss])
        yield
        P.op("act", lambda e: e.activation(out=osq[:], in_=po[:], func=AF.Square), reads=[po], writes=[osq])
        for h in range(4):
            j, r = h // 2, h % 2
            P.op("dve", lambda e, h=h, j=j, r=r, pss=pss: e.scalar_tensor_tensor(
                out=S[r * 64:(r + 1) * 64, j, :], in0=S[r * 64:(r + 1) * 64, j, :],
                scalar=EbT[r * 64:(r + 1) * 64, j, base + 63:base + 64], in1=pss[r * 64:(r + 1) * 64, h * 128:(h + 1) * 128],
                op0=ALU.mult, op1=ALU.add), reads=[S, EbT, pss, Sbf, po], writes=[S])
        P.op("pool", lambda e: e.tensor_copy(out=Sbf[:], in_=S[:]), reads=[S], writes=[Sbf])
        yield
        P.op("pe", lambda e: e.matmul(pn[:], lhsT=ones_dv[:], rhs=osq[:], start=True, stop=True), reads=[ones_dv, osq], writes=[pn])
        yield
        P.op("act", lambda e: e.activation(out=orstd[:], in_=pn[:], func=AF.Ln, bias=EPS, scale=1.0), reads=[pn], writes=[orstd])
        P.op("act", lambda e: e.activation(out=orstd[:], in_=orstd[:], func=AF.Exp, scale=-0.5), reads=[orstd], writes=[orstd])
        yield
        P.op("dve", lambda e: e.tensor_tensor(out=o1[:], in0=po[:], in1=orstd[:], op=ALU.mult), reads=[po, orstd], writes=[o1])
        P.op("dve", lambda e: e.scalar_tensor_tensor(
            out=mixT[:, 0:4, cols], in0=o1[:].rearrange("p (h t) -> p h t", t=64), scalar=anorm[:, l:l + 1],
            in1=gate[:, :, cols], op0=ALU.mult, op1=ALU.mult), reads=[o1, anorm, gate], writes=[mixT])

    def attn_c(l, cols, piecesC):
        pts = []
        nfull = 0
        for (kt_t, kfn, v_t, vfn, pb, nk, var) in piecesC:
            for h in range(4):
                j, r = h // 2, h % 2
                P.op("pe", lambda e, h=h, j=j, r=r, kfn=kfn, pb=pb, nk=nk: e.matmul(
                    psc_c[pb:pb + nk, h * 64:(h + 1) * 64], lhsT=kfn(j), rhs=qcz[:, j, r, cols], start=(h == 0), stop=False,
                    skip_group_check=True), reads=[kt_t, qcz], writes=[psc_c])
            P.op("pe", lambda e, pb=pb, nk=nk, var=var: e.matmul(
                psc_c[pb:pb + nk, 0:256], lhsT=ident[:, pb:pb + nk], rhs=biasC[:, l, CVMAP[var], :], start=False, stop=True,
                skip_group_check=True), reads=[ident, biasC], writes=[psc_c])
            yield
            if nk == 128:
                pT = pTc_full[nfull]
                nfull += 1
            else:
                pT = pTc_half[pb]
            P.op("act", lambda e, pT=pT, pb=pb, nk=nk: e.activation(out=pT[pb:pb + nk, :], in_=psc_c[pb:pb + nk, :], func=AF.Exp),
                 reads=[psc_c], writes=[pT])
            pts.append((pT, v_t, vfn))
        yield
        npc = len(pts)
        for h in range(4):
            j, r = h // 2, h % 2
            for pi, (pT, v_t, vfn) in enumerate(pts):
                P.op("pe", lambda e, h=h, j=j, r=r, vfn=vfn, pT=pT, pi=pi: e.matmul(
                    pnd_c[r * 64:(r + 1) * 64, j * 64:(j + 1) * 64], lhsT=vfn(h), rhs=pT[:, h * 64:(h + 1) * 64],
                    start=(pi == 0), stop=(pi == npc - 1)), reads=[v_t, pT], writes=[pnd_c])
            for pi, (pT, v_t, vfn) in enumerate(pts):
                P.op("pe", lambda e, h=h, j=j, r=r, pT=pT, pi=pi: e.matmul(
                    pnd_c[r * 64:(r + 1) * 64, 128 + j * 64:128 + (j + 1) * 64], lhsT=ones_1[:, :],
                    rhs=pT[:, h * 64:(h + 1) * 64], start=(pi == 0), stop=(pi == npc - 1)),
                    reads=[ones_1, pT], writes=[pnd_c])
            if h == 1:
                yield
        yield
        P.op("dve", lambda e: e.reciprocal(rden[:], pnd_c[:, 128:256].rearrange("p (j t) -> p j t", t=64)), reads=[pnd_c], writes=[rden])
        P.op("dve", lambda e: e.tensor_tensor(out=mixT[:, 6:8, cols], in0=pnd_c[:, 0:128].rearrange("p (j t) -> p j t", t=64),
                                              in1=rden[:], op=ALU.mult), reads=[pnd_c, rden], writes=[mixT])

    def attn_b(l, cols, piecesB):
        pts = []
        for (kt_t, kfn, v_t, vfn, pb, nk, var) in piecesB:
            for g in range(2):
                for r in range(2):
                    P.op("pe", lambda e, g=g, r=r, kfn=kfn, pb=pb, nk=nk: e.matmul(
                        psc_b[pb:pb + nk, g * 128 + r * 64:g * 128 + (r + 1) * 64], lhsT=kfn(None),
                        rhs=qbz[:, r, g, cols], start=(g == 0 and r == 0), stop=False, skip_group_check=True),
                        reads=[kt_t, qbz], writes=[psc_b])
            P.op("pe", lambda e, pb=pb, nk=nk, var=var: e.matmul(
                psc_b[pb:pb + nk, 0:256], lhsT=ident[:, pb:pb + nk], rhs=biasB[:, var, :], start=False, stop=True,
                skip_group_check=True), reads=[ident, biasB], writes=[psc_b])
            yield
            pT = pTb_full[0] if nk == 128 else pTb_half[pb]
            P.op("act", lambda e, pT=pT, pb=pb, nk=nk: e.activation(out=pT[pb:pb + nk, :], in_=psc_b[pb:pb + nk, :], func=AF.Exp),
                 reads=[psc_b], writes=[pT])
            pts.append((pT, v_t, vfn))
        yield
        npb = len(pts)
        for h in range(4):
            g, r = h // 2, h % 2
            for pi, (pT, v_t, vfn) in enumerate(pts):
                P.op("pe", lambda e, h=h, g=g, r=r, vfn=vfn, pT=pT, pi=pi: e.matmul(
                    pnd_b[r * 64:(r + 1) * 64, g * 64:(g + 1) * 64], lhsT=vfn(g),
                    rhs=pT[:, g * 128 + r * 64:g * 128 + (r + 1) * 64],
                    start=(pi == 0), stop=(pi == npb - 1)), reads=[v_t, pT], writes=[pnd_b])
            for pi, (pT, v_t, vfn) in enumerate(pts):
                P.op("pe", lambda e, h=h, g=g, r=r, pT=pT, pi=pi: e.matmul(
                    pnd_b[r * 64:(r + 1) * 64, 128 + g * 64:128 + (g + 1) * 64], lhsT=ones_1[:, :],
                    rhs=pT[:, g * 128 + r * 64:g * 128 + (r + 1) * 64],
                    start=(pi == 0), stop=(pi == npb - 1)), reads=[ones_1, pT], writes=[pnd_b])
        yield
        for j in range(2):
            P.op("dve", lambda e, j=j: e.tensor_scalar(rden_b[:, j, :], pnd_b[:, 128 + j * 64:128 + (j + 1) * 64], sinke[:, l, j:j + 1], None,
                                                       op0=ALU.add), reads=[pnd_b, sinke], writes=[rden_b])
        P.op("dve", lambda e: e.reciprocal(rden_b[:], rden_b[:]), reads=[rden_b], writes=[rden_b])
        P.op("dve", lambda e: e.tensor_tensor(out=mixT[:, 4:6, cols], in0=pnd_b[:, 0:128].rearrange("p (j t) -> p j t", t=64),
                                              in1=rden_b[:], op=ALU.mult), reads=[pnd_b, rden_b], writes=[mixT])

    def run_interleaved(gens):
        gens = list(gens)
        while gens:
            for g_ in list(gens):
                try:
                    next(g_)
                except StopIteration:
                    gens.remove(g_)

    def process_group(gi, is_sample):
        NT = 2 if is_sample else 4
        TG = NT * 128
        if is_sample:
            segs = [(s * 64, (s + 1) * 64, 1 + s) for s in range(4)]
            src = xTs[:, :, :]
        else:
            segs = [(0, TG, 0)]
            src = xTp[:, :, gi * 512:(gi + 1) * 512]
        t0 = gi * 4
        P.dma("act", lambda e: e.dma_start(out=xT[:, :, 0:TG], in_=src), writes=[xT])
        for l in range(2):
            if is_sample:
                for s in range(4):
                    P.dma("sp", lambda e, s=s: e.dma_start(out=kbs[l, s, 0:64, :], in_=cbk[l, s, 64:128, :]), writes=[OUT["kbs"]])
                    P.dma("sp", lambda e, s=s: e.dma_start(out=vbs[l, s, 0:64, :], in_=cbv[l, s, 64:128, :]), writes=[OUT["vbs"]])
                    P.dma("sp", lambda e, s=s: e.dma_start(out=kcs[l, s, 0:448, :], in_=cck[l, s, 64:512, :]), writes=[OUT["kcs"]])
                    P.dma("sp", lambda e, s=s: e.dma_start(out=vcs[l, s, 0:448, :], in_=ccv[l, s, 64:512, :]), writes=[OUT["vcs"]])
            if l == 0:
                stats_all(TG)
            norm_apply(TG, segs,
                       lambda c, s: (Amix[:, l, c, s:s + 1], Amix),
                       lambda c, s: (mod_ap(l, 0, c, s), mod), hT)
            stage(2)
            if is_sample:
                rcol0 = 0
                kcT_dst, kbT_dst = own_kcT, own_kbT
            else:
                rcol0 = (t0 % NR) * 128
                kcT_dst, kbT_dst = ring_kcT[l], ring_kbT[l]
            for p in range(4):
                wb = wget("wf%d_%d" % (l, p))
                for q in range(4):
                    cc = p * 4 + q
                    if cc >= int(os.environ.get("KERNEL_CCMAX", "16")):
                        continue
                    M = 16 if cc == 15 else 128
                    pf = nextbig()
                    for kc in range(8):
                        P.op("pe", lambda e, wb=wb, q=q, kc=kc, pf=pf, M=M: e.matmul(
                            pf[0:M, 0:TG], lhsT=wb[:, kc, q * 128:q * 128 + M], rhs=hT[:, kc, 0:TG],
                            start=(kc == 0), stop=(kc == 7)), reads=[wb, hT], writes=[pf])
                    if os.environ.get("KERNEL_SKIPEVAC") == "1":
                        continue
                    if cc in (0, 1):
                        P.op("act", lambda e, cc=cc, pf=pf: e.copy(qaT[:, cc, 0:TG], pf[:, 0:TG]),
                             reads=[pf], writes=[qaT])
                    elif cc in (2, 3):
                        P.op("dve", lambda e, cc=cc, pf=pf: e.tensor_copy(out=kaT[:, cc - 2, 0:TG], in_=pf[:, 0:TG]),
                             reads=[pf], writes=[kaT])
                    elif cc in (4, 5, 6, 7):
                        t = nexttmp()
                        P.op("act", lambda e, pf=pf, t=t: e.activation(out=t[:, 0:TG], in_=pf[:, 0:TG], func=AF.Exp, scale=-1.0),
                             reads=[pf], writes=[t])
                        P.op("dve", lambda e, t=t: e.tensor_scalar(t[:, 0:TG], t[:, 0:TG], 1.0, None, op0=ALU.add), reads=[t], writes=[t])
                        P.op("dve", lambda e, t=t: e.reciprocal(t[:, 0:TG], t[:, 0:TG]), reads=[t], writes=[t])
                        P.op("dve", lambda e, cc=cc, pf=pf, t=t: e.tensor_tensor(out=gate[:, cc - 4, 0:TG], in0=pf[:, 0:TG], in1=t[:, 0:TG],
                                                                                  op=ALU.mult), reads=[pf, t], writes=[gate])
                    elif cc in (8, 9):
                        for g_ in range(2):
                            P.op("act", lambda e, cc=cc, pf=pf, g_=g_: e.mul(qbz[g_ * 64:(g_ + 1) * 64, cc - 8, g_, 0:TG],
                                                                             pf[g_ * 64:(g_ + 1) * 64, 0:TG], 0.125),
                                 reads=[pf], writes=[qbz])
                    elif cc == 10:
                        P.op("act", lambda e, pf=pf: e.copy(kbT_dst[:, rcol0:rcol0 + TG], pf[:, 0:TG]),
                             reads=[pf], writes=[kbT_dst])
                    elif cc in (11, 12):
                        for r_ in range(2):
                            P.op("act", lambda e, cc=cc, pf=pf, r_=r_: e.mul(qcz[r_ * 64:(r_ + 1) * 64, cc - 11, r_, 0:TG],
                                                                             pf[r_ * 64:(r_ + 1) * 64, 0:TG], 0.125),
                                 reads=[pf], writes=[qcz])
                    elif cc in (13, 14):
                        P.op("dve", lambda e, cc=cc, pf=pf: e.tensor_copy(out=kcT_dst[:, cc - 13, rcol0:rcol0 + TG], in_=pf[:, 0:TG]),
                             reads=[pf], writes=[kcT_dst])
                    else:
                        P.op("dve", lambda e, pf=pf: e.tensor_copy(out=raT[0:16, 0:TG], in_=pf[0:16, 0:TG]), reads=[pf], writes=[raT])
            stage(3)
            wts = [wget("wt%d_%d" % (l, p)) for p in range(3)]
            last_group = (not is_sample) and gi == n_pgroups - 1

            def tile_prep(i):
                pp = i % 2
                va_bf, EbT, qtz, ktT, khat_z = va_bf_l[pp], EbT_l[pp], qtz_l[pp], ktT_l[pp], khat_z_l[pp]
                tcols = slice(i * 128, (i + 1) * 128)
                slot = (t0 + i) % NR
                pbank = []
                for p in range(3):
                    pk = nextbig()
                    for kc in range(8):
                        P.op("pe", lambda e, p=p, kc=kc, pk=pk: e.matmul(pk[:, :], lhsT=hT[:, kc, tcols], rhs=wts[p][:, kc, :],
                                                                         start=(kc == 0), stop=(kc == 7)), reads=[hT, wts[p]], writes=[pk])
                    pbank.append(pk)
                pA, pB, pC = pbank
                P.op("dve", lambda e, pA=pA: e.tensor_copy(out=ka_tok[:], in_=pA[:, 0:256]), reads=[pA], writes=[ka_tok])
                P.op("act", lambda e, pB=pB: e.copy(va_bf[:], pB[:, :]), reads=[pB], writes=[va_bf])
                if is_sample:
                    vb_dst, vb_ap = own_vb, own_vb[:, i, :]
                    vc_dst, vc_ap = own_vc, own_vc[:, i, :]
                else:
                    vb_dst, vb_ap = ring_vb[l], ring_vb[l][:, slot, :]
                    vc_dst, vc_ap = ring_vc[l], ring_vc[l][:, slot, :]
                P.op("act", lambda e, pA=pA, vb_ap=vb_ap: e.copy(vb_ap, pA[:, 384:512]), reads=[pA], writes=[vb_dst])
                P.op("act", lambda e, pC=pC, vc_ap=vc_ap: e.copy(vc_ap, pC[:, 256:512]), reads=[pC], writes=[vc_dst])
                last_group = (not is_sample) and gi == n_pgroups - 1
                NOOUT = os.environ.get("KERNEL_NOOUT") == "1"
                OUTM = os.environ.get("KERNEL_OUTM", "CcBb")
                if (is_sample or last_group) and not NOOUT:
                    if "C" in OUTM:
                        P.op("dve", lambda e, pC=pC: e.tensor_copy(out=stg_c[:, 0:256], in_=pC[:, 0:256]), reads=[pC], writes=[stg_c])
                        P.op("act", lambda e, pC=pC: e.copy(stg_c[:, 256:512], pC[:, 256:512]), reads=[pC], writes=[stg_c])
                    if "c" not in OUTM:
                        pass
                    elif is_sample:
                        for ci in range(2):
                            s = 2 * i + ci
                            b0 = 64 * ci
                            P.dma(os.environ.get("KERNEL_OUTQ", "sp"), lambda e, s=s, b0=b0: e.dma_start(out=kcs[l, s, 448:512, :], in_=stg_c[b0:b0 + 64, 0:256]),
                                  reads=[stg_c], writes=[OUT["kcs"]])
                            P.dma(os.environ.get("KERNEL_OUTQ", "sp"), lambda e, s=s, b0=b0: e.dma_start(out=vcs[l, s, 448:512, :], in_=stg_c[b0:b0 + 64, 256:512]),
                                  reads=[stg_c], writes=[OUT["vcs"]])
                    else:
                        P.dma("sp", lambda e, i=i: e.dma_start(out=kcp[l, i * 128:(i + 1) * 128, :], in_=stg_c[:, 0:256]),
                              reads=[stg_c], writes=[OUT["kcp"]])
                        P.dma("sp", lambda e, i=i: e.dma_start(out=vcp[l, i * 128:(i + 1) * 128, :], in_=stg_c[:, 256:512]),
                              reads=[stg_c], writes=[OUT["vcp"]])
                if (is_sample or (last_group and i == NT - 1)) and not NOOUT:
                    if "B" in OUTM:
                        P.op("dve", lambda e, pA=pA: e.tensor_copy(out=stg_b[:, 0:128], in_=pA[:, 256:384]), reads=[pA], writes=[stg_b])
                        P.op("act", lambda e, pA=pA: e.copy(stg_b[:, 128:256], pA[:, 384:512]), reads=[pA], writes=[stg_b])
                    if "b" not in OUTM:
                        pass
                    elif is_sample:
                        for ci in range(2):
                            s = 2 * i + ci
                            b0 = 64 * ci
                            P.dma(os.environ.get("KERNEL_OUTQ", "sp"), lambda e, s=s, b0=b0: e.dma_start(out=kbs[l, s, 64:128, :], in_=stg_b[b0:b0 + 64, 0:128]),
                                  reads=[stg_b], writes=[OUT["kbs"]])
                            P.dma(os.environ.get("KERNEL_OUTQ", "sp"), lambda e, s=s, b0=b0: e.dma_start(out=vbs[l, s, 64:128, :], in_=stg_b[b0:b0 + 64, 128:256]),
                                  reads=[stg_b], writes=[OUT["vbs"]])
                    else:
                        P.dma("sp", lambda e: e.dma_start(out=kbp[l, :, :], in_=stg_b[:, 0:128]), reads=[stg_b], writes=[OUT["kbp"]])
                        P.dma("sp", lambda e: e.dma_start(out=vbp[l, :, :], in_=stg_b[:, 128:256]), reads=[stg_b], writes=[OUT["vbp"]])
                stage(4)
                P.op("pe", lambda e: e.matmul(pz[:], lhsT=raT[0:32, tcols], rhs=wa2[:, l, :], start=True, stop=True),
                     reads=[raT, wa2], writes=[pz])
                P.op("act", lambda e: e.activation(out=ez[:], in_=pz[:], func=AF.Exp, scale=-1.0), reads=[pz], writes=[ez])
                P.op("act", lambda e: e.activation(out=sp_t[:], in_=ez[:], func=AF.Ln, bias=1.0, scale=1.0), reads=[ez], writes=[sp_t])
                for j in range(2):
                    P.op("pe", lambda e, j=j: e.matmul(pbT[:, j * 128:(j + 1) * 128], lhsT=sp_t[:, j * 128:(j + 1) * 128], rhs=lincl[:],
                                                       start=True, stop=True), reads=[sp_t, lincl], writes=[pbT])
                P.op("pe", lambda e: e.matmul(pbrem[:], lhsT=lafter[:], rhs=sp_t[:], start=True, stop=True),
                     reads=[lafter, sp_t], writes=[pbrem])
                P.op("act", lambda e: e.activation(out=EbT[:], in_=pbT[:].rearrange("p (j t) -> p j t", t=128), func=AF.Exp),
                     reads=[pbT], writes=[EbT])
                P.op("act", lambda e: e.activation(out=EnbT[:], in_=pbT[:].rearrange("p (j t) -> p j t", t=128), func=AF.Exp, scale=-1.0),
                     reads=[pbT], writes=[EnbT])
                P.op("act", lambda e: e.activation(out=Ebrem[:], in_=pbrem[:], func=AF.Exp), reads=[pbrem], writes=[Ebrem])
                for r_ in range(2):
                    P.op("dve", lambda e, r_=r_: e.scalar_tensor_tensor(
                        out=qtz[r_ * 64:(r_ + 1) * 64, :, r_, :], in0=qaT[r_ * 64:(r_ + 1) * 64, :, tcols], scalar=0.125,
                        in1=EbT[r_ * 64:(r_ + 1) * 64, :, :], op0=ALU.mult, op1=ALU.mult), reads=[qaT, EbT], writes=[qtz])
                P.op("dve", lambda e: e.tensor_tensor(out=ktT[:], in0=kaT[:, :, tcols], in1=EnbT[:], op=ALU.mult),
                     reads=[kaT, EnbT], writes=[ktT])
                for c_ in range(2):
                    P.op("dve", lambda e, c_=c_: e.tensor_tensor(out=khat_z[c_][c_ * 64:(c_ + 1) * 64, :], in0=ka_tok[c_ * 64:(c_ + 1) * 64, :],
                                                                 in1=Ebrem[c_ * 64:(c_ + 1) * 64, :], op=ALU.mult),
                         reads=[ka_tok, Ebrem], writes=[khat_z[c_]])

            def tile_chunks(i):
                stage(5)
                for ci in range(2):
                    cols = slice(i * 128 + 64 * ci, i * 128 + 64 * ci + 64)
                    if is_sample:
                        s = 2 * i + ci
                        S, Sbf = S_s[s], Sbf_s[s]
                        P.dma("sp", lambda e, s=s: e.dma_start(out=S_s[s][:], in_=st_d[l, s, :, :, :]), writes=[S_s[s]])
                        P.op("pool", lambda e, s=s: e.tensor_copy(out=Sbf_s[s][:], in_=S_s[s][:]), reads=[S_s[s]], writes=[Sbf_s[s]])
                        P.dma("pool", lambda e, s=s: e.dma_start(out=s_kcT[s][:], in_=kcT_c[l, s, :, :, :]), writes=[s_kcT[s]])
                        P.dma("pool", lambda e, s=s: e.dma_start(out=s_vc[s][:], in_=vc_c[l, s, :, :, :]), writes=[s_vc[s]])
                        P.dma("pool", lambda e, s=s: e.dma_start(out=s_kbT[s][:], in_=kbT_c[l, s, :, :]), writes=[s_kbT[s]])
                        P.dma("pool", lambda e, s=s: e.dma_start(out=s_vb[s][:, 0, :], in_=vb_c[l, s, :, :]), writes=[s_vb[s]])
                    else:
                        S, Sbf = S_p[l], Sbf_p[l]
                    gen_gla = gla_chunk(l, i, ci, S, Sbf)
                    stage(6)
                    pb_own = 64 * ci
                    if is_sample:
                        pC_l = []
                        for k4 in range(4):
                            pC_l.append((s_kcT[s], (lambda j, k4=k4, s=s: s_kcT[s][:, j, k4 * 128:(k4 + 1) * 128]),
                                         s_vc[s], (lambda h, k4=k4, s=s: s_vc[s][:, k4, h * 64:(h + 1) * 64]), 0, 128, k4))
                        pC_l.append((own_kcT, (lambda j, i=i, pb_own=pb_own: own_kcT[:, j, i * 128 + pb_own:i * 128 + pb_own + 64]),
                                     own_vc, (lambda h, i=i: own_vc[:, i, h * 64:(h + 1) * 64]), pb_own, 64, 4))
                        pB_l = [(s_kbT[s], (lambda j, s=s: s_kbT[s][:, 0:128]),
                                 s_vb[s], (lambda g, s=s: s_vb[s][:, 0, g * 64:(g + 1) * 64]), 0, 128, 0),
                                (own_kbT, (lambda j, i=i, pb_own=pb_own: own_kbT[:, i * 128 + pb_own:i * 128 + pb_own + 64]),
                                 own_vb, (lambda g, i=i: own_vb[:, i, g * 64:(g + 1) * 64]), pb_own, 64, 1)]
                    else:
                        t = t0 + i
                        rk, rv, rkb, rvb = ring_kcT[l], ring_vc[l], ring_kbT[l], ring_vb[l]

                        def mkC(tk, pb, nk, var):
                            sl = tk % NR
                            return (rk, (lambda j, sl=sl, pb=pb, nk=nk, rk=rk: rk[:, j, sl * 128 + pb:sl * 128 + pb + nk]),
                                    rv, (lambda h, sl=sl, rv=rv: rv[:, sl, h * 64:(h + 1) * 64]), pb, nk, var)

                        def mkB(tk, pb, nk, var):
                            sl = tk % NR
                            return (rkb, (lambda j, sl=sl, pb=pb, nk=nk, rkb=rkb: rkb[:, sl * 128 + pb:sl * 128 + pb + nk]),
                                    rvb, (lambda g, sl=sl, rvb=rvb: rvb[:, sl, g * 64:(g + 1) * 64]), pb, nk, var)
                        pC_l, pB_l = [], []
                        if ci == 0:
                            for k4 in range(4):
                                tk = t - 4 + k4
                                if tk >= 0:
                                    pC_l.append(mkC(tk, 0, 128, k4))
                            pC_l.append(mkC(t, 0, 64, 4))
                            if t - 1 >= 0:
                                pB_l.append(mkB(t - 1, 0, 128, 0))
                            pB_l.append(mkB(t, 0, 64, 1))
                        else:
                            if t - 4 >= 0:
                                pC_l.append(mkC(t - 4, 64, 64, 5))
                            for k4 in range(4):
                                tk = t - 3 + k4
                                if tk >= 0:
                                    pC_l.append(mkC(tk, 0, 128, 6 + k4))
                            if t - 1 >= 0:
                                pB_l.append(mkB(t - 1, 64, 64, 2))
                            pB_l.append(mkB(t, 0, 128, 3))
                    run_interleaved([gen_gla, attn_c(l, cols, pC_l), attn_b(l, cols, pB_l)])
                    if is_sample:
                        P.dma("sp", lambda e, s=s: e.dma_start(out=sgs[l, s, :, :, :], in_=S_s[s][:]), reads=[S_s[s]], writes=[OUT["sgs"]])
                    elif last_group and i == NT - 1 and ci == 1:
                        P.dma("sp", lambda e: e.dma_start(out=sgp[l, :, :, :], in_=S_p[l][:]), reads=[S_p[l]], writes=[OUT["sgp"]])

            tile_prep(0)
            for i in range(NT):
                if i + 1 < NT:
                    tile_prep(i + 1)
                tile_chunks(i)
            stage(7)
            wos = [wget("wo%d_%d" % (l, p)) for p in range(2)]
            pend = []
            for oc in range(8):
                wb = wos[oc // 4]
                q = oc % 4
                pf = nextbig()
                for kc in range(8):
                    P.op("pe", lambda e, wb=wb, q=q, kc=kc, pf=pf: e.matmul(pf[:, 0:TG], lhsT=wb[:, kc, q * 128:(q + 1) * 128], rhs=mixT[:, kc, 0:TG],
                                                                            start=(kc == 0), stop=(kc == 7)), reads=[wb, mixT], writes=[pf])
                for (c0, c1, s) in segs:
                    P.op("dve", lambda e, oc=oc, pf=pf, c0=c0, c1=c1, s=s: e.scalar_tensor_tensor(
                        out=xT[:, oc, c0:c1], in0=pf[:, c0:c1], scalar=mod_ap(l, 2, oc, s), in1=xT[:, oc, c0:c1],
                        op0=ALU.mult, op1=ALU.add), reads=[pf, mod, xT], writes=[xT])
                pend.append(stats_chunk(oc, TG))
                if len(pend) > 2:
                    pend.pop(0)()
            while pend:
                pend.pop(0)()
            stage(8)
            norm_apply(TG, segs,
                       lambda c, s: (Amlp[:, l, c, s:s + 1], Amlp),
                       lambda c, s: (mod_ap(l, 3, c, s), mod), hT)
            def up_block(fb, wu, u):
                for fc in range(4):
                    pf = nextbig()
                    for kc in range(8):
                        P.op("pe", lambda e, wu=wu, fc=fc, kc=kc, pf=pf: e.matmul(pf[:, 0:TG], lhsT=wu[:, kc, fc * 128:(fc + 1) * 128], rhs=hT[:, kc, 0:TG],
                                                                                  start=(kc == 0), stop=(kc == 7)), reads=[wu, hT], writes=[pf])
                    t = nexttmp()
                    P.op("act", lambda e, pf=pf, t=t: e.activation(out=t[:, 0:TG], in_=pf[:, 0:TG], func=AF.Relu), reads=[pf], writes=[t])
                    P.op("pool", lambda e, u=u, fc=fc, t=t: e.tensor_tensor(out=u[:, fc, 0:TG], in0=t[:, 0:TG], in1=t[:, 0:TG], op=ALU.mult),
                         reads=[t], writes=[u])

            def down_block(fb, wd, u):
                for oc in range(8):
                    hf, q = oc // 4, oc % 4
                    pf = nextbig()
                    for fc in range(4):
                        P.op("pe", lambda e, wd=wd, fc=fc, hf=hf, q=q, pf=pf, u=u: e.matmul(
                            pf[:, 0:TG], lhsT=wd[:, fc * 2 + hf, q * 128:(q + 1) * 128], rhs=u[:, fc, 0:TG],
                            start=(fc == 0), stop=(fc == 3)), reads=[wd, u], writes=[pf])
                    for (c0, c1, s) in segs:
                        P.op("dve", lambda e, oc=oc, pf=pf, c0=c0, c1=c1, s=s: e.scalar_tensor_tensor(
                            out=xT[:, oc, c0:c1], in0=pf[:, c0:c1], scalar=mod_ap(l, 5, oc, s), in1=xT[:, oc, c0:c1],
                            op0=ALU.mult, op1=ALU.add), reads=[pf, mod, xT], writes=[xT])
                    if fb == 7:
                        pend.append(stats_chunk(oc, TG))
                        if len(pend) > 2:
                            pend.pop(0)()
                while fb == 7 and pend:
                    pend.pop(0)()

            wu_cur = wget("wu%d_%d" % (l, 0))
            up_block(0, wu_cur, uT[0])
            for fb in range(8):
                wd_cur = wget("wd%d_%d" % (l, fb))
                if fb + 1 < 8:
                    wu_nxt = wget("wu%d_%d" % (l, fb + 1))
                    up_block(fb + 1, wu_nxt, uT[(fb + 1) % 2])
                down_block(fb, wd_cur, uT[fb % 2])
        rstd_from_stats(TG)
        for c in range(8):
            yc = yst[c % 2]
            P.op("dve", lambda e, c=c, yc=yc: e.scalar_tensor_tensor(out=yc[:, 0:TG], in0=xT[:, c, 0:TG], scalar=gfin[:, c:c + 1],
                                                                     in1=rstd[:, 0:TG], op0=ALU.mult, op1=ALU.mult),
                 reads=[xT, gfin, rstd], writes=[yc])
            if is_sample:
                P.dma("act", lambda e, c=c, yc=yc: e.dma_start(out=yTs[:, c, :], in_=yc[:, 0:TG]), reads=[yc], writes=[OUT["yTs"]])
            else:
                P.dma("act", lambda e, c=c, yc=yc: e.dma_start(out=yTp[:, c, gi * 512:(gi + 1) * 512], in_=yc[:, 0:TG]),
                      reads=[yc], writes=[OUT["yTp"]])

    try:
        stage(1)
        process_group(0, True)
        for gi in range(n_pgroups):
            process_group(gi, False)
        assert wstate["used"] == len(specs), (wstate, len(specs))
    except StopBuild:
        pass
    P.build()
    return nc, P.stats


def _feat_major(x2d):
    T_ = x2d.shape[0]
    return np.ascontiguousarray(x2d.reshape(T_, 8, 128).transpose(2, 1, 0))


def _from_feat_major(yT):
    T_ = yT.shape[2]
    return np.ascontiguousarray(yT.transpose(2, 1, 0).reshape(T_, 1024))


def _vec_fm(v):
    lead = v.shape[:-1]
    a = v.reshape(*lead, 8, 128)
    a = np.moveaxis(a, -1, 0)
    return np.ascontiguousarray(a)


_N_PGROUPS = int(os.environ.get("KERNEL_NPG", "32"))


def kernel(x_prompt, x_sample, c_prompt, c_sample, state_gla, cache_b_k, cache_b_v, cache_c_k, cache_c_v,
           w_ada, b_ada, norm_mix_g, norm_mlp_g, w_in, w_a2, b_a2, a_norm_g, b_sink, t5_bias, c_rel_bias,
           w_out, w_up, w_down, final_norm_g):
    f = lambda a: np.asarray(a, dtype=np.float32)
    x_prompt, x_sample, c_prompt, c_sample = f(x_prompt), f(x_sample), f(c_prompt), f(c_sample)
    state_gla, cache_b_k, cache_b_v, cache_c_k, cache_c_v = f(state_gla), f(cache_b_k), f(cache_b_v), f(cache_c_k), f(cache_c_v)
    w_ada, b_ada, norm_mix_g, norm_mlp_g, w_in, w_a2, b_a2 = f(w_ada), f(b_ada), f(norm_mix_g), f(norm_mlp_g), f(w_in), f(w_a2), f(b_a2)
    a_norm_g, b_sink, t5_bias, c_rel_bias, w_out, w_up, w_down, final_norm_g = (
        f(a_norm_g), f(b_sink), f(t5_bias), f(c_rel_bias), f(w_out), f(w_up), f(w_down), f(final_norm_g))
    npg = _N_PGROUPS
    nc, stats = build_program(npg)

    o = np.cumsum([0, 256, 256, 512, 512, 16, 256, 128, 128, 256, 256, 256])
    qa, ka, va, ga, ra, qb, kb, vb, qc, kc, vc = [np.arange(o[i], o[i + 1]) for i in range(11)]
    qb_re = np.concatenate([qb[0:64], qb[128:192], qb[64:128], qb[192:256]])
    feat_cols = np.concatenate([qa, ka, ga, qb_re, kb, qc, kc, ra])
    Wf = np.zeros((2, 1024, 2048), np.float32)
    Wf[:, :, :feat_cols.size] = w_in[:, :, feat_cols]
    tok_cols = np.concatenate([ka, kb, vb, va, kc, vc])
    Wt = np.ascontiguousarray(w_in[:, :, tok_cols])
    wa2 = np.zeros((32, 2, 256), np.float32)
    wa2[0:16] = w_a2.transpose(1, 0, 2)
    wa2[16] = b_a2
    badaR = np.ascontiguousarray(np.broadcast_to(
        b_ada.reshape(2, 48, 128).transpose(2, 0, 1)[:, :, :, None], (128, 2, 48, 5)))
    gmix = _vec_fm(norm_mix_g)
    gmlp = _vec_fm(norm_mlp_g)
    gfin = _vec_fm(final_norm_g)
    anorm = np.ascontiguousarray(a_norm_g.T)
    sinkT = np.zeros((128, 2, 2), np.float32)
    for l in range(2):
        for j in range(2):
            sinkT[0:64, l, j] = b_sink[l, 2 * j]
            sinkT[64:128, l, j] = b_sink[l, 2 * j + 1]
    biasC, biasB = build_bias_tables(t5_bias, c_rel_bias)
    sidx = np.arange(128)
    same = (sidx[:, None] // 64) == (sidx[None, :] // 64)
    lincl = np.where(same & (sidx[:, None] <= sidx[None, :]), -1.0 / 16.0, 0.0).astype(np.float32)
    lafter = np.where(same & (sidx[:, None] > sidx[None, :]), -1.0 / 16.0, 0.0).astype(np.float32)
    m = ((sidx[:, None] % 64) <= np.arange(64)[None, :]).astype(np.float32)
    mask01 = np.ascontiguousarray(np.broadcast_to(m[:, None, :], (128, 4, 64)).reshape(128, 256))

    xTp = _feat_major(x_prompt[0, :npg * 512])
    cache_k_b2 = cache_b_k.reshape(2, 32, 128, 128)
    cache_v_b2 = cache_b_v.reshape(2, 32, 128, 128)
    cache_k_c2 = cache_c_k.reshape(2, 32, 512, 256)
    cache_v_c2 = cache_c_v.reshape(2, 32, 512, 256)
    shared = dict(xTp=xTp, Wf=Wf, Wt=Wt, Wo=np.ascontiguousarray(w_out), Wu=np.ascontiguousarray(w_up), Wd=np.ascontiguousarray(w_down),
                  Wa=np.ascontiguousarray(w_ada), badaR=badaR, gmix=gmix, gmlp=gmlp, gfin=gfin, wa2=wa2, anorm=anorm,
                  sinkT=sinkT, biasB=biasB, biasC=biasC, lincl=lincl, lafter=lafter, mask01=mask01,
                  ident=np.eye(128, dtype=np.float32))
    in_maps = []
    for c in range(NCORES):
        bs = slice(c * 4, (c + 1) * 4)
        xs = x_sample[bs].reshape(256, 1024)
        call = np.concatenate([c_prompt, c_sample[bs]], axis=0)
        cT = np.ascontiguousarray(call.reshape(5, 8, 128).transpose(2, 1, 0))
        st = state_gla[:, bs].reshape(2, 4, 2, 2, 64, 128).transpose(0, 1, 3, 4, 2, 5).reshape(2, 4, 128, 2, 128)
        kbT_c = cache_k_b2[:, bs].transpose(0, 1, 3, 2)
        kcT_c = cache_c_k[:, bs].reshape(2, 4, 512, 2, 2, 64).transpose(0, 1, 4, 5, 3, 2).reshape(2, 4, 128, 2, 512)
        vc_c = cache_v_c2[:, bs].reshape(2, 4, 4, 128, 256).transpose(0, 1, 3, 2, 4)
        d = dict(shared)
        d.update(xTs=_feat_major(xs), cT=cT, st=np.ascontiguousarray(st), kbT_c=np.ascontiguousarray(kbT_c),
                 vb_c=np.ascontiguousarray(cache_v_b2[:, bs]), kcT_c=np.ascontiguousarray(kcT_c), vc_c=np.ascontiguousarray(vc_c),
                 cbk=np.ascontiguousarray(cache_k_b2[:, bs]), cbv=np.ascontiguousarray(cache_v_b2[:, bs]),
                 cck=np.ascontiguousarray(cache_k_c2[:, bs]), ccv=np.ascontiguousarray(cache_v_c2[:, bs]))
        in_maps.append(d)
    res = run_bass_kernel_spmd(nc, in_maps, core_ids=list(range(NCORES)))
    R = res.results

    def unS(a):
        lead = a.shape[:-3]
        a = a.reshape(*lead, 2, 64, 2, 128)
        a = np.moveaxis(a, -2, -4)
        return np.ascontiguousarray(a.reshape(*lead, 4, 64, 128))

    y_prompt = _from_feat_major(R[0]["yTp"])[None]
    y_sample = np.concatenate([_from_feat_major(R[c]["yTs"]).reshape(4, 64, 1024) for c in range(NCORES)], axis=0)
    sg_p = unS(R[0]["sgp"])[:, None]
    kb_p = R[0]["kbp"].reshape(2, 1, 128, 2, 64)
    vb_p = R[0]["vbp"].reshape(2, 1, 128, 2, 64)
    kc_p = R[0]["kcp"].reshape(2, 1, 512, 4, 64)
    vc_p = R[0]["vcp"].reshape(2, 1, 512, 4, 64)
    sg_s = np.concatenate([unS(R[c]["sgs"]) for c in range(NCORES)], axis=1)
    kb_s = np.concatenate([R[c]["kbs"].reshape(2, 4, 128, 2, 64) for c in range(NCORES)], axis=1)
    vb_s = np.concatenate([R[c]["vbs"].reshape(2, 4, 128, 2, 64) for c in range(NCORES)], axis=1)
    kc_s = np.concatenate([R[c]["kcs"].reshape(2, 4, 512, 4, 64) for c in range(NCORES)], axis=1)
    vc_s = np.concatenate([R[c]["vcs"].reshape(2, 4, 512, 4, 64) for c in range(NCORES)], axis=1)
    outs = (y_prompt, y_sample, sg_p, kb_p, vb_p, kc_p, vc_p, sg_s, kb_s, vb_s, kc_s, vc_s)
    return tuple(np.ascontiguousarray(o_, dtype=np.float32) for o_ in outs)
```

```python
import contextlib
import os
import types
import numpy as np
import concourse.bass as bass
import concourse.mybir as mybir
from concourse.bass_utils import run_bass_kernel_spmd

F32 = mybir.dt.float32
BF16 = mybir.dt.bfloat16
AF = mybir.ActivationFunctionType
ALU = mybir.AluOpType
ENGS = ("pe", "act", "dve", "pool", "sp")

NCORES = 8
D = 1024
SEQ = 16384
NSEQ_CORE = 4
EPS = 1e-6


class T:
    __slots__ = ("name", "ap", "last_w", "readers", "dsem", "dval", "excl", "last_acc", "t")

    def __init__(self, name, ap, excl=False):
        self.name = name
        self.ap = ap
        self.last_w = None
        self.readers = []
        self.dsem = {}
        self.dval = {}
        self.excl = excl
        self.last_acc = {}
        self.t = self

    def __getitem__(self, k):
        return self.ap[k]


class V:
    __slots__ = ("t", "ap")

    def __init__(self, t, ap):
        self.t = t
        self.ap = ap

    def __getitem__(self, k):
        return self.ap[k]


class Ins:
    __slots__ = ("eng", "fn", "reads", "writes", "is_dma", "deps", "signal", "tok", "dtile", "ndma")

    def __init__(self, eng, fn, reads, writes, is_dma=False, dtile=None, ndma=1):
        self.eng = eng
        self.fn = fn
        self.reads = reads
        self.writes = writes
        self.is_dma = is_dma
        self.deps = []
        self.signal = False
        self.tok = None
        self.dtile = dtile
        self.ndma = ndma


def _freeze(fn):
    if fn.__closure__ is None:
        return fn
    cells = []
    for c in fn.__closure__:
        try:
            cells.append(types.CellType(c.cell_contents))
        except ValueError:
            cells.append(c)
    g = types.FunctionType(fn.__code__, fn.__globals__, fn.__name__, fn.__defaults__, tuple(cells))
    g.__kwdefaults__ = fn.__kwdefaults__
    return g


class Prog:
    def __init__(self, nc, same_engine_sync=True):
        self.nc = nc
        self.ins = []
        self.stack = contextlib.ExitStack()
        self.same_engine_sync = same_engine_sync
        self.tiles = []

    def tile(self, name, ap):
        t = T(name, ap)
        self.tiles.append(t)
        return t

    def new(self, name, shape, dtype, psum=False):
        if psum:
            h = self.stack.enter_context(self.nc.psum_tensor(name, list(shape), dtype))
        else:
            h = self.stack.enter_context(self.nc.sbuf_tensor(name, list(shape), dtype))
        return self.tile(name, h)

    def op(self, eng, fn, reads=(), writes=()):
        self.ins.append(Ins(eng, _freeze(fn), [x.t for x in reads], [x.t for x in writes]))

    def dma(self, eng, fns, reads=(), writes=(), dtile=None):
        if not isinstance(fns, (list, tuple)):
            fns = [fns]
        reads = [x.t for x in reads]
        writes = [x.t for x in writes]
        if dtile is None:
            dtile = (writes + reads)[0]
        self.ins.append(Ins(eng, [_freeze(f) for f in fns], reads, writes, True, dtile, len(fns)))

    def build(self):
        nc = self.nc
        ins = self.ins
        for idx, i in enumerate(ins):
            deps = set()
            for t in i.reads:
                if t.last_w is not None:
                    deps.add(t.last_w)
            for t in i.writes:
                if t.last_w is not None:
                    deps.add(t.last_w)
                deps.update(t.readers)
            for t in set(i.reads + i.writes):
                if t.excl:
                    for f_eng, f_idx in t.last_acc.items():
                        if f_eng != i.eng:
                            deps.add(f_idx)
                    t.last_acc[i.eng] = idx
            deps.discard(idx)
            for t in i.reads:
                if not i.is_dma:
                    t.readers = [r for r in t.readers if ins[r].is_dma or ins[r].eng != i.eng]
                if not t.readers or t.readers[-1] != idx:
                    t.readers.append(idx)
            for t in i.writes:
                t.last_w = idx
                t.readers = []
            keep = []
            for d in deps:
                p = ins[d]
                if (not p.is_dma) and (not i.is_dma) and p.eng == i.eng:
                    if p.eng == "pe" or not self.same_engine_sync:
                        continue
                keep.append(d)
            i.deps = keep
            for d in keep:
                ins[d].signal = True
        esem = {e: self.stack.enter_context(nc.semaphore("s_" + e)) for e in ENGS}
        ecnt = {e: 0 for e in ENGS}
        for i in ins:
            if i.is_dma:
                t = i.dtile
                kq = "sw" if i.eng == "pool" else "hw"
                if kq not in t.dsem:
                    t.dsem[kq] = self.stack.enter_context(nc.semaphore("d%s_%s" % (kq, t.name)))
                    t.dval[kq] = 0
                t.dval[kq] += 16 * i.ndma
                i.tok = (t.dsem[kq], t.dval[kq])
            elif i.signal:
                ecnt[i.eng] += 1
                i.tok = (esem[i.eng], ecnt[i.eng])
        progs = {e: [] for e in ENGS}
        waited = {e: {} for e in ENGS}
        nw = 0
        for i in ins:
            w = {}
            for d in i.deps:
                s, v = ins[d].tok
                k = id(s)
                if k not in w or w[k][1] < v:
                    w[k] = (s, v)
            for k, (s, v) in w.items():
                if waited[i.eng].get(k, 0) >= v:
                    continue
                waited[i.eng][k] = v
                progs[i.eng].append(("w", s, v))
                nw += 1
            progs[i.eng].append(("i", i))
        for t in self.tiles:
            for kq in t.dsem:
                progs["sp"].append(("w", t.dsem[kq], t.dval[kq]))
        self.stats = dict(n_ins=len(ins), n_waits=nw, ecnt=dict(ecnt))

        def run(name, e):
            for item in progs[name]:
                if item[0] == "w":
                    e.wait_ge(item[1], item[2])
                else:
                    i = item[1]
                    if i.is_dma:
                        for f in i.fn:
                            f(e).then_inc(i.tok[0], 16)
                    else:
                        r = i.fn(e)
                        if i.signal:
                            r.then_inc(i.tok[0], 1)

        with nc.Block() as block:
            @block.tensor
            def _(e):
                run("pe", e)

            @block.scalar
            def _(e):
                run("act", e)

            @block.vector
            def _(e):
                run("dve", e)

            @block.gpsimd
            def _(e):
                run("pool", e)

            @block.sync
            def _(e):
                run("sp", e)
        self.stack.close()


def _t5_bucket_np(rel):
    import jax
    with jax.default_device(jax.devices("cpu")[0]):
        return _t5_bucket_impl(rel)


def _t5_bucket_impl(rel):
    import jax.numpy as jnp
    import math
    rel = jnp.asarray(rel)
    half = 16
    max_exact = 8
    n = jnp.abs(rel)
    log_ratio = jnp.log(jnp.maximum(n, 1).astype(jnp.float32) / max_exact) / math.log(128 / max_exact)
    large = jnp.minimum(max_exact + (log_ratio * (half - max_exact)).astype(jnp.int32), half - 1)
    return np.asarray(jnp.where(rel > 0, half, 0) + jnp.where(n < max_exact, n, large))


C_EVEN = [(-512, 128), (-384, 128), (-256, 128), (-128, 128), (0, 64)]
C_ODD = [(-512, 64), (-448, 128), (-320, 128), (-192, 128), (-64, 128)]
B_EVEN = [(-128, 128), (0, 64)]
B_ODD = [(-128, 64), (-64, 128)]


def _rel_for(a, nk):
    j = np.arange(128)
    if nk == 64:
        j = j % 64
    q = np.arange(64)
    return a + j[:, None] - q[None, :]


def build_bias_tables(t5_bias, c_rel_bias):
    biasC = np.zeros((128, 2, 10, 4, 64), np.float32)
    for v, (a, nk) in enumerate(C_EVEN + C_ODD):
        idx = np.clip(_rel_for(a, nk), -256, 256) + 256
        for l in range(2):
            biasC[:, l, v] = np.transpose(c_rel_bias[l][idx], (0, 2, 1))
    biasB = np.zeros((128, 4, 2, 2, 64), np.float32)
    for v, (a, nk) in enumerate(B_EVEN + B_ODD):
        bk = _t5_bucket_np(_rel_for(a, nk))
        tb = t5_bias[bk]
        for g in range(2):
            for r in range(2):
                biasB[:, v, g, r] = tb[:, :, 2 * g + r]
    keep = [0, 2, 3, 4, 7, 8, 9]
    for v in (1, 5, 6):
        assert np.array_equal(biasC[:, :, v], biasC[:, :, 0])
    return np.ascontiguousarray(biasC[:, :, keep].reshape(128, 2, 7, 256)), biasB.reshape(128, 4, 256)


def build_program(n_pgroups):
    nc = bass.Bass("TRN2", target_bir_lowering=False)
    NTOK_P = n_pgroups * 512

    def din(name, shape, dt=F32):
        return nc.dram_tensor(name, list(shape), dt, kind="ExternalInput").ap()

    def dout(name, shape, dt=F32):
        return nc.dram_tensor(name, list(shape), dt, kind="ExternalOutput").ap()

    xTp = din("xTp", [128, 8, NTOK_P])
    xTs = din("xTs", [128, 8, 256])
    cT_d = din("cT", [128, 8, 5])
    st_d = din("st", [2, 4, 128, 2, 128])
    kbT_c = din("kbT_c", [2, 4, 128, 128])
    vb_c = din("vb_c", [2, 4, 128, 128])
    kcT_c = din("kcT_c", [2, 4, 128, 2, 512])
    vc_c = din("vc_c", [2, 4, 128, 4, 256])
    cbk = din("cbk", [2, 4, 128, 128])
    cbv = din("cbv", [2, 4, 128, 128])
    cck = din("cck", [2, 4, 512, 256])
    ccv = din("ccv", [2, 4, 512, 256])
    Wf = din("Wf", [2, 1024, 2048])
    Wt = din("Wt", [2, 1024, 1536])
    Wo = din("Wo", [2, 1024, 1024])
    Wu = din("Wu", [2, 1024, 4096])
    Wd = din("Wd", [2, 4096, 1024])
    Wa = din("Wa", [2, 1024, 6144])
    badaR = din("badaR", [128, 2, 48, 5])
    gmix_d = din("gmix", [128, 2, 8])
    gmlp_d = din("gmlp", [128, 2, 8])
    gfin_d = din("gfin", [128, 8])
    wa2_d = din("wa2", [32, 2, 256])
    anorm_d = din("anorm", [128, 2])
    sink_d = din("sinkT", [128, 2, 2])
    biasB_d = din("biasB", [128, 4, 256])
    biasC_d = din("biasC", [128, 2, 7, 256])
    lincl_d = din("lincl", [128, 128])
    lafter_d = din("lafter", [128, 128])
    mask_d = din("mask01", [128, 256])
    ident_d = din("ident", [128, 128])

    yTp = dout("yTp", [128, 8, NTOK_P])
    yTs = dout("yTs", [128, 8, 256])
    sgp = dout("sgp", [2, 128, 2, 128])
    kbp = dout("kbp", [2, 128, 128])
    vbp = dout("vbp", [2, 128, 128])
    kcp = dout("kcp", [2, 512, 256])
    vcp = dout("vcp", [2, 512, 256])
    sgs = dout("sgs", [2, 4, 128, 2, 128])
    kbs = dout("kbs", [2, 4, 128, 128])
    vbs = dout("vbs", [2, 4, 128, 128])
    kcs = dout("kcs", [2, 4, 512, 256])
    vcs = dout("vcs", [2, 4, 512, 256])

    P = Prog(nc)
    OUT = {n: P.tile(n, a) for n, a in [("yTp", yTp), ("yTs", yTs), ("sgp", sgp), ("kbp", kbp), ("vbp", vbp),
                                        ("kcp", kcp), ("vcp", vcp), ("sgs", sgs), ("kbs", kbs), ("vbs", vbs),
                                        ("kcs", kcs), ("vcs", vcs)]}

    def load_const(name, src, shape, dt=F32, eng="sp"):
        t = P.new(name, shape, dt)
        P.dma(eng, lambda e: e.dma_start(out=t[:], in_=src), writes=[t])
        return t

    cT = load_const("cT_sb", cT_d[:, :, :], [128, 8, 5])
    bada = load_const("bada_sb", badaR[:, :, :, :], [128, 2, 48, 5])
    gmix = load_const("gmix_sb", gmix_d[:, :, :], [128, 2, 8])
    gmlp = load_const("gmlp_sb", gmlp_d[:, :, :], [128, 2, 8])
    gfin = load_const("gfin_sb", gfin_d[:, :], [128, 8])
    wa2 = load_const("wa2_sb", wa2_d[:, :, :], [32, 2, 256])
    anorm = load_const("anorm_sb", anorm_d[:, :], [128, 2])
    sinkr = load_const("sink_sb", sink_d[:, :, :], [128, 2, 2])
    biasB = load_const("biasB_sb", biasB_d[:, :, :], [128, 4, 256], BF16, eng="pool")
    biasC = load_const("biasC_sb", biasC_d[:, :, :, :], [128, 2, 7, 256], BF16, eng="pool")
    ident = load_const("ident_sb", ident_d[:, :], [128, 128], BF16, eng="pool")
    CVMAP = {0: 0, 1: 0, 5: 0, 6: 0, 2: 1, 3: 2, 4: 3, 7: 4, 8: 5, 9: 6}
    lincl = load_const("lincl_sb", lincl_d[:, :], [128, 128])
    lafter = load_const("lafter_sb", lafter_d[:, :], [128, 128])
    mask01 = load_const("mask_sb", mask_d[:, :], [128, 256])

    stg_b = P.new("stg_b", [128, 256], F32)
    stg_c = P.new("stg_c", [128, 512], F32)
    ones_n = P.new("ones_n", [128, 128], BF16)
    ones_dv = P.new("ones_dv", [128, 128], BF16)
    ones_1 = P.new("ones_1", [128, 64], BF16)
    P.op("pool", lambda e: e.memset(ones_n[:], 1.0 / 1024.0), writes=[ones_n])
    P.op("pool", lambda e: e.memset(ones_dv[:], 1.0 / 128.0), writes=[ones_dv])
    P.op("pool", lambda e: e.memset(ones_1[:], 1.0), writes=[ones_1])
    sinke = P.new("sinke", [128, 2, 2], F32)
    P.op("act", lambda e: e.activation(out=sinke[:], in_=sinkr[:], func=AF.Exp), reads=[sinkr], writes=[sinke])

    banks = [P.stack.enter_context(nc.psum_tensor("bank%d" % i, [128, 512], F32)) for i in range(8)]
    bankT = [P.tile("bank%d" % i, banks[i]) for i in range(8)]
    for b in bankT:
        b.excl = True
    big = [bankT[0], bankT[1], bankT[2]]
    bigc = [0]

    def nextbig():
        b = big[bigc[0] % 3]
        bigc[0] += 1
        return b

    pz = V(bankT[4], banks[4][:, 0:256])
    pbrem = V(bankT[4], banks[4][:, 256:512])
    pbT = V(bankT[5], banks[5][:, 0:256])
    pat = V(bankT[5], banks[5][:, 256:512])
    po = V(bankT[6], banks[6][:, 0:256])
    pn = V(bankT[6], banks[6][:, 256:512])
    ss_bank = V(bankT[4], banks[4][:, 0:512])
    psc_c = V(bankT[3], banks[3][:, 0:256])
    pnd_c = V(bankT[3], banks[3][:, 256:512])
    psc_b = V(bankT[7], banks[7][:, 0:256])
    pnd_b = V(bankT[7], banks[7][:, 256:512])

    NB = int(os.environ.get('KERNEL_NB', '5'))
    wbuf = [P.new("wbuf%d" % i, [128, 8, 512], BF16) for i in range(NB)]
    specs = []

    def wspec_layer(l):
        s = []
        for p in range(4):
            s.append(("wf%d_%d" % (l, p), Wf[l].rearrange("(kc p) n -> p kc n", p=128)[:, :, p * 512:(p + 1) * 512]))
        for p in range(3):
            s.append(("wt%d_%d" % (l, p), Wt[l].rearrange("(kc p) n -> p kc n", p=128)[:, :, p * 512:(p + 1) * 512]))
        for p in range(2):
            s.append(("wo%d_%d" % (l, p), Wo[l].rearrange("(kc p) n -> p kc n", p=128)[:, :, p * 512:(p + 1) * 512]))
        for fb in range(8):
            s.append(("wu%d_%d" % (l, fb), Wu[l].rearrange("(kc p) n -> p kc n", p=128)[:, :, fb * 512:(fb + 1) * 512]))
            s.append(("wd%d_%d" % (l, fb),
                      Wd[l][fb * 512:(fb + 1) * 512, :].rearrange("(fc p) (hf n) -> p fc hf n", p=128, hf=2)))
        return s

    for l in range(2):
        for p in range(12):
            specs.append(("wa%d_%d" % (l, p), Wa[l].rearrange("(kc p) n -> p kc n", p=128)[:, :, p * 512:(p + 1) * 512]))
    n_groups = n_pgroups + 1
    for g in range(n_groups):
        for l in range(2):
            specs.extend(wspec_layer(l))
    Wbf = nc.dram_tensor("Wbf", [50, 128, 4096], BF16, kind="Internal").ap()
    scratch = {}
    sidx = 0
    for l in range(2):
        groups = {}
        order = []
        for name, src in wspec_layer(l):
            kind = name[:2] + str(l)
            if kind not in groups:
                groups[kind] = []
                order.append(kind)
            groups[kind].append((sidx, src))
            scratch[name] = (sidx, kind)
            sidx += 1
        for kind in order:
            kt = P.tile("wbf_" + kind, Wbf)
            fns = []
            for (ix, src) in groups[kind]:
                if len(src.shape) == 4:
                    fns.append(lambda e, ix=ix, src=src: e.dma_start(
                        out=Wbf[ix].rearrange("p (fc hf n) -> p fc hf n", fc=4, hf=2), in_=src))
                else:
                    fns.append(lambda e, ix=ix, src=src: e.dma_start(out=Wbf[ix].rearrange("p (a b) -> p a b", a=8), in_=src))
            for name in [n_ for n_, v in scratch.items() if v[1] == kind]:
                scratch[name] = (scratch[name][0], kt)
            groups[kind] = (kt, fns)
        scratch["__order%d" % l] = [groups[k] for k in order]
    wstate = dict(issued=0, used=0, precast=False)
    PREF = NB - 3

    def w_issue_upto(n):
        while wstate["issued"] < min(n, len(specs)):
            k = wstate["issued"]
            buf = wbuf[k % NB]
            src = specs[k][1]
            if not specs[k][0].startswith("wa"):
                if not wstate["precast"]:
                    wstate["precast"] = True
                    for l_ in range(2):
                        for (kt, fns) in scratch["__order%d" % l_]:
                            P.dma("pool", fns, writes=[kt])
                ix, kt = scratch[specs[k][0]]
                P.dma("sp", lambda e, buf=buf, ix=ix: e.dma_start(out=buf[:], in_=Wbf[ix].rearrange("p (a b) -> p a b", a=8)),
                      reads=[kt], writes=[buf])
                wstate["issued"] += 1
                continue
            if len(src.shape) == 4:
                P.dma("pool", lambda e, buf=buf, src=src: e.dma_start(
                    out=buf[:].rearrange("p (fc hf) n -> p fc hf n", hf=2), in_=src), writes=[buf])
            else:
                P.dma("pool", lambda e, buf=buf, src=src: e.dma_start(out=buf[:], in_=src), writes=[buf])
            wstate["issued"] += 1

    def wget(prefix):
        k = wstate["used"]
        assert specs[k][0].startswith(prefix), (specs[k][0], prefix)
        w_issue_upto(k + 1 + PREF)
        wstate["used"] += 1
        return wbuf[k % NB]

    ce = P.new("ce", [128, 8, 5], F32)
    csil = P.new("csil", [128, 8, 5], BF16)
    P.op("act", lambda e: e.activation(out=ce[:], in_=cT[:], func=AF.Exp, scale=-1.0), reads=[cT], writes=[ce])
    P.op("dve", lambda e: e.tensor_scalar(ce[:], ce[:], 1.0, None, op0=ALU.add), reads=[ce], writes=[ce])
    P.op("dve", lambda e: e.reciprocal(ce[:], ce[:]), reads=[ce], writes=[ce])
    P.op("dve", lambda e: e.tensor_tensor(out=csil[:], in0=cT[:], in1=ce[:], op=ALU.mult), reads=[cT, ce], writes=[csil])
    mod = P.new("mod", [128, 2, 48, 5], F32)
    for l in range(2):
        pm = nextbig()
        for p in range(12):
            wb = wget("wa%d_%d" % (l, p))
            for q in range(4):
                oc = p * 4 + q
                for kc in range(8):
                    P.op("pe", lambda e, wb=wb, q=q, kc=kc, oc=oc, pm=pm: e.matmul(
                        pm[:, oc * 5:(oc + 1) * 5], lhsT=wb[:, kc, q * 128:(q + 1) * 128], rhs=csil[:, kc, :],
                        start=(kc == 0), stop=(kc == 7)), reads=[wb, csil], writes=[pm])
        P.op("dve", lambda e, l=l, pm=pm: e.tensor_tensor(
            out=mod[:, l, :, :], in0=pm[:, 0:240].rearrange("p (a b) -> p a b", b=5), in1=bada[:, l, :, :], op=ALU.add),
            reads=[pm, bada], writes=[mod])
    Amix = P.new("Amix", [128, 2, 8, 5], F32)
    Amlp = P.new("Amlp", [128, 2, 8, 5], F32)
    for l in range(2):
        for c in range(8):
            P.op("dve", lambda e, l=l, c=c: e.tensor_scalar(Amix[:, l, c, :], mod[:, l, 8 + c, :], 1.0, gmix[:, l, c:c + 1],
                                                            op0=ALU.add, op1=ALU.mult), reads=[mod, gmix], writes=[Amix])
            P.op("dve", lambda e, l=l, c=c: e.tensor_scalar(Amlp[:, l, c, :], mod[:, l, 32 + c, :], 1.0, gmlp[:, l, c:c + 1],
                                                            op0=ALU.add, op1=ALU.mult), reads=[mod, gmlp], writes=[Amlp])

    STAGE = int(os.environ.get("KERNEL_STAGE", "99"))

    class StopBuild(Exception):
        pass

    def stage(k):
        if STAGE < k:
            raise StopBuild()

    def mod_ap(l, kind, c, s):
        return mod[:, l, kind * 8 + c, s:s + 1]

    NR = 8
    ring_kcT = [P.new("rkcT%d" % l, [128, 2, NR * 128], BF16) for l in range(2)]
    ring_vc = [P.new("rvc%d" % l, [128, NR, 256], BF16) for l in range(2)]
    ring_kbT = [P.new("rkbT%d" % l, [128, NR * 128], BF16) for l in range(2)]
    ring_vb = [P.new("rvb%d" % l, [128, NR, 128], BF16) for l in range(2)]
    S_p = [P.new("S_p%d" % l, [128, 2, 128], F32) for l in range(2)]
    Sbf_p = [P.new("Sbf_p%d" % l, [128, 2, 128], BF16) for l in range(2)]
    for l in range(2):
        P.op("pool", lambda e, l=l: e.memset(S_p[l][:], 0.0), writes=[S_p[l]])
        P.op("pool", lambda e, l=l: e.memset(Sbf_p[l][:], 0.0), writes=[Sbf_p[l]])
    S_s = [P.new("S_s0", [128, 2, 128], F32)] * 4
    Sbf_s = [P.new("Sbf_s0", [128, 2, 128], BF16)] * 4
    s_kcT = [P.new("s_kcT0", [128, 2, 512], BF16)] * 4
    s_vc = [P.new("s_vc0", [128, 4, 256], BF16)] * 4
    s_kbT = [P.new("s_kbT0", [128, 128], BF16)] * 4
    s_vb = [P.new("s_vb0", [128, 1, 128], BF16)] * 4
    own_kcT = P.new("own_kcT", [128, 2, 256], BF16)
    own_vc = P.new("own_vc", [128, 2, 256], BF16)
    own_kbT = P.new("own_kbT", [128, 256], BF16)
    own_vb = P.new("own_vb", [128, 2, 128], BF16)

    xT = P.new("xT", [128, 8, 512], F32)
    hT_h = P.stack.enter_context(nc.sbuf_tensor("hT", [128, 8, 512], BF16))
    hT = [P.tile("hT%d" % c, hT_h[:, c, :]) for c in range(8)]
    sq = [P.new("sq%d" % i, [128, 512], BF16) for i in range(4)]
    rstd = P.new("rstd", [128, 512], F32)
    tmpf = [P.new("tmpf%d" % i, [128, 512], F32) for i in range(2)]
    tmpc = [0]

    def nexttmp():
        t = tmpf[tmpc[0] % 2]
        tmpc[0] += 1
        return t

    qaT = P.new("qaT", [128, 2, 512], F32)
    kaT = P.new("kaT", [128, 2, 512], F32)
    gate = P.new("gate", [128, 4, 512], BF16)
    qbz = P.new("qbz", [128, 2, 2, 512], BF16)
    qcz = P.new("qcz", [128, 2, 2, 512], BF16)
    P.op("pool", lambda e: e.memset(qbz[:], 0.0), writes=[qbz])
    P.op("pool", lambda e: e.memset(qcz[:], 0.0), writes=[qcz])
    raT = P.new("raT", [32, 512], F32)
    P.op("pool", lambda e: e.memset(raT[:], 1.0), writes=[raT])
    mixT = P.new("mixT", [128, 8, 512], BF16)
    uT = [P.new("uT%d" % i, [128, 4, 512], BF16) for i in range(2)]
    ka_tok = P.new("ka_tok", [128, 256], F32)
    va_bf_l = [P.new("va_bf%d" % i, [128, 512], BF16) for i in range(2)]
    ez = P.new("ez", [128, 256], F32)
    sp_t = P.new("sp_t", [128, 256], F32)
    EbT_l = [P.new("EbT%d" % i, [128, 2, 128], F32) for i in range(2)]
    EnbT = P.new("EnbT", [128, 2, 128], F32)
    qtz_l = [P.new("qtz%d" % i, [128, 2, 2, 128], BF16) for i in range(2)]
    for i_ in range(2):
        P.op("pool", lambda e, i_=i_: e.memset(qtz_l[i_][:], 0.0), writes=[qtz_l[i_]])
    ktT_l = [P.new("ktT%d" % i, [128, 2, 128], BF16) for i in range(2)]
    Ebrem = P.new("Ebrem", [128, 256], F32)
    khat_z_l = [[P.new("khat_z%d_%d" % (pp_, i), [128, 256], BF16) for i in range(2)] for pp_ in range(2)]
    attn_z = [P.new("attn_z%d" % i, [128, 4, 64], BF16) for i in range(2)]
    for i_ in range(2):
        for pp_ in range(2):
            P.op("pool", lambda e, i_=i_, pp_=pp_: e.memset(khat_z_l[pp_][i_][:], 0.0), writes=[khat_z_l[pp_][i_]])
        P.op("pool", lambda e, i_=i_: e.memset(attn_z[i_][:], 0.0), writes=[attn_z[i_]])
    osq = P.new("osq", [128, 256], BF16)
    orstd = P.new("orstd", [128, 256], F32)
    o1 = P.new("o1", [128, 256], F32)
    pTc_full = [P.new("pTc_f%d" % i, [128, 256], BF16) for i in range(4)]
    pTc_half = {0: P.new("pTc_h0", [128, 256], BF16), 64: P.new("pTc_h64", [128, 256], BF16)}
    pTb_full = [P.new("pTb_f0", [128, 256], BF16)]
    pTb_half = {0: P.new("pTb_h0", [128, 256], BF16), 64: P.new("pTb_h64", [128, 256], BF16)}
    for t_ in (pTc_half[0], pTc_half[64], pTb_half[0], pTb_half[64]):
        P.op("pool", lambda e, t_=t_: e.memset(t_[:], 0.0), writes=[t_])
    sbc = [0]
    rden = P.new("rden", [128, 2, 64], F32)
    rden_b = P.new("rden_b", [128, 2, 64], F32)
    yst = [P.new("yst%d" % i, [128, 512], F32) for i in range(2)]

    def stats_chunk(c, TG):
        sqc = sq[c % 4]
        P.op("act", lambda e, c=c, sqc=sqc: e.activation(out=sqc[:, 0:TG], in_=xT[:, c, 0:TG], func=AF.Square),
             reads=[xT], writes=[sqc])

        def mm():
            P.op("pe", lambda e, c=c, sqc=sqc: e.matmul(ss_bank[:, 0:TG], lhsT=ones_n[:], rhs=sqc[:, 0:TG],
                                                         start=(c == 0), stop=(c == 7)), reads=[ones_n, sqc], writes=[ss_bank])
        return mm

    def stats_all(TG):
        for c in range(8):
            stats_chunk(c, TG)()

    def rstd_from_stats(TG):
        P.op("act", lambda e: e.activation(out=rstd[:, 0:TG], in_=ss_bank[:, 0:TG], func=AF.Ln, bias=EPS, scale=1.0),
             reads=[ss_bank], writes=[rstd])
        P.op("act", lambda e: e.activation(out=rstd[:, 0:TG], in_=rstd[:, 0:TG], func=AF.Exp, scale=-0.5),
             reads=[rstd], writes=[rstd])

    def norm_apply(TG, segs, Asel, Bsel, out_t):
        rstd_from_stats(TG)
        for c in range(8):
            t = nexttmp()
            P.op("dve", lambda e, c=c, t=t: e.tensor_tensor(out=t[:, 0:TG], in0=xT[:, c, 0:TG], in1=rstd[:, 0:TG], op=ALU.mult),
                 reads=[xT, rstd], writes=[t])
            for (c0, c1, s) in segs:
                a_ap, a_t = Asel(c, s)
                b_ap, b_t = Bsel(c, s)
                P.op("act", lambda e, c=c, t=t, c0=c0, c1=c1, a_ap=a_ap, b_ap=b_ap: e.activation(
                    out=out_t[c][:, c0:c1], in_=t[:, c0:c1], func=AF.Identity, bias=b_ap, scale=a_ap),
                    reads=[t, a_t, b_t], writes=[out_t[c]])

    def gla_chunk(l, i, ci, S, Sbf):
        base = 64 * ci
        cols = slice(i * 128 + base, i * 128 + base + 64)
        pp = i % 2
        va_bf, EbT, qtz, ktT, khat_z = va_bf_l[pp], EbT_l[pp], qtz_l[pp], ktT_l[pp], khat_z_l[pp]
        az = attn_z[ci]
        for h in range(4):
            j, r = h // 2, h % 2
            P.op("pe", lambda e, h=h, j=j, r=r: e.matmul(
                pat[base:base + 64, h * 64:(h + 1) * 64], lhsT=ktT[:, j, base:base + 64],
                rhs=qtz[:, j, r, base:base + 64], start=True, stop=True), reads=[ktT, qtz], writes=[pat])
        yield
        P.op("dve", lambda e: e.tensor_tensor(out=az[base:base + 64, :, :],
                                              in0=pat[base:base + 64, :].rearrange("p (h t) -> p h t", t=64),
                                              in1=mask01[base:base + 64, :].rearrange("p (h t) -> p h t", t=64), op=ALU.mult),
             reads=[pat, mask01], writes=[az])
        yield
        for h in range(4):
            j, r = h // 2, h % 2
            P.op("pe", lambda e, h=h: e.matmul(po[:, h * 64:(h + 1) * 64], lhsT=va_bf[:, h * 128:(h + 1) * 128],
                                               rhs=az[:, h, :], start=True, stop=False),
                 reads=[va_bf, az], writes=[po])
            P.op("pe", lambda e, h=h, j=j, r=r: e.matmul(po[:, h * 64:(h + 1) * 64], lhsT=Sbf[:, j, :],
                                                         rhs=qtz[:, j, r, base:base + 64], start=False, stop=True),
                 reads=[Sbf, qtz], writes=[po])
        pss = nextbig()
        kz = khat_z[ci]
        for h in range(4):
            j = h // 2
            P.op("pe", lambda e, h=h, j=j, pss=pss: e.matmul(pss[:, h * 128:(h + 1) * 128],
                                                             lhsT=kz[:, j * 128:(j + 1) * 128],
                                                             rhs=va_bf[:, h * 128:(h + 1) * 128], start=True, stop=True),
                 reads=[kz, va_bf], writes=[pss])
        yield
        P.op("act", lambda e: e.activation(out=osq[:], in_=po[:], func=AF.Square), reads=[po], writes=[osq])
        for h in range(4):
            j, r = h // 2, h % 2
            P.op("dve", lambda e, h=h, j=j, r=r, pss=pss: e.scalar_tensor_tensor(
                out=S[r * 64:(r + 1) * 64, j, :], in0=S[r * 64:(r + 1) * 64, j, :],
                scalar=EbT[r * 64:(r + 1) * 64, j, base + 63:base + 64], in1=pss[r * 64:(r + 1) * 64, h * 128:(h + 1) * 128],
                op0=ALU.mult, op1=ALU.add), reads=[S, EbT, pss, Sbf, po], writes=[S])
        P.op("pool", lambda e: e.tensor_copy(out=Sbf[:], in_=S[:]), reads=[S], writes=[Sbf])
        yield
        P.op("pe", lambda e: e.matmul(pn[:], lhsT=ones_dv[:], rhs=osq[:], start=True, stop=True), reads=[ones_dv, osq], writes=[pn])
        yield
        P.op("act", lambda e: e.activation(out=orstd[:], in_=pn[:], func=AF.Ln, bias=EPS, scale=1.0), reads=[pn], writes=[orstd])
        P.op("act", lambda e: e.activation(out=orstd[:], in_=orstd[:], func=AF.Exp, scale=-0.5), reads=[orstd], writes=[orstd])
        yield
        P.op("dve", lambda e: e.tensor_tensor(out=o1[:], in0=po[:], in1=orstd[:], op=ALU.mult), reads=[po, orstd], writes=[o1])
        P.op("dve", lambda e: e.scalar_tensor_tensor(
            out=mixT[:, 0:4, cols], in0=o1[:].rearrange("p (h t) -> p h t", t=64), scalar=anorm[:, l:l + 1],
            in1=gate[:, :, cols], op0=ALU.mult, op1=ALU.mult), reads=[o1, anorm, gate], writes=[mixT])

    def attn_c(l, cols, piecesC):
        pts = []
        nfull = 0
        for (kt_t, kfn, v_t, vfn, pb, nk, var) in piecesC:
            for h in range(4):
                j, r = h // 2, h % 2
                P.op("pe", lambda e, h=h, j=j, r=r, kfn=kfn, pb=pb, nk=nk: e.matmul(
                    psc_c[pb:pb + nk, h * 64:(h + 1) * 64], lhsT=kfn(j), rhs=qcz[:, j, r, cols], start=(h == 0), stop=False,
                    skip_group_check=True), reads=[kt_t, qcz], writes=[psc_c])
            P.op("pe", lambda e, pb=pb, nk=nk, var=var: e.matmul(
                psc_c[pb:pb + nk, 0:256], lhsT=ident[:, pb:pb + nk], rhs=biasC[:, l, CVMAP[var], :], start=False, stop=True,
                skip_group_check=True), reads=[ident, biasC], writes=[psc_c])
            yield
            if nk == 128:
                pT = pTc_full[nfull]
                nfull += 1
            else:
                pT = pTc_half[pb]
            P.op("act", lambda e, pT=pT, pb=pb, nk=nk: e.activation(out=pT[pb:pb + nk, :], in_=psc_c[pb:pb + nk, :], func=AF.Exp),
                 reads=[psc_c], writes=[pT])
            pts.append((pT, v_t, vfn))
        yield
        npc = len(pts)
        for h in range(4):
            j, r = h // 2, h % 2
            for pi, (pT, v_t, vfn) in enumerate(pts):
                P.op("pe", lambda e, h=h, j=j, r=r, vfn=vfn, pT=pT, pi=pi: e.matmul(
                    pnd_c[r * 64:(r + 1) * 64, j * 64:(j + 1) * 64], lhsT=vfn(h), rhs=pT[:, h * 64:(h + 1) * 64],
                    start=(pi == 0), stop=(pi == npc - 1)), reads=[v_t, pT], writes=[pnd_c])
            for pi, (pT, v_t, vfn) in enumerate(pts):
                P.op("pe", lambda e, h=h, j=j, r=r, pT=pT, pi=pi: e.matmul(
                    pnd_c[r * 64:(r + 1) * 64, 128 + j * 64:128 + (j + 1) * 64], lhsT=ones_1[:, :],
                    rhs=pT[:, h * 64:(h + 1) * 64], start=(pi == 0), stop=(pi == npc - 1)),
                    reads=[ones_1, pT], writes=[pnd_c])
            if h == 1:
                yield
        yield
        P.op("dve", lambda e: e.reciprocal(rden[:], pnd_c[:, 128:256].rearrange("p (j t) -> p j t", t=64)), reads=[pnd_c], writes=[rden])
        P.op("dve", lambda e: e.tensor_tensor(out=mixT[:, 6:8, cols], in0=pnd_c[:, 0:128].rearrange("p (j t) -> p j t", t=64),
                                              in1=rden[:], op=ALU.mult), reads=[pnd_c, rden], writes=[mixT])

    def attn_b(l, cols, piecesB):
        pts = []
        for (kt_t, kfn, v_t, vfn, pb, nk, var) in piecesB:
            for g in range(2):
                for r in range(2):
                    P.op("pe", lambda e, g=g, r=r, kfn=kfn, pb=pb, nk=nk: e.matmul(
                        psc_b[pb:pb + nk, g * 128 + r * 64:g * 128 + (r + 1) * 64], lhsT=kfn(None),
                        rhs=qbz[:, r, g, cols], start=(g == 0 and r == 0), stop=False, skip_group_check=True),
                        reads=[kt_t, qbz], writes=[psc_b])
            P.op("pe", lambda e, pb=pb, nk=nk, var=var: e.matmul(
                psc_b[pb:pb + nk, 0:256], lhsT=ident[:, pb:pb + nk], rhs=biasB[:, var, :], start=False, stop=True,
                skip_group_check=True), reads=[ident, biasB], writes=[psc_b])
            yield
            pT = pTb_full[0] if nk == 128 else pTb_half[pb]
            P.op("act", lambda e, pT=pT, pb=pb, nk=nk: e.activation(out=pT[pb:pb + nk, :], in_=psc_b[pb:pb + nk, :], func=AF.Exp),
                 reads=[psc_b], writes=[pT])
            pts.append((pT, v_t, vfn))
        yield
        npb = len(pts)
        for h in range(4):
            g, r = h // 2, h % 2
            for pi, (pT, v_t, vfn) in enumerate(pts):
                P.op("pe", lambda e, h=h, g=g, r=r, vfn=vfn, pT=pT, pi=pi: e.matmul(
                    pnd_b[r * 64:(r + 1) * 64, g * 64:(g + 1) * 64], lhsT=vfn(g),
                    rhs=pT[:, g * 128 + r * 64:g * 128 + (r + 1) * 64],
                    start=(pi == 0), stop=(pi == npb - 1)), reads=[v_t, pT], writes=[pnd_b])
            for pi, (pT, v_t, vfn) in enumerate(pts):
                P.op("pe", lambda e, h=h, g=g, r=r, pT=pT, pi=pi: e.matmul(
                    pnd_b[r * 64:(r + 1) * 64, 128 + g * 64:128 + (g + 1) * 64], lhsT=ones_1[:, :],
                    rhs=pT[:, g * 128 + r * 64:g * 128 + (r + 1) * 64],
                    start=(pi == 0), stop=(pi == npb - 1)), reads=[ones_1, pT], writes=[pnd_b])
        yield
        for j in range(2):
            P.op("dve", lambda e, j=j: e.tensor_scalar(rden_b[:, j, :], pnd_b[:, 128 + j * 64:128 + (j + 1) * 64], sinke[:, l, j:j + 1], None,
                                                       op0=ALU.add), reads=[pnd_b, sinke], writes=[rden_b])
        P.op("dve", lambda e: e.reciprocal(rden_b[:], rden_b[:]), reads=[rden_b], writes=[rden_b])
        P.op("dve", lambda e: e.tensor_tensor(out=mixT[:, 4:6, cols], in0=pnd_b[:, 0:128].rearrange("p (j t) -> p j t", t=64),
                                              in1=rden_b[:], op=ALU.mult), reads=[pnd_b, rden_b], writes=[mixT])

    def run_interleaved(gens):
        gens = list(gens)
        while gens:
            for g_ in list(gens):
                try:
                    next(g_)
                except StopIteration:
                    gens.remove(g_)

    def process_group(gi, is_sample):
        NT = 2 if is_sample else 4
        TG = NT * 128
        if is_sample:
            segs = [(s * 64, (s + 1) * 64, 1 + s) for s in range(4)]
            src = xTs[:, :, :]
        else:
            segs = [(0, TG, 0)]
            src = xTp[:, :, gi * 512:(gi + 1) * 512]
        t0 = gi * 4
        P.dma("act", lambda e: e.dma_start(out=xT[:, :, 0:TG], in_=src), writes=[xT])
        for l in range(2):
            if is_sample:
                for s in range(4):
                    P.dma("sp", lambda e, s=s: e.dma_start(out=kbs[l, s, 0:64, :], in_=cbk[l, s, 64:128, :]), writes=[OUT["kbs"]])
                    P.dma("sp", lambda e, s=s: e.dma_start(out=vbs[l, s, 0:64, :], in_=cbv[l, s, 64:128, :]), writes=[OUT["vbs"]])
                    P.dma("sp", lambda e, s=s: e.dma_start(out=kcs[l, s, 0:448, :], in_=cck[l, s, 64:512, :]), writes=[OUT["kcs"]])
                    P.dma("sp", lambda e, s=s: e.dma_start(out=vcs[l, s, 0:448, :], in_=ccv[l, s, 64:512, :]), writes=[OUT["vcs"]])
            if l == 0:
                stats_all(TG)
            norm_apply(TG, segs,
                       lambda c, s: (Amix[:, l, c, s:s + 1], Amix),
                       lambda c, s: (mod_ap(l, 0, c, s), mod), hT)
            stage(2)
            if is_sample:
                rcol0 = 0
                kcT_dst, kbT_dst = own_kcT, own_kbT
            else:
                rcol0 = (t0 % NR) * 128
                kcT_dst, kbT_dst = ring_kcT[l], ring_kbT[l]
            wb0 = wget("wf%d_%d" % (l, 0))
            pre_pf = {}
            for q in range(3):
                pre_pf[q] = nextbig()
            for kc in range(8):
                for q in range(3):
                    pf = pre_pf[q]
                    P.op("pe", lambda e, wb0=wb0, q=q, kc=kc, pf=pf: e.matmul(
                        pf[:, 0:TG], lhsT=wb0[:, kc, q * 128:(q + 1) * 128], rhs=hT[kc][:, 0:TG],
                        start=(kc == 0), stop=(kc == 7)), reads=[wb0, hT[kc]], writes=[pf])
            for p in range(4):
                wb = wb0 if p == 0 else wget("wf%d_%d" % (l, p))
                for q in range(4):
                    cc = p * 4 + q
                    M = 16 if cc == 15 else 128
                    if cc in pre_pf:
                        pf = pre_pf[cc]
                    else:
                        pf = nextbig()
                        for kc in range(8):
                            P.op("pe", lambda e, wb=wb, q=q, kc=kc, pf=pf, M=M: e.matmul(
                                pf[0:M, 0:TG], lhsT=wb[:, kc, q * 128:q * 128 + M], rhs=hT[kc][:, 0:TG],
                                start=(kc == 0), stop=(kc == 7)), reads=[wb, hT[kc]], writes=[pf])
                    if os.environ.get("KERNEL_SKIPEVAC") == "1":
                        continue
                    if cc in (0, 1):
                        P.op("act", lambda e, cc=cc, pf=pf: e.copy(qaT[:, cc, 0:TG], pf[:, 0:TG]),
                             reads=[pf], writes=[qaT])
                    elif cc in (2, 3):
                        P.op("dve", lambda e, cc=cc, pf=pf: e.tensor_copy(out=kaT[:, cc - 2, 0:TG], in_=pf[:, 0:TG]),
                             reads=[pf], writes=[kaT])
                    elif cc in (4, 5, 6, 7):
                        t = nexttmp()
                        P.op("act", lambda e, pf=pf, t=t: e.activation(out=t[:, 0:TG], in_=pf[:, 0:TG], func=AF.Exp, scale=-1.0),
                             reads=[pf], writes=[t])
                        P.op("dve", lambda e, t=t: e.tensor_scalar(t[:, 0:TG], t[:, 0:TG], 1.0, None, op0=ALU.add), reads=[t], writes=[t])
                        P.op("dve", lambda e, t=t: e.reciprocal(t[:, 0:TG], t[:, 0:TG]), reads=[t], writes=[t])
                        P.op("dve", lambda e, cc=cc, pf=pf, t=t: e.tensor_tensor(out=gate[:, cc - 4, 0:TG], in0=pf[:, 0:TG], in1=t[:, 0:TG],
                                                                                  op=ALU.mult), reads=[pf, t], writes=[gate])
                    elif cc in (8, 9):
                        for g_ in range(2):
                            P.op("act", lambda e, cc=cc, pf=pf, g_=g_: e.mul(qbz[g_ * 64:(g_ + 1) * 64, cc - 8, g_, 0:TG],
                                                                             pf[g_ * 64:(g_ + 1) * 64, 0:TG], 0.125),
                                 reads=[pf], writes=[qbz])
                    elif cc == 10:
                        P.op("act", lambda e, pf=pf: e.copy(kbT_dst[:, rcol0:rcol0 + TG], pf[:, 0:TG]),
                             reads=[pf], writes=[kbT_dst])
                    elif cc in (11, 12):
                        for r_ in range(2):
                            P.op("act", lambda e, cc=cc, pf=pf, r_=r_: e.mul(qcz[r_ * 64:(r_ + 1) * 64, cc - 11, r_, 0:TG],
                                                                             pf[r_ * 64:(r_ + 1) * 64, 0:TG], 0.125),
                                 reads=[pf], writes=[qcz])
                    elif cc in (13, 14):
                        P.op("dve", lambda e, cc=cc, pf=pf: e.tensor_copy(out=kcT_dst[:, cc - 13, rcol0:rcol0 + TG], in_=pf[:, 0:TG]),
                             reads=[pf], writes=[kcT_dst])
                    else:
                        P.op("dve", lambda e, pf=pf: e.tensor_copy(out=raT[0:16, 0:TG], in_=pf[0:16, 0:TG]), reads=[pf], writes=[raT])
            stage(3)
            wts = [wget("wt%d_%d" % (l, p)) for p in range(3)]
            last_group = (not is_sample) and gi == n_pgroups - 1

            def tile_prep(i):
                pp = i % 2
                va_bf, EbT, qtz, ktT, khat_z = va_bf_l[pp], EbT_l[pp], qtz_l[pp], ktT_l[pp], khat_z_l[pp]
                tcols = slice(i * 128, (i + 1) * 128)
                slot = (t0 + i) % NR
                pbank = []
                for p in range(3):
                    pk = nextbig()
                    for kc in range(8):
                        P.op("pe", lambda e, p=p, kc=kc, pk=pk: e.matmul(pk[:, :], lhsT=hT[kc][:, tcols], rhs=wts[p][:, kc, :],
                                                                         start=(kc == 0), stop=(kc == 7)), reads=[hT[kc], wts[p]], writes=[pk])
                    pbank.append(pk)
                pA, pB, pC = pbank
                P.op("dve", lambda e, pA=pA: e.tensor_copy(out=ka_tok[:], in_=pA[:, 0:256]), reads=[pA], writes=[ka_tok])
                P.op("act", lambda e, pB=pB: e.copy(va_bf[:], pB[:, :]), reads=[pB], writes=[va_bf])
                if is_sample:
                    vb_dst, vb_ap = own_vb, own_vb[:, i, :]
                    vc_dst, vc_ap = own_vc, own_vc[:, i, :]
                else:
                    vb_dst, vb_ap = ring_vb[l], ring_vb[l][:, slot, :]
                    vc_dst, vc_ap = ring_vc[l], ring_vc[l][:, slot, :]
                P.op("act", lambda e, pA=pA, vb_ap=vb_ap: e.copy(vb_ap, pA[:, 384:512]), reads=[pA], writes=[vb_dst])
                P.op("act", lambda e, pC=pC, vc_ap=vc_ap: e.copy(vc_ap, pC[:, 256:512]), reads=[pC], writes=[vc_dst])
                last_group = (not is_sample) and gi == n_pgroups - 1
                NOOUT = os.environ.get("KERNEL_NOOUT") == "1"
                OUTM = os.environ.get("KERNEL_OUTM", "CcBb")
                if (is_sample or last_group) and not NOOUT:
                    if "C" in OUTM:
                        P.op("dve", lambda e, pC=pC: e.tensor_copy(out=stg_c[:, 0:256], in_=pC[:, 0:256]), reads=[pC], writes=[stg_c])
                        P.op("act", lambda e, pC=pC: e.copy(stg_c[:, 256:512], pC[:, 256:512]), reads=[pC], writes=[stg_c])
                    if "c" not in OUTM:
                        pass
                    elif is_sample:
                        for ci in range(2):
                            s = 2 * i + ci
                            b0 = 64 * ci
                            P.dma(os.environ.get("KERNEL_OUTQ", "sp"), lambda e, s=s, b0=b0: e.dma_start(out=kcs[l, s, 448:512, :], in_=stg_c[b0:b0 + 64, 0:256]),
                                  reads=[stg_c], writes=[OUT["kcs"]])
                            P.dma(os.environ.get("KERNEL_OUTQ", "sp"), lambda e, s=s, b0=b0: e.dma_start(out=vcs[l, s, 448:512, :], in_=stg_c[b0:b0 + 64, 256:512]),
                                  reads=[stg_c], writes=[OUT["vcs"]])
                    else:
                        P.dma("sp", lambda e, i=i: e.dma_start(out=kcp[l, i * 128:(i + 1) * 128, :], in_=stg_c[:, 0:256]),
                              reads=[stg_c], writes=[OUT["kcp"]])
                        P.dma("sp", lambda e, i=i: e.dma_start(out=vcp[l, i * 128:(i + 1) * 128, :], in_=stg_c[:, 256:512]),
                              reads=[stg_c], writes=[OUT["vcp"]])
                if (is_sample or (last_group and i == NT - 1)) and not NOOUT:
                    if "B" in OUTM:
                        P.op("dve", lambda e, pA=pA: e.tensor_copy(out=stg_b[:, 0:128], in_=pA[:, 256:384]), reads=[pA], writes=[stg_b])
                        P.op("act", lambda e, pA=pA: e.copy(stg_b[:, 128:256], pA[:, 384:512]), reads=[pA], writes=[stg_b])
                    if "b" not in OUTM:
                        pass
                    elif is_sample:
                        for ci in range(2):
                            s = 2 * i + ci
                            b0 = 64 * ci
                            P.dma(os.environ.get("KERNEL_OUTQ", "sp"), lambda e, s=s, b0=b0: e.dma_start(out=kbs[l, s, 64:128, :], in_=stg_b[b0:b0 + 64, 0:128]),
                                  reads=[stg_b], writes=[OUT["kbs"]])
                            P.dma(os.environ.get("KERNEL_OUTQ", "sp"), lambda e, s=s, b0=b0: e.dma_start(out=vbs[l, s, 64:128, :], in_=stg_b[b0:b0 + 64, 128:256]),
                                  reads=[stg_b], writes=[OUT["vbs"]])
                    else:
                        P.dma("sp", lambda e: e.dma_start(out=kbp[l, :, :], in_=stg_b[:, 0:128]), reads=[stg_b], writes=[OUT["kbp"]])
                        P.dma("sp", lambda e: e.dma_start(out=vbp[l, :, :], in_=stg_b[:, 128:256]), reads=[stg_b], writes=[OUT["vbp"]])
                stage(4)
                P.op("pe", lambda e: e.matmul(pz[:], lhsT=raT[0:32, tcols], rhs=wa2[:, l, :], start=True, stop=True),
                     reads=[raT, wa2], writes=[pz])
                P.op("act", lambda e: e.activation(out=ez[:], in_=pz[:], func=AF.Exp, scale=-1.0), reads=[pz], writes=[ez])
                P.op("act", lambda e: e.activation(out=sp_t[:], in_=ez[:], func=AF.Ln, bias=1.0, scale=1.0), reads=[ez], writes=[sp_t])
                for j in range(2):
                    P.op("pe", lambda e, j=j: e.matmul(pbT[:, j * 128:(j + 1) * 128], lhsT=sp_t[:, j * 128:(j + 1) * 128], rhs=lincl[:],
                                                       start=True, stop=True), reads=[sp_t, lincl], writes=[pbT])
                P.op("pe", lambda e: e.matmul(pbrem[:], lhsT=lafter[:], rhs=sp_t[:], start=True, stop=True),
                     reads=[lafter, sp_t], writes=[pbrem])
                P.op("act", lambda e: e.activation(out=EbT[:], in_=pbT[:].rearrange("p (j t) -> p j t", t=128), func=AF.Exp),
                     reads=[pbT], writes=[EbT])
                P.op("act", lambda e: e.activation(out=EnbT[:], in_=pbT[:].rearrange("p (j t) -> p j t", t=128), func=AF.Exp, scale=-1.0),
                     reads=[pbT], writes=[EnbT])
                P.op("act", lambda e: e.activation(out=Ebrem[:], in_=pbrem[:], func=AF.Exp), reads=[pbrem], writes=[Ebrem])
                for r_ in range(2):
                    P.op("dve", lambda e, r_=r_: e.scalar_tensor_tensor(
                        out=qtz[r_ * 64:(r_ + 1) * 64, :, r_, :], in0=qaT[r_ * 64:(r_ + 1) * 64, :, tcols], scalar=0.125,
                        in1=EbT[r_ * 64:(r_ + 1) * 64, :, :], op0=ALU.mult, op1=ALU.mult), reads=[qaT, EbT], writes=[qtz])
                P.op("dve", lambda e: e.tensor_tensor(out=ktT[:], in0=kaT[:, :, tcols], in1=EnbT[:], op=ALU.mult),
                     reads=[kaT, EnbT], writes=[ktT])
                for c_ in range(2):
                    P.op("dve", lambda e, c_=c_: e.tensor_tensor(out=khat_z[c_][c_ * 64:(c_ + 1) * 64, :], in0=ka_tok[c_ * 64:(c_ + 1) * 64, :],
                                                                 in1=Ebrem[c_ * 64:(c_ + 1) * 64, :], op=ALU.mult),
                         reads=[ka_tok, Ebrem], writes=[khat_z[c_]])

            def tile_chunks(i):
                stage(5)
                for ci in range(2):
                    cols = slice(i * 128 + 64 * ci, i * 128 + 64 * ci + 64)
                    if is_sample:
                        s = 2 * i + ci
                        S, Sbf = S_s[s], Sbf_s[s]
                        P.dma("sp", lambda e, s=s: e.dma_start(out=S_s[s][:], in_=st_d[l, s, :, :, :]), writes=[S_s[s]])
                        P.op("pool", lambda e, s=s: e.tensor_copy(out=Sbf_s[s][:], in_=S_s[s][:]), reads=[S_s[s]], writes=[Sbf_s[s]])
                        P.dma("pool", lambda e, s=s: e.dma_start(out=s_kcT[s][:], in_=kcT_c[l, s, :, :, :]), writes=[s_kcT[s]])
                        P.dma("pool", lambda e, s=s: e.dma_start(out=s_vc[s][:], in_=vc_c[l, s, :, :, :]), writes=[s_vc[s]])
                        P.dma("pool", lambda e, s=s: e.dma_start(out=s_kbT[s][:], in_=kbT_c[l, s, :, :]), writes=[s_kbT[s]])
                        P.dma("pool", lambda e, s=s: e.dma_start(out=s_vb[s][:, 0, :], in_=vb_c[l, s, :, :]), writes=[s_vb[s]])
                    else:
                        S, Sbf = S_p[l], Sbf_p[l]
                    gen_gla = gla_chunk(l, i, ci, S, Sbf)
                    stage(6)
                    pb_own = 64 * ci
                    if is_sample:
                        pC_l = []
                        for k4 in range(4):
                            pC_l.append((s_kcT[s], (lambda j, k4=k4, s=s: s_kcT[s][:, j, k4 * 128:(k4 + 1) * 128]),
                                         s_vc[s], (lambda h, k4=k4, s=s: s_vc[s][:, k4, h * 64:(h + 1) * 64]), 0, 128, k4))
                        pC_l.append((own_kcT, (lambda j, i=i, pb_own=pb_own: own_kcT[:, j, i * 128 + pb_own:i * 128 + pb_own + 64]),
                                     own_vc, (lambda h, i=i: own_vc[:, i, h * 64:(h + 1) * 64]), pb_own, 64, 4))
                        pB_l = [(s_kbT[s], (lambda j, s=s: s_kbT[s][:, 0:128]),
                                 s_vb[s], (lambda g, s=s: s_vb[s][:, 0, g * 64:(g + 1) * 64]), 0, 128, 0),
                                (own_kbT, (lambda j, i=i, pb_own=pb_own: own_kbT[:, i * 128 + pb_own:i * 128 + pb_own + 64]),
                                 own_vb, (lambda g, i=i: own_vb[:, i, g * 64:(g + 1) * 64]), pb_own, 64, 1)]
                    else:
                        t = t0 + i
                        rk, rv, rkb, rvb = ring_kcT[l], ring_vc[l], ring_kbT[l], ring_vb[l]

                        def mkC(tk, pb, nk, var):
                            sl = tk % NR
                            return (rk, (lambda j, sl=sl, pb=pb, nk=nk, rk=rk: rk[:, j, sl * 128 + pb:sl * 128 + pb + nk]),
                                    rv, (lambda h, sl=sl, rv=rv: rv[:, sl, h * 64:(h + 1) * 64]), pb, nk, var)

                        def mkB(tk, pb, nk, var):
                            sl = tk % NR
                            return (rkb, (lambda j, sl=sl, pb=pb, nk=nk, rkb=rkb: rkb[:, sl * 128 + pb:sl * 128 + pb + nk]),
                                    rvb, (lambda g, sl=sl, rvb=rvb: rvb[:, sl, g * 64:(g + 1) * 64]), pb, nk, var)
                        pC_l, pB_l = [], []
                        if ci == 0:
                            for k4 in range(4):
                                tk = t - 4 + k4
                                if tk >= 0:
                                    pC_l.append(mkC(tk, 0, 128, k4))
                            pC_l.append(mkC(t, 0, 64, 4))
                            if t - 1 >= 0:
                                pB_l.append(mkB(t - 1, 0, 128, 0))
                            pB_l.append(mkB(t, 0, 64, 1))
                        else:
                            if t - 4 >= 0:
                                pC_l.append(mkC(t - 4, 64, 64, 5))
                            for k4 in range(4):
                                tk = t - 3 + k4
                                if tk >= 0:
                                    pC_l.append(mkC(tk, 0, 128, 6 + k4))
                            if t - 1 >= 0:
                                pB_l.append(mkB(t - 1, 64, 64, 2))
                            pB_l.append(mkB(t, 0, 128, 3))
                    run_interleaved([gen_gla, attn_c(l, cols, pC_l), attn_b(l, cols, pB_l)])
                    if is_sample:
                        P.dma("sp", lambda e, s=s: e.dma_start(out=sgs[l, s, :, :, :], in_=S_s[s][:]), reads=[S_s[s]], writes=[OUT["sgs"]])
                    elif last_group and i == NT - 1 and ci == 1:
                        P.dma("sp", lambda e: e.dma_start(out=sgp[l, :, :, :], in_=S_p[l][:]), reads=[S_p[l]], writes=[OUT["sgp"]])

            tile_prep(0)
            for i in range(NT):
                if i + 1 < NT:
                    tile_prep(i + 1)
                tile_chunks(i)
            stage(7)
            wos = [wget("wo%d_%d" % (l, p)) for p in range(2)]
            pend = []
            for oc in range(8):
                wb = wos[oc // 4]
                q = oc % 4
                pf = nextbig()
                for kc in range(8):
                    P.op("pe", lambda e, wb=wb, q=q, kc=kc, pf=pf: e.matmul(pf[:, 0:TG], lhsT=wb[:, kc, q * 128:(q + 1) * 128], rhs=mixT[:, kc, 0:TG],
                                                                            start=(kc == 0), stop=(kc == 7)), reads=[wb, mixT], writes=[pf])
                for (c0, c1, s) in segs:
                    P.op("dve", lambda e, oc=oc, pf=pf, c0=c0, c1=c1, s=s: e.scalar_tensor_tensor(
                        out=xT[:, oc, c0:c1], in0=pf[:, c0:c1], scalar=mod_ap(l, 2, oc, s), in1=xT[:, oc, c0:c1],
                        op0=ALU.mult, op1=ALU.add), reads=[pf, mod, xT], writes=[xT])
                pend.append(stats_chunk(oc, TG))
                if len(pend) > 2:
                    pend.pop(0)()
            while pend:
                pend.pop(0)()
            stage(8)
            norm_apply(TG, segs,
                       lambda c, s: (Amlp[:, l, c, s:s + 1], Amlp),
                       lambda c, s: (mod_ap(l, 3, c, s), mod), hT)
            def up_block(fb, wu, u):
                pre = {}
                if fb == 0:
                    for fc in range(3):
                        pre[fc] = nextbig()
                    for kc in range(8):
                        for fc in range(3):
                            pf = pre[fc]
                            P.op("pe", lambda e, wu=wu, fc=fc, kc=kc, pf=pf: e.matmul(pf[:, 0:TG], lhsT=wu[:, kc, fc * 128:(fc + 1) * 128], rhs=hT[kc][:, 0:TG],
                                                                                      start=(kc == 0), stop=(kc == 7)), reads=[wu, hT[kc]], writes=[pf])
                for fc in range(4):
                    if fc in pre:
                        pf = pre[fc]
                    else:
                        pf = nextbig()
                        for kc in range(8):
                            P.op("pe", lambda e, wu=wu, fc=fc, kc=kc, pf=pf: e.matmul(pf[:, 0:TG], lhsT=wu[:, kc, fc * 128:(fc + 1) * 128], rhs=hT[kc][:, 0:TG],
                                                                                      start=(kc == 0), stop=(kc == 7)), reads=[wu, hT[kc]], writes=[pf])
                    t = nexttmp()
                    P.op("act", lambda e, pf=pf, t=t: e.activation(out=t[:, 0:TG], in_=pf[:, 0:TG], func=AF.Relu), reads=[pf], writes=[t])
                    P.op("pool", lambda e, u=u, fc=fc, t=t: e.tensor_tensor(out=u[:, fc, 0:TG], in0=t[:, 0:TG], in1=t[:, 0:TG], op=ALU.mult),
                         reads=[t], writes=[u])

            def down_block(fb, wd, u):
                for oc in range(8):
                    hf, q = oc // 4, oc % 4
                    pf = nextbig()
                    for fc in range(4):
                        P.op("pe", lambda e, wd=wd, fc=fc, hf=hf, q=q, pf=pf, u=u: e.matmul(
                            pf[:, 0:TG], lhsT=wd[:, fc * 2 + hf, q * 128:(q + 1) * 128], rhs=u[:, fc, 0:TG],
                            start=(fc == 0), stop=(fc == 3)), reads=[wd, u], writes=[pf])
                    for (c0, c1, s) in segs:
                        P.op("dve", lambda e, oc=oc, pf=pf, c0=c0, c1=c1, s=s: e.scalar_tensor_tensor(
                            out=xT[:, oc, c0:c1], in0=pf[:, c0:c1], scalar=mod_ap(l, 5, oc, s), in1=xT[:, oc, c0:c1],
                            op0=ALU.mult, op1=ALU.add), reads=[pf, mod, xT], writes=[xT])
                    if fb == 7:
                        pend.append(stats_chunk(oc, TG))
                        if len(pend) > 2:
                            pend.pop(0)()
                while fb == 7 and pend:
                    pend.pop(0)()

            wu_cur = wget("wu%d_%d" % (l, 0))
            up_block(0, wu_cur, uT[0])
            for fb in range(8):
                wd_cur = wget("wd%d_%d" % (l, fb))
                if fb + 1 < 8:
                    wu_nxt = wget("wu%d_%d" % (l, fb + 1))
                    up_block(fb + 1, wu_nxt, uT[(fb + 1) % 2])
                down_block(fb, wd_cur, uT[fb % 2])
        rstd_from_stats(TG)
        for c in range(8):
            yc = yst[c % 2]
            P.op("dve", lambda e, c=c, yc=yc: e.scalar_tensor_tensor(out=yc[:, 0:TG], in0=xT[:, c, 0:TG], scalar=gfin[:, c:c + 1],
                                                                     in1=rstd[:, 0:TG], op0=ALU.mult, op1=ALU.mult),
                 reads=[xT, gfin, rstd], writes=[yc])
            if is_sample:
                P.dma("act", lambda e, c=c, yc=yc: e.dma_start(out=yTs[:, c, :], in_=yc[:, 0:TG]), reads=[yc], writes=[OUT["yTs"]])
            else:
                P.dma("act", lambda e, c=c, yc=yc: e.dma_start(out=yTp[:, c, gi * 512:(gi + 1) * 512], in_=yc[:, 0:TG]),
                      reads=[yc], writes=[OUT["yTp"]])

    try:
        stage(1)
        process_group(0, True)
        for gi in range(n_pgroups):
            process_group(gi, False)
        assert wstate["used"] == len(specs), (wstate, len(specs))
    except StopBuild:
        pass
    P.build()
    return nc, P.stats


def _feat_major(x2d):
    T_ = x2d.shape[0]
    return np.ascontiguousarray(x2d.reshape(T_, 8, 128).transpose(2, 1, 0))


def _from_feat_major(yT):
    T_ = yT.shape[2]
    return np.ascontiguousarray(yT.transpose(2, 1, 0).reshape(T_, 1024))


def _vec_fm(v):
    lead = v.shape[:-1]
    a = v.reshape(*lead, 8, 128)
    a = np.moveaxis(a, -1, 0)
    return np.ascontiguousarray(a)


_N_PGROUPS = int(os.environ.get("KERNEL_NPG", "32"))


def kernel(x_prompt, x_sample, c_prompt, c_sample, state_gla, cache_b_k, cache_b_v, cache_c_k, cache_c_v,
           w_ada, b_ada, norm_mix_g, norm_mlp_g, w_in, w_a2, b_a2, a_norm_g, b_sink, t5_bias, c_rel_bias,
           w_out, w_up, w_down, final_norm_g):
    f = lambda a: np.asarray(a, dtype=np.float32)
    x_prompt, x_sample, c_prompt, c_sample = f(x_prompt), f(x_sample), f(c_prompt), f(c_sample)
    state_gla, cache_b_k, cache_b_v, cache_c_k, cache_c_v = f(state_gla), f(cache_b_k), f(cache_b_v), f(cache_c_k), f(cache_c_v)
    w_ada, b_ada, norm_mix_g, norm_mlp_g, w_in, w_a2, b_a2 = f(w_ada), f(b_ada), f(norm_mix_g), f(norm_mlp_g), f(w_in), f(w_a2), f(b_a2)
    a_norm_g, b_sink, t5_bias, c_rel_bias, w_out, w_up, w_down, final_norm_g = (
        f(a_norm_g), f(b_sink), f(t5_bias), f(c_rel_bias), f(w_out), f(w_up), f(w_down), f(final_norm_g))
    npg = _N_PGROUPS
    nc, stats = build_program(npg)

    o = np.cumsum([0, 256, 256, 512, 512, 16, 256, 128, 128, 256, 256, 256])
    qa, ka, va, ga, ra, qb, kb, vb, qc, kc, vc = [np.arange(o[i], o[i + 1]) for i in range(11)]
    qb_re = np.concatenate([qb[0:64], qb[128:192], qb[64:128], qb[192:256]])
    feat_cols = np.concatenate([qa, ka, ga, qb_re, kb, qc, kc, ra])
    Wf = np.zeros((2, 1024, 2048), np.float32)
    Wf[:, :, :feat_cols.size] = w_in[:, :, feat_cols]
    tok_cols = np.concatenate([ka, kb, vb, va, kc, vc])
    Wt = np.ascontiguousarray(w_in[:, :, tok_cols])
    wa2 = np.zeros((32, 2, 256), np.float32)
    wa2[0:16] = w_a2.transpose(1, 0, 2)
    wa2[16] = b_a2
    badaR = np.ascontiguousarray(np.broadcast_to(
        b_ada.reshape(2, 48, 128).transpose(2, 0, 1)[:, :, :, None], (128, 2, 48, 5)))
    gmix = _vec_fm(norm_mix_g)
    gmlp = _vec_fm(norm_mlp_g)
    gfin = _vec_fm(final_norm_g)
    anorm = np.ascontiguousarray(a_norm_g.T)
    sinkT = np.zeros((128, 2, 2), np.float32)
    for l in range(2):
        for j in range(2):
            sinkT[0:64, l, j] = b_sink[l, 2 * j]
            sinkT[64:128, l, j] = b_sink[l, 2 * j + 1]
    biasC, biasB = build_bias_tables(t5_bias, c_rel_bias)
    sidx = np.arange(128)
    same = (sidx[:, None] // 64) == (sidx[None, :] // 64)
    lincl = np.where(same & (sidx[:, None] <= sidx[None, :]), -1.0 / 16.0, 0.0).astype(np.float32)
    lafter = np.where(same & (sidx[:, None] > sidx[None, :]), -1.0 / 16.0, 0.0).astype(np.float32)
    m = ((sidx[:, None] % 64) <= np.arange(64)[None, :]).astype(np.float32)
    mask01 = np.ascontiguousarray(np.broadcast_to(m[:, None, :], (128, 4, 64)).reshape(128, 256))

    xTp = _feat_major(x_prompt[0, :npg * 512])
    cache_k_b2 = cache_b_k.reshape(2, 32, 128, 128)
    cache_v_b2 = cache_b_v.reshape(2, 32, 128, 128)
    cache_k_c2 = cache_c_k.reshape(2, 32, 512, 256)
    cache_v_c2 = cache_c_v.reshape(2, 32, 512, 256)
    shared = dict(xTp=xTp, Wf=Wf, Wt=Wt, Wo=np.ascontiguousarray(w_out), Wu=np.ascontiguousarray(w_up), Wd=np.ascontiguousarray(w_down),
                  Wa=np.ascontiguousarray(w_ada), badaR=badaR, gmix=gmix, gmlp=gmlp, gfin=gfin, wa2=wa2, anorm=anorm,
                  sinkT=sinkT, biasB=biasB, biasC=biasC, lincl=lincl, lafter=lafter, mask01=mask01,
                  ident=np.eye(128, dtype=np.float32))
    in_maps = []
    for c in range(NCORES):
        bs = slice(c * 4, (c + 1) * 4)
        xs = x_sample[bs].reshape(256, 1024)
        call = np.concatenate([c_prompt, c_sample[bs]], axis=0)
        cT = np.ascontiguousarray(call.reshape(5, 8, 128).transpose(2, 1, 0))
        st = state_gla[:, bs].reshape(2, 4, 2, 2, 64, 128).transpose(0, 1, 3, 4, 2, 5).reshape(2, 4, 128, 2, 128)
        kbT_c = cache_k_b2[:, bs].transpose(0, 1, 3, 2)
        kcT_c = cache_c_k[:, bs].reshape(2, 4, 512, 2, 2, 64).transpose(0, 1, 4, 5, 3, 2).reshape(2, 4, 128, 2, 512)
        vc_c = cache_v_c2[:, bs].reshape(2, 4, 4, 128, 256).transpose(0, 1, 3, 2, 4)
        d = dict(shared)
        d.update(xTs=_feat_major(xs), cT=cT, st=np.ascontiguousarray(st), kbT_c=np.ascontiguousarray(kbT_c),
                 vb_c=np.ascontiguousarray(cache_v_b2[:, bs]), kcT_c=np.ascontiguousarray(kcT_c), vc_c=np.ascontiguousarray(vc_c),
                 cbk=np.ascontiguousarray(cache_k_b2[:, bs]), cbv=np.ascontiguousarray(cache_v_b2[:, bs]),
                 cck=np.ascontiguousarray(cache_k_c2[:, bs]), ccv=np.ascontiguousarray(cache_v_c2[:, bs]))
        in_maps.append(d)
    res = run_bass_kernel_spmd(nc, in_maps, core_ids=list(range(NCORES)))
    R = res.results

    def unS(a):
        lead = a.shape[:-3]
        a = a.reshape(*lead, 2, 64, 2, 128)
        a = np.moveaxis(a, -2, -4)
        return np.ascontiguousarray(a.reshape(*lead, 4, 64, 128))

    y_prompt = _from_feat_major(R[0]["yTp"])[None]
    y_sample = np.concatenate([_from_feat_major(R[c]["yTs"]).reshape(4, 64, 1024) for c in range(NCORES)], axis=0)
    sg_p = unS(R[0]["sgp"])[:, None]
    kb_p = R[0]["kbp"].reshape(2, 1, 128, 2, 64)
    vb_p = R[0]["vbp"].reshape(2, 1, 128, 2, 64)
    kc_p = R[0]["kcp"].reshape(2, 1, 512, 4, 64)
    vc_p = R[0]["vcp"].reshape(2, 1, 512, 4, 64)
    sg_s = np.concatenate([unS(R[c]["sgs"]) for c in range(NCORES)], axis=1)
    kb_s = np.concatenate([R[c]["kbs"].reshape(2, 4, 128, 2, 64) for c in range(NCORES)], axis=1)
    vb_s = np.concatenate([R[c]["vbs"].reshape(2, 4, 128, 2, 64) for c in range(NCORES)], axis=1)
    kc_s = np.concatenate([R[c]["kcs"].reshape(2, 4, 512, 4, 64) for c in range(NCORES)], axis=1)
    vc_s = np.concatenate([R[c]["vcs"].reshape(2, 4, 512, 4, 64) for c in range(NCORES)], axis=1)
    outs = (y_prompt, y_sample, sg_p, kb_p, vb_p, kc_p, vc_p, sg_s, kb_s, vb_s, kc_s, vc_s)
    return tuple(np.ascontiguousarray(o_, dtype=np.float32) for o_ in outs)
```
